# Optimizing a Trainium2 kernel written in Bass

```python
import math
import jax, jax.numpy as jnp
from jax import lax
import numpy as np

D_MODEL = 1024
BATCH = 2
SEQ = 16384
DEPTH = 1

A_HEADS = 8
A_KV_HEADS = 2
A_HEAD_DIM = 64
WINDOW = 128
BLK = WINDOW
A_WIDTH = A_HEADS * A_HEAD_DIM
NUM_BUCKETS = 32
T5_MAX_DIST = 128
B_HEADS = 4
Q_LORA = 256
KV_LORA = 128
NOPE_DIM = 128
ROPE_DIM = 64
V_DIM = 128
ROPE_THETA = 10000.0
QB = 128
B_WIDTH = B_HEADS * V_DIM
MIX_WIDTH = A_WIDTH + B_WIDTH
IN_SPLITS = (A_HEADS * A_HEAD_DIM, A_KV_HEADS * A_HEAD_DIM, A_KV_HEADS * A_HEAD_DIM,
             Q_LORA, KV_LORA, ROPE_DIM)
IN_COLS = sum(IN_SPLITS)
D_FF = 2816
CONV_W = 3
EPS = 1e-6
NEG = -1e30

kernel_name = "hymba_swa_sink_mla_convffn"


def rmsnorm(x, g):
    xf = x.astype(jnp.float32)
    y = xf * lax.rsqrt(jnp.mean(xf * xf, axis=-1, keepdims=True) + EPS)
    return (y * g.astype(jnp.float32)).astype(x.dtype)


def t5_bucket(dist):
    max_exact = NUM_BUCKETS // 2
    n = jnp.maximum(dist, 0)
    large = max_exact + (jnp.log(jnp.maximum(n, 1).astype(jnp.float32) / max_exact)
                         / math.log(T5_MAX_DIST / max_exact)
                         * (NUM_BUCKETS - max_exact)).astype(jnp.int32)
    large = jnp.minimum(large, NUM_BUCKETS - 1)
    return jnp.where(n < max_exact, n, large)


def rope(x, ang):
    half = x.shape[-1] // 2
    x1, x2 = x[..., :half].astype(jnp.float32), x[..., half:].astype(jnp.float32)
    c, s = jnp.cos(ang), jnp.sin(ang)
    return jnp.concatenate([x1 * c - x2 * s, x1 * s + x2 * c], axis=-1).astype(x.dtype)


def sliding_window_attention(q, k, v, sinks, bias_table):
    b, s, hq, dh = q.shape
    hkv = k.shape[2]
    g = hq // hkv
    nb = s // BLK
    qb = q.reshape(b, nb, BLK, hkv, g, dh)
    kb = k.reshape(b, nb, BLK, hkv, dh)
    vb = v.reshape(b, nb, BLK, hkv, dh)
    pad = jnp.zeros_like(kb[:, :1])
    kk = jnp.concatenate([jnp.concatenate([pad, kb[:, :-1]], axis=1), kb], axis=2)
    vv = jnp.concatenate([jnp.concatenate([pad, vb[:, :-1]], axis=1), vb], axis=2)
    scores = jnp.einsum('bnqhgd,bnkhd->bnhgqk', qb, kk).astype(jnp.float32) * (dh ** -0.5)
    q_idx = BLK + jnp.arange(BLK)
    k_idx = jnp.arange(2 * BLK)
    dist = q_idx[:, None] - k_idx[None, :]
    in_window = (dist >= 0) & (dist < WINDOW)
    not_pad = (jnp.arange(nb)[:, None, None] > 0) | (k_idx >= BLK)[None, None, :]
    mask = in_window[None] & not_pad
    bias = bias_table[t5_bucket(dist)].astype(jnp.float32)
    bias = jnp.transpose(bias, (2, 0, 1)).reshape(hkv, g, BLK, 2 * BLK)
    scores = jnp.where(mask[None, :, None, None], scores + bias, NEG)
    sink = sinks.astype(jnp.float32).reshape(1, 1, hkv, g, 1, 1)
    m = jnp.maximum(jnp.max(scores, axis=-1, keepdims=True), sink)
    p = jnp.exp(scores - m)
    denom = jnp.sum(p, axis=-1, keepdims=True) + jnp.exp(sink - m)
    out = jnp.einsum('bnhgqk,bnkhd->bnqhgd', (p / denom).astype(v.dtype), vv)
    return out.reshape(b, s, hq * dh)


def dense_causal_attention(q, k, v):
    b, s, h, dqk = q.shape
    dv = v.shape[-1]
    nb = s // QB
    scale = dqk ** -0.5
    k_pos = jnp.arange(s)

    def block(n):
        qs = lax.dynamic_slice_in_dim(q, n * QB, QB, axis=1)
        sc = jnp.einsum('bqhd,bkhd->bhqk', qs, k).astype(jnp.float32) * scale
        causal = (n * QB + jnp.arange(QB))[:, None] >= k_pos[None, :]
        p = jax.nn.softmax(jnp.where(causal, sc, NEG), axis=-1)
        return jnp.einsum('bhqk,bkhd->bqhd', p.astype(v.dtype), v)

    out = lax.map(block, jnp.arange(nb))
    return jnp.transpose(out, (1, 0, 2, 3, 4)).reshape(b, s, h * dv)


def causal_dwconv(u, w, bias):
    s = u.shape[1]
    up = jnp.pad(u, ((0, 0), (CONV_W - 1, 0), (0, 0)))
    return sum(up[:, j:j + s] * w[j] for j in range(CONV_W)) + bias


def setup_inputs(seed: int = 0) -> dict:
    key = jax.random.key(seed)
    ks = jax.random.split(key, 20)
    f32 = jnp.float32

    def w(k, shape, fan_in):
        return jax.random.normal(k, shape, f32) * fan_in ** -0.5

    def gain(k, shape):
        return 1.0 + 0.05 * jax.random.normal(k, shape, f32)

    x = jax.random.normal(ks[0], (BATCH, SEQ, D_MODEL), f32)
    offsets = jax.random.randint(ks[1], (BATCH, 1), 0, 4096, dtype=jnp.int32)
    positions = offsets + jnp.arange(SEQ, dtype=jnp.int32)[None, :]
    return {
        "x": x,
        "positions": positions,
        "rel_bias_table": 0.5 * jax.random.normal(ks[2], (NUM_BUCKETS, A_HEADS), f32),
        "attn_norm_g": gain(ks[3], (DEPTH, D_MODEL)),
        "w_in": w(ks[4], (DEPTH, D_MODEL, IN_COLS), D_MODEL),
        "sinks": 0.5 * jax.random.normal(ks[5], (DEPTH, A_HEADS), f32),
        "q_norm_g": gain(ks[6], (DEPTH, Q_LORA)),
        "w_q_b": w(ks[7], (DEPTH, Q_LORA, B_HEADS * (NOPE_DIM + ROPE_DIM)), Q_LORA),
        "kv_norm_g": gain(ks[8], (DEPTH, KV_LORA)),
        "w_kv_b": w(ks[9], (DEPTH, KV_LORA, B_HEADS * (NOPE_DIM + V_DIM)), KV_LORA),
        "a_out_norm_g": gain(ks[10], (DEPTH, A_WIDTH)),
        "b_out_norm_g": gain(ks[11], (DEPTH, B_WIDTH)),
        "w_out": w(ks[12], (DEPTH, MIX_WIDTH, D_MODEL), MIX_WIDTH),
        "ffn_norm_g": gain(ks[13], (DEPTH, D_MODEL)),
        "w_up": w(ks[14], (DEPTH, D_MODEL, 2 * D_FF), D_MODEL),
        "conv_w": w(ks[15], (DEPTH, CONV_W, 2 * D_FF), CONV_W),
        "conv_b": 0.01 * jax.random.normal(ks[16], (DEPTH, 2 * D_FF), f32),
        "w_down": w(ks[17], (DEPTH, D_FF, D_MODEL), D_FF),
        "final_norm_g": gain(ks[18], (D_MODEL,)),
    }


def reference(x, positions, rel_bias_table, attn_norm_g, w_in, sinks, q_norm_g, w_q_b,
              kv_norm_g, w_kv_b, a_out_norm_g, b_out_norm_g, w_out, ffn_norm_g, w_up,
              conv_w, conv_b, w_down, final_norm_g):
    b, s, _ = x.shape
    inv_freq = ROPE_THETA ** (-jnp.arange(0, ROPE_DIM, 2, dtype=jnp.float32) / ROPE_DIM)
    ang = positions.astype(jnp.float32)[..., None] * inv_freq
    cuts = np.cumsum(IN_SPLITS)[:-1].tolist()

    for l in range(DEPTH):
        h = rmsnorm(x, attn_norm_g[l])
        proj = h @ w_in[l]
        qa, ka, va, c_q, c_kv, k_pe = jnp.split(proj, cuts, axis=-1)

        qa = qa.reshape(b, s, A_HEADS, A_HEAD_DIM)
        ka = ka.reshape(b, s, A_KV_HEADS, A_HEAD_DIM)
        va = va.reshape(b, s, A_KV_HEADS, A_HEAD_DIM)
        out_a = sliding_window_attention(qa, ka, va, sinks[l], rel_bias_table)

        qb = (rmsnorm(c_q, q_norm_g[l]) @ w_q_b[l]).reshape(b, s, B_HEADS, NOPE_DIM + ROPE_DIM)
        q_nope, q_pe = qb[..., :NOPE_DIM], qb[..., NOPE_DIM:]
        q_pe = rope(q_pe, ang[:, :, None, :])
        kv = (rmsnorm(c_kv, kv_norm_g[l]) @ w_kv_b[l]).reshape(b, s, B_HEADS, NOPE_DIM + V_DIM)
        k_nope, vb = kv[..., :NOPE_DIM], kv[..., NOPE_DIM:]
        k_pe = jnp.broadcast_to(rope(k_pe, ang)[:, :, None, :], (b, s, B_HEADS, ROPE_DIM))
        qm = jnp.concatenate([q_nope, q_pe], axis=-1)
        km = jnp.concatenate([k_nope, k_pe], axis=-1)
        out_b = dense_causal_attention(qm, km, vb)

        mixed = jnp.concatenate([rmsnorm(out_a, a_out_norm_g[l]),
                                 rmsnorm(out_b, b_out_norm_g[l])], axis=-1)
        x = x + mixed @ w_out[l]

        h = rmsnorm(x, ffn_norm_g[l])
        u = causal_dwconv(h @ w_up[l], conv_w[l], conv_b[l])
        gate, val = u[..., :D_FF], u[..., D_FF:]
        x = x + (jax.nn.silu(gate) * val) @ w_down[l]

    return rmsnorm(x, final_norm_g)
```

```python
import math
from contextlib import ExitStack

import numpy as np
import concourse.bass as bass
import concourse.mybir as mybir
from concourse.bass_utils import run_bass_kernel_spmd

F32 = mybir.dt.float32
BF16 = mybir.dt.bfloat16
I32 = mybir.dt.int32
ALU = mybir.AluOpType
AF = mybir.ActivationFunctionType
AX = mybir.AxisListType

D = 1024
SEQ = 16384
NB = 2
KC = 8
TS = 512
OWN = 510
HALO = 128
TX = TS + HALO
NSLOT = 9
NKT = SEQ // TS
NKB = SEQ // 128
DFF = 2816
NFC = DFF // 128
EPS = 1e-6
SC_A = 64 ** -0.5
SC_B = 192 ** -0.5
NEGB = -30000.0
NC_CONST = 216
DEBUG = False


def stripe_of(c, j):
    if j == 0:
        return 0 if c == 0 else None
    return 4 * j - 3 + c


def slot_extent(j):
    if j == 0:
        return 4, 0
    kmax = 4 * j
    kmin = 4 * j - 3
    last_q = OWN * kmax - 2 + TS - 1
    nk = min(NKB, last_q // 128 + 1)
    q0min = OWN * kmin - 2
    nfull = max(0, (q0min + 1) // 128)
    return nk, min(nfull, nk)


class _Ins:
    __slots__ = ("eng", "fn", "deps", "sig", "dma", "need")

    def __init__(self, eng, fn, dma):
        self.eng = eng
        self.fn = fn
        self.dma = dma
        self.deps = []
        self.sig = None
        self.need = False


class _Set:
    __slots__ = ("eng", "dma")

    def __init__(self):
        self.eng = {}
        self.dma = []

    def add(self, ins):
        if ins.dma is None:
            self.eng[ins.eng] = ins
        else:
            self.dma.append(ins)

    def all(self):
        return list(self.eng.values()) + self.dma


PARENTS = {"hT": "R1_hT", "hk": "hk_all", "qa": "qa_all", "qpe": "qpe_all", "qlat": "qlat_all", "aT": "R1_aT"}


class Sched:
    ENGS = ("pe", "act", "dve", "pool", "sp")

    def __init__(self):
        self.streams = {e: [] for e in self.ENGS}
        self.order = []
        self.lw = {}
        self.rd = {}
        self.rc = {}
        self.pc = {}
        self.alias = {}
        self.batch = set()

    def add_alias(self, keys):
        for k in keys:
            self.alias.setdefault(k, set()).update(x for x in keys if x != k)

    @staticmethod
    def parent(k):
        if isinstance(k, tuple) and k[0] in PARENTS:
            return PARENTS[k[0]]
        return None

    def _g(self, d, k):
        v = d.get(k)
        if v is None:
            v = d[k] = _Set()
        return v

    def add(self, eng, fn, reads=(), writes=(), dma=None, batch=False):
        ins = _Ins(eng, fn, dma)
        if batch:
            self.batch.add(dma)
        deps = {}

        def dep(i):
            if i is not None:
                deps[id(i)] = i

        def depall(st):
            if st is not None:
                for i in st.all():
                    deps[id(i)] = i

        for k in reads:
            p = self.parent(k)
            dep(self.lw.get(k))
            depall(self.pc.get(k))
            if p is not None:
                dep(self.lw.get(p))
        for k in writes:
            p = self.parent(k)
            dep(self.lw.get(k)); depall(self.rd.get(k)); depall(self.rc.get(k)); depall(self.pc.get(k))
            if p is not None:
                dep(self.lw.get(p)); depall(self.rd.get(p))
            for a in self.alias.get(p if p is not None else k, ()):
                dep(self.lw.get(a)); depall(self.rd.get(a)); depall(self.rc.get(a))
        for d in deps.values():
            if d.eng == eng and d.dma is None and dma is None and eng == "pe":
                continue
            d.need = True
            ins.deps.append(d)
        for k in reads:
            p = self.parent(k)
            self._g(self.rd, k).add(ins)
            if p is not None:
                self._g(self.rc, p).add(ins)
        for k in writes:
            p = self.parent(k)
            self.lw[k] = ins
            self.rd[k] = _Set(); self.rc[k] = _Set(); self.pc[k] = _Set()
            if p is not None:
                self._g(self.pc, p).add(ins)
            for a in self.alias.get(p if p is not None else k, ()):
                self._g(self.pc, a).add(ins)
        self.streams[eng].append(ins)
        self.order.append(ins)
        return ins

    def dma_keys(self):
        ks = []
        seen = set()
        for ins in self.order:
            if ins.dma is not None and ins.dma not in seen:
                seen.add(ins.dma)
                ks.append(ins.dma)
        return ks

    def finalize(self, final_wait):
        for ins in final_wait:
            ins.need = True
        cnt = {}
        for ins in self.order:
            if ins.dma in self.batch:
                ins.need = True
            if not ins.need:
                continue
            key = ("dma", ins.dma) if ins.dma is not None else ("eng", ins.eng)
            cnt[key] = cnt.get(key, 0) + (16 if ins.dma is not None else 1)
            ins.sig = (key, cnt[key])
        for ins in self.order:
            if ins.dma in self.batch:
                ins.sig = (("dma", ins.dma), cnt[("dma", ins.dma)])

    def emit_stream(self, eng_name, eng, sems, extra_waits=()):
        waited = {}
        for ins in self.streams[eng_name]:
            for d in ins.deps:
                key, val = d.sig
                if waited.get(key, 0) < val:
                    eng.wait_ge(sems[key], val)
                    waited[key] = val
            bi = ins.fn(eng)
            if ins.sig is not None:
                key, val = ins.sig
                bi.then_inc(sems[key], 16 if key[0] == "dma" else 1)
        for d in extra_waits:
            key, val = d.sig
            if waited.get(key, 0) < val:
                eng.wait_ge(sems[key], val)
                waited[key] = val


def build_nc(nslot=NSLOT, nkt=NKT, dbg=False):
    nc = bass.Bass("TRN2", target_bir_lowering=False)
    S = Sched()

    def din(name, shape, dt=F32):
        return nc.dram_tensor(name, list(shape), dt, kind="ExternalInput")

    xT_seq = din("xT_seq", [D, SEQ])
    xT_str = din("xT_str", [NSLOT, D, TX])
    pos_seq = din("pos_seq", [1, SEQ], I32)
    pos_str = din("pos_str", [1, NSLOT * TS], I32)
    tq_str = din("tq_str", [1, NSLOT * TS])
    kval_str = din("kval_str", [NSLOT, 128, 5])
    consts_d = din("consts", [128, NC_CONST])
    w_in_d = din("w_in_ext", [D, 1408])
    w_qb_d = din("w_qb_ext", [256, 1536])
    w_kvT_d = din("w_kvT", [1024, 128])
    w_kv_d = din("w_kv", [128, 1024])
    w_out_d = din("w_out", [1024, D])
    w_up_d = din("w_up", [D, 2 * DFF])
    w_dn_d = din("w_down", [DFF, D])
    biasT_d = din("biasT", [128, 2 * 8 * 128])
    sinks_d = din("sinks", [1, 8])
    relb_d = din("relb", [1, 256])
    gkv_row_d = din("gkv_row", [1, 128])
    ident_d = din("ident", [128, 128])
    out_d = nc.dram_tensor("out", [NSLOT, D, TS], F32, kind="ExternalOutput")
    wup_s = nc.dram_tensor("wup_s", [2 * NFC, 128, KC * 128], BF16)
    wdn_s = nc.dram_tensor("wdn_s", [2, NFC, 128, 512], BF16)
    dbg_d = {}
    if dbg:
        dbg_d["kt"] = nc.dram_tensor("dbg_kt", [128, 512], F32, kind="ExternalOutput")
        dbg_d["kpe"] = nc.dram_tensor("dbg_kpe", [128, 256], F32, kind="ExternalOutput")
        dbg_d["v"] = nc.dram_tensor("dbg_v", [128, 512], F32, kind="ExternalOutput")
        dbg_d["x1"] = nc.dram_tensor("dbg_x1", [D, TS], F32, kind="ExternalOutput")
        dbg_d["oa"] = nc.dram_tensor("dbg_oa", [64, 8 * TS], F32, kind="ExternalOutput")
        dbg_d["ob"] = nc.dram_tensor("dbg_ob", [128, 4 * TS], F32, kind="ExternalOutput")

    es = ExitStack()
    with es:
        def sb(name, shape, dt):
            return es.enter_context(nc.sbuf_tensor(name, list(shape), dt))

        KT = sb("KT", [128, SEQ], BF16)
        KPE = sb("KPE", [128, SEQ // 2], BF16)
        V = sb("V", [128, NKB, 128], BF16)
        consts = sb("consts_s", [128, NC_CONST], F32)
        wq = sb("wq", [128, 2, 1536], BF16)
        wkT = sb("wkT", [128, 8, 128], BF16)
        wv = sb("wv", [128, 1024], BF16)
        ident = sb("ident_s", [128, 128], BF16)
        ones = sb("ones", [128, 128], BF16)
        ukeys = sb("ukeys", [128, NKB], F32)
        biasT = sb("biasT_s", [128, 2, 8, 128], F32)
        psink = sb("psink", [1, 1024], BF16)
        ones_row = sb("ones_row", [1, 64], BF16)
        small = sb("small", [128, 32], F32)
        xs = sb("xs", [128, KC, TS], F32)
        rstd = sb("rstd", [128, TX], F32)
        lnt = sb("lnt", [128, TX], F32)
        R1 = sb("R1", [128, 24 * 1024 // 2], BF16)
        hT = R1[:, 0:KC * TX].rearrange("p (a b) -> p a b", a=KC)
        wob = R1[:, 0:4 * 1024].rearrange("p (a b) -> p a b", a=4)
        woa = R1[0:64, 4 * 1024:12 * 1024].rearrange("p (a b) -> p a b", a=8)
        aT = R1[:, 0:NFC * TS].rearrange("p (a b) -> p a b", a=NFC)
        R3 = sb("R3", [128, 8 * 1024], BF16)
        wst = R3[:, :].rearrange("p (a b) -> p a b", a=KC)
        out_a = R3[0:64, 0:8 * TS].rearrange("p (a b) -> p a b", a=8)
        o_b = R3[:, 4096:4096 + 4 * TS].rearrange("p (a b) -> p a b", a=4)
        pT = [R3[:, 6144 + i * TS:6144 + (i + 1) * TS] for i in range(4)]
        R2 = sb("R2", [128, 12480], BF16)
        qlat = R2[:, 0:2048].rearrange("p (a b) -> p a b", a=4)
        qpe = R2[:, 2048:4096].rearrange("p (a b) -> p a b", a=4)
        qa = R2[0:64, 4096:8192].rearrange("p (a b) -> p a b", a=8)
        ka = R2[0:64, 8192:8192 + 2 * TX].rearrange("p (a b) -> p a b", a=2)
        va = R2[:, 9472:9472 + 5 * 128].rearrange("p (a b) -> p a b", a=5)
        onesv = R2[:, 10112:10112 + 5 * 64].rearrange("p (a b) -> p a b", a=5)
        cqn = R2[:, 10432:10432 + 2 * TS].rearrange("p (a b) -> p a b", a=2)
        qn = R2[:, 11456:11456 + TS]
        An = R2[:, 11968:11968 + TS]
        h1 = R2[:, 0:KC * TS].rearrange("p (a b) -> p a b", a=KC)
        wu = [R2[:, 4096 + i * 2048:4096 + (i + 1) * 2048].rearrange("p (a b) -> p a b", a=2 * KC) for i in range(4)]
        wdb = [R3[:, i * 4096:(i + 1) * 4096].rearrange("p (a b) -> p a b", a=8) for i in range(2)]
        sgb = [R1[:, 11264 + i * 512:11264 + (i + 1) * 512] for i in range(2)]
        mk = [R2[:, 4096 + i * 512:4096 + (i + 1) * 512] for i in range(3)]
        tq_b = sb("tq_b", [128, TS], F32)
        cosq = rstd[:, 0:TS]
        sinq = lnt[:, 0:TS]
        t32 = [sb("t32_%d" % i, [128, TS], F32) for i in range(4)]
        pos_i = lnt[:, 0:TS].bitcast(I32)
        sq = [sb("sq%d" % i, [128, TX], BF16) for i in range(3)]
        kval = sb("kval", [128, 5], F32)
        ps = [es.enter_context(nc.psum_tensor("ps%d" % i, [128, TS], F32)) for i in range(7)]
        pst = es.enter_context(nc.psum_tensor("pst", [128, 4, 128], BF16))
        xk = [xs[:, :, :], R1[:, 0:8192].bitcast(F32).rearrange("p (a b) -> p a b", a=KC)]
        xh = R1[:, 6144:8192].bitcast(F32).rearrange("p (a b) -> p a b", a=KC)
        hk = R2[:, 0:KC * TS].rearrange("p (a b) -> p a b", a=KC)
        wk = R3[:, 0:KC * 384].rearrange("p (a b) -> p a b", a=KC)

        S.add_alias(["R1_hT", "R1_aT"])
        for wkey in ("woa", "wob"):
            for X in ("R1_hT", "R1_aT", ("xk", 1)):
                S.add_alias([wkey, X])
        for X in ("xh", ("sgb", 0), ("sgb", 1)):
            S.add_alias(["woa", X])
        S.add_alias([("sgb", 0), "R1_hT"]); S.add_alias([("sgb", 1), "R1_hT"])
        S.add_alias([("xk", 1), "R1_hT"]); S.add_alias([("xk", 1), "R1_aT"])
        S.add_alias(["xh", "R1_aT"]); S.add_alias(["xh", ("xk", 1)])
        S.add_alias(["R3_wst", "R3_oa"]); S.add_alias(["R3_wst", "R3_ob"])
        for i in range(4):
            S.add_alias(["R3_wst", ("pT", i)])
            S.add_alias([("wdb", 1), ("pT", i)])
        S.add_alias(["R3_wst", "R3_wk"]); S.add_alias(["R3_wk", "R3_oa"])
        S.add_alias([("wdb", 0), "R3_wst"]); S.add_alias([("wdb", 0), "R3_oa"]); S.add_alias([("wdb", 0), "R3_wk"])
        S.add_alias([("wdb", 1), "R3_wst"]); S.add_alias([("wdb", 1), "R3_ob"])
        att_keys = ["qlat_all", "qpe_all", "qa_all", "ka", "va", "onesv", "cqn", "qn", "An", "hk_all",
                    ("mk", 0), ("mk", 1), ("mk", 2)]
        ffn_keys = ["h1"] + [("wug", i) for i in range(4)] + [("wuv", i) for i in range(4)]
        for a_ in att_keys:
            for f_ in ffn_keys:
                S.add_alias([a_, f_])
        for i in range(3):
            S.add_alias([("mk", i), "qa_all"])
        S.add_alias(["hk_all", "qlat_all"]); S.add_alias(["hk_all", "qpe_all"])
        S.add_alias([("xk", 0), "xs"])
        S.add_alias(["lnt", "pos_i", "sinq"]); S.add_alias(["rstd", "cosq"])

        def C(col, n=1, rows=128):
            return consts[0:rows, col:col + n]

        def mm(out, lhsT, rhs, start, stop, r, w):
            return S.add("pe", lambda e: e.matmul(out, lhsT=lhsT, rhs=rhs, start=start, stop=stop), reads=r, writes=w)

        def act(out, in_, func, r, w, bias=None, scale=None):
            kw = {}
            if bias is not None:
                kw["bias"] = bias
            if scale is not None:
                kw["scale"] = scale
            return S.add("act", lambda e: e.activation(out=out, in_=in_, func=func, **kw), reads=r, writes=w)

        def tt(eng, out, in0, in1, op, r, w):
            return S.add(eng, lambda e: e.tensor_tensor(out=out, in0=in0, in1=in1, op=op), reads=r, writes=w)

        def ts(eng, out, in0, s1, s2, op0, op1, r, w):
            if op1 is None:
                return S.add(eng, lambda e: e.tensor_scalar(out=out, in0=in0, scalar1=s1, scalar2=None, op0=op0), reads=r, writes=w)
            return S.add(eng, lambda e: e.tensor_scalar(out=out, in0=in0, scalar1=s1, scalar2=s2, op0=op0, op1=op1), reads=r, writes=w)

        def stt(eng, out, in0, scalar, in1, op0, op1, r, w):
            return S.add(eng, lambda e: e.scalar_tensor_tensor(out=out, in0=in0, scalar=scalar, in1=in1, op0=op0, op1=op1), reads=r, writes=w)

        def cp(eng, out, in_, r, w):
            if eng == "act":
                return S.add(eng, lambda e: e.activation(out=out, in_=in_, func=AF.Identity), reads=r, writes=w)
            return S.add(eng, lambda e: e.tensor_copy(out=out, in_=in_), reads=r, writes=w)

        def rmax(out, in_, r, w):
            return S.add("dve", lambda e: e.reduce_max(out=out, in_=in_, axis=AX.X), reads=r, writes=w)

        def dma(eng, out, in_, r, w, key, batch=False):
            return S.add(eng, lambda e: e.dma_start(out=out, in_=in_), reads=r, writes=w, dma=key, batch=batch)

        def rsqrt_from_ps(dst, ps_ap, n, inv_n, r_key, w_key, tmp, tmp_key):
            act(tmp, ps_ap, AF.Ln, [r_key], [tmp_key], bias=EPS, scale=inv_n)
            act(dst, tmp, AF.Exp, [tmp_key], [w_key], scale=-0.5)

        def rope_tables(pos_ap_dram, n, cos_t, sin_t, ck, sk, tmpa, tmpa_k, tmpb, tmpb_k, dkey):
            MAGIC = 12582912.0
            C1 = 6.28125
            C2 = 2 * math.pi - 6.28125
            dma("sp", pos_i[:, 0:n], pos_ap_dram, [], ["pos_i"], dkey)
            cp("dve", tmpa, pos_i[:, 0:n], ["pos_i"], [tmpa_k])
            ts("dve", tmpa, tmpa, C(39), None, ALU.mult, None, [tmpa_k, "consts"], [tmpa_k])
            ts("dve", tmpb, tmpa, 1.0 / (2 * math.pi), MAGIC, ALU.mult, ALU.add, [tmpa_k], [tmpb_k])
            ts("dve", tmpb, tmpb, MAGIC, None, ALU.subtract, None, [tmpb_k], [tmpb_k])
            stt("dve", tmpa, tmpb, -C1, tmpa, ALU.mult, ALU.add, [tmpa_k, tmpb_k], [tmpa_k])
            stt("dve", tmpa, tmpb, -C2, tmpa, ALU.mult, ALU.add, [tmpa_k, tmpb_k], [tmpa_k])
            ts("dve", tmpa, tmpa, -math.pi, math.pi, ALU.max, ALU.min, [tmpa_k], [tmpa_k])
            stt("dve", tmpb, tmpa, -1.0, tmpa, ALU.mult, ALU.max, [tmpa_k], [tmpb_k])
            act(sin_t, tmpa, AF.Sin, [tmpa_k], [sk])
            act(cos_t, tmpb, AF.Sin, [tmpb_k], [ck], bias=math.pi / 2, scale=-1.0)

        dma("sp", consts[:, :], consts_d.ap(), [], ["consts"], "c_consts")
        dma("sp", biasT[:, :, :, :].rearrange("p a b c -> p (a b c)"), biasT_d.ap(), [], ["biasT"], "c_bias")
        dma("pool", ident[:, :], ident_d.ap(), [], ["ident"], "c_ident")
        dma("pool", wq[:, :, :], w_qb_d.ap().rearrange("(a p) n -> p a n", p=128), [], ["wq"], "c_wq")
        dma("pool", wkT[:, :, :], w_kvT_d.ap().rearrange("(a p) n -> p a n", p=128), [], ["wkT"], "c_wkT")
        dma("pool", wv[:, :], w_kv_d.ap(), [], ["wv"], "c_wv")
        dma("pool", wk, w_in_d.ap()[:, 1024:1408].rearrange("(a p) n -> p a n", p=128), [], ["R3_wk"], "c_wk")
        S.add("pool", lambda e: e.memset(ones[:, :], 1.0), writes=["ones"])
        S.add("pool", lambda e: e.memset(ones_row[:, :], 1.0), writes=["ones_row"])
        S.add("pool", lambda e: e.memset(small[:, :], 0.0), writes=["small"])
        dma("sp", small[0:1, 20:28], sinks_d.ap(), ["small"], ["sinks8"], "c_sink")
        S.add("pool", lambda e: e.iota(ukeys[:, :], pattern=[[128, NKB]], base=0, channel_multiplier=1,
                                        allow_small_or_imprecise_dtypes=True), writes=["ukeys"])
        for o in (256, 320):
            ts("dve", wk[:, :, o:o + 32], wk[:, :, o:o + 32], -1.0, None, ALU.mult, None, ["R3_wk"], ["R3_wk"])
        wq4 = wq[:, :, :].rearrange("p a (h c) -> p a h c", h=4)
        for o in (256, 320):
            for a in range(2):
                ts("dve", wq4[:, a, :, o:o + 32], wq4[:, a, :, o:o + 32], -1.0, None, ALU.mult, None, ["wq"], ["wq"])
        dma("sp", t32[0][:, 0:128], bass.AP(gkv_row_d, 0, [[0, 128], [1, 128]]), [], [("t32", 0)], "c_t0")
        tt("dve", t32[0][:, 0:128], t32[0][:, 0:128], t32[0][:, 0:128], ALU.mult, [("t32", 0)], [("t32", 0)])
        rmax(small[:, 1:2], t32[0][:, 0:128], [("t32", 0)], ["small1"])
        dma("sp", t32[1][:, 0:256], bass.AP(relb_d, 0, [[0, 128], [1, 256]]), [], [("t32", 1)], "c_t1")
        rmax(small[:, 3:4], t32[1][:, 0:256], [("t32", 1)], ["small3"])
        dma("sp", t32[2][:, 0:8], bass.AP(sinks_d, 0, [[0, 128], [1, 8]]), [], [("t32", 2)], "c_t2")
        rmax(small[:, 4:5], t32[2][:, 0:8], [("t32", 2)], ["small4"])
        wup_v = w_up_d.ap().rearrange("(a p) n -> p a n", p=128)
        for i in range(2 * NFC):
            dst = wup_s.ap()[i].rearrange("p (a c) -> p a c", a=KC)
            dma("pool", dst, wup_v[:, :, i * 128:(i + 1) * 128], [], [("wup_s", i)], "c_wups", batch=True)
        for pss in range(2):
            for i in range(NFC):
                dma("pool", wdn_s.ap()[pss, i], w_dn_d.ap()[i * 128:(i + 1) * 128, pss * 512:(pss + 1) * 512],
                    [], [("wdn_s", pss, i)], "c_wdns", batch=True)

        xT_v = xT_seq.ap().rearrange("(a p) t -> p a t", p=128)
        hk2 = [R2[:, b_ * 4096:(b_ + 1) * 4096].rearrange("p (a b) -> p a b", a=KC) for b_ in range(2)]
        rstdK = [(rstd[:, 0:TS], "rstd"), (tq_b[:, :], "tq_b")]
        csK = R1[:, 8192:12288].bitcast(F32)
        cosK = [csK[:, b_ * 1024:b_ * 1024 + 512] for b_ in range(2)]
        sinK = [csK[:, b_ * 1024 + 512:(b_ + 1) * 1024] for b_ in range(2)]
        tAB = R2[:, 8192:12288].bitcast(F32)
        tA, tB = tAB[:, 0:512], tAB[:, 512:1024]
        sqK = [R3[:, 3072 + i * 512:3072 + (i + 1) * 512] for i in range(8)]
        for b_ in range(2):
            for X_ in ("woa", "R1_aT", ("sgb", 0), ("sgb", 1)):
                S.add_alias([("cosK", b_), X_]); S.add_alias([("sinK", b_), X_])
        for X_ in att_keys + ffn_keys:
            S.add_alias(["tA", X_]); S.add_alias(["tB", X_])
        for i in range(8):
            for X_ in ["R3_wst", "R3_oa", "R3_ob", ("wdb", 0), ("wdb", 1)] + [("pT", q_) for q_ in range(4)]:
                S.add_alias([("sqK", i), X_])
        S.add_alias(["hk_all", "qa_all"])
        for i in range(3):
            S.add_alias(["hk_all", ("mk", i)])
        for kc in range(KC):
            ts("dve", wk[:, kc, :], wk[:, kc, :], C(kc), None, ALU.mult, None, ["R3_wk", "consts"], ["R3_wk"])

        def kfront(it):
            b = it % 2
            xkk = ("xk", b)
            X = xk[b]
            t0 = it * TS
            rs_ap, rs_k = rstdK[b]
            dma("sp", X, xT_v[:, :, t0:t0 + TS], [], [xkk], "d_xk%d" % b)
            for kc in range(KC):
                if kc < 5:
                    tt("pool", sqK[kc], X[:, kc, :], X[:, kc, :], ALU.mult, [xkk], [("sqK", kc)])
                else:
                    act(sqK[kc], X[:, kc, :], AF.Square, [xkk], [("sqK", kc)])
            for kc in list(range(5, KC)) + list(range(5)):
                mm(ps[0][:, :], ones[:, :], sqK[kc], kc == 5, kc == 4, [("sqK", kc), "ones"], [("ps", 0)])
            rsqrt_from_ps(rs_ap, ps[0][:, :], TS, 1.0 / D, ("ps", 0), rs_k, lnt[:, 0:TS], "lnt")
            for kc in range(KC):
                eng = "dve" if kc < 5 else "pool"
                tt(eng, hk2[b][:, kc, :], X[:, kc, :], rs_ap, ALU.mult, [xkk, rs_k], [("hk", b * 8 + kc)])
            rope_tables(bass.AP(pos_seq, t0, [[0, 128], [1, TS]]), TS, cosK[b], sinK[b], ("cosK", b), ("sinK", b),
                        t32[2][:, :], ("t32", 2), t32[3][:, :], ("t32", 3), "d_pos")

        def kback(it):
            b = it % 2
            t0 = it * TS
            for (o, pi) in ((0, 1), (128, 2), (256, 3)):
                for kc in range(KC):
                    mm(ps[pi][:, :], wk[:, kc, o:o + 128], hk2[b][:, kc, :], kc == 0, kc == KC - 1,
                       ["R3_wk", ("hk", b * 8 + kc)], [("ps", pi)])
            act(sq[0][:, 0:TS], ps[1][:, :], AF.Square, [("ps", 1)], [("sq", 0)])
            mm(ps[4][:, :], ones[:, :], sq[0][:, 0:TS], True, True, [("sq", 0), "ones"], [("ps", 4)])
            rsqrt_from_ps(t32[0][:, :], ps[4][:, :], TS, 1.0 / 128, ("ps", 4), ("t32", 0), t32[1][:, :], ("t32", 1))
            stt("dve", KT[:, t0:t0 + TS], ps[1][:, :], C(26), t32[0][:, :], ALU.mult, ALU.mult,
                [("ps", 1), "consts", ("t32", 0)], [("KT", it)])
            act(sq[1][0:64, 0:TS], ps[2][0:64, :], AF.Square, [("ps", 2)], [("sq", 1)])
            mm(ps[5][:, :], ones[0:64, :], sq[1][0:64, 0:TS], True, True, [("sq", 1), "ones"], [("ps", 5)])
            rmax(small[:, 5:6], ps[5][:, :], [("ps", 5)], ["small5"])
            tt("dve", small[:, 0:1], small[:, 0:1], small[:, 5:6], ALU.max, ["small", "small5"], ["small0"])
            tt("dve", tA, ps[2][:, :], cosK[b], ALU.mult, [("ps", 2), ("cosK", b)], ["tA"])
            tt("dve", tB, ps[3][:, :], sinK[b], ALU.mult, [("ps", 3), ("sinK", b)], ["tB"])
            kdst = KPE[:, 2 * it * 128:(2 * it + 2) * 128].rearrange("p (a b) -> p a b", a=2)
            a3 = tA.rearrange("p (a b) -> p a b", a=4)
            b3 = tB.rearrange("p (a b) -> p a b", a=4)
            tt("dve", kdst[0:64, :, :], a3[0:64, 0:4:2, :], b3[0:64, 0:4:2, :], ALU.add, ["tA", "tB"], [("KPEa", it)])
            tt("dve", kdst[64:128, :, :], a3[64:128, 1:4:2, :], b3[64:128, 1:4:2, :], ALU.add, ["tA", "tB"], [("KPEb", it)])
            for q in range(4):
                S.add("pe", (lambda o_=pst[:, q, :], i_=KT[:, t0 + q * 128:t0 + (q + 1) * 128]: (lambda e: e.transpose(o_, i_, ident[:, :])))(),
                      reads=[("KT", it), "ident"], writes=["pst"])
            cp("act", V[:, 4 * it:4 * it + 4, :], pst[:, :, :], ["pst"], [("V", it)])

        kfront(0)
        for it in range(nkt):
            if it + 1 < nkt:
                kfront(it + 1)
            kback(it)
        ts("dve", small[:, 2:3], small[:, 1:2], 128.0, None, ALU.mult, None, ["small1"], ["small2"])
        tt("dve", small[:, 2:3], small[:, 2:3], small[:, 0:1], ALU.add, ["small2", "small0"], ["small2"])
        ts("dve", small[:, 2:3], small[:, 2:3], 1.05, None, ALU.mult, None, ["small2"], ["small2"])
        if dbg:
            cp("dve", t32[0][:, :], KT[:, 0:TS], [("KT", 0)], [("t32", 0)])
            dma("sp", dbg_d["kt"].ap(), t32[0][:, :], [("t32", 0)], [], "dbg0")
            cp("dve", t32[1][:, 0:256], KPE[:, 0:256], [("KPEa", 0), ("KPEb", 0)], [("t32", 1)])
            dma("sp", dbg_d["kpe"].ap(), t32[1][:, 0:256], [("t32", 1)], [], "dbg1")
            cp("dve", t32[2][:, :], V[:, 0:4, :].rearrange("p a b -> p (a b)"), [("V", 0)], [("t32", 2)])
            dma("sp", dbg_d["v"].ap(), t32[2][:, :], [("t32", 2)], [], "dbg2")

        outs = []
        xstr_v = xT_str.ap().rearrange("j (a p) t -> j p a t", p=128)
        win_v = w_in_d.ap()[:, 0:1024].rearrange("(a p) n -> p a n", p=128)
        woa_v = w_out_d.ap()[0:512, :].rearrange("(h d) n -> d h n", d=64)
        wob_v = w_out_d.ap()[512:1024, :].rearrange("(h p) n -> p h n", p=128)
        out_v = out_d.ap().rearrange("j (a p) t -> j p a t", p=128)
        for j in range(nslot):
            nk, nfull = slot_extent(j)
            dma("sp", xs[:, :, :], xstr_v[j][:, :, HALO:TX], [], ["xs"], "d_xs")
            dma("sp", xh, xstr_v[j][:, :, 0:HALO], [], ["xh"], "d_xh")
            dma("pool", wst, win_v, [], ["R3_wst"], "d_wst")
            dma("sp", tq_b[:, :], bass.AP(tq_str, j * TS, [[0, 128], [1, TS]]), [], ["tq_b"], "d_tq")
            dma("sp", kval[:, :], kval_str.ap()[j], [], ["kval"], "d_kval")
            for kc in range(KC):
                s_ = sq[kc % 3]
                act(s_[:, HALO:TX], xs[:, kc, :], AF.Square, ["xs"], [("sq", kc % 3)])
                act(s_[:, 0:HALO], xh[:, kc, :], AF.Square, ["xh", ("sq", kc % 3)], [("sq", kc % 3)])
                mm(ps[0][:, :], ones[:, :], s_[:, 0:TS], kc == 0, kc == KC - 1, [("sq", kc % 3), "ones"], [("ps", 0)])
                mm(ps[1][:, 0:HALO], ones[:, :], s_[:, TS:TX], kc == 0, kc == KC - 1, [("sq", kc % 3), "ones"], [("ps", 1)])
            rsqrt_from_ps(rstd[:, 0:TS], ps[0][:, :], TS, 1.0 / D, ("ps", 0), "rstd", lnt[:, 0:TS], "lnt")
            rsqrt_from_ps(rstd[:, TS:TX], ps[1][:, 0:HALO], HALO, 1.0 / D, ("ps", 1), "rstd", lnt[:, TS:TX], "lnt")
            for kc in range(KC):
                eng = "dve"
                stt(eng, hT[:, kc, HALO:TX], xs[:, kc, :], C(kc), rstd[:, HALO:TX], ALU.mult, ALU.mult,
                    ["xs", "consts", "rstd"], [("hT", kc)])
                stt(eng, hT[:, kc, 0:HALO], xh[:, kc, :], C(kc), rstd[:, 0:HALO], ALU.mult, ALU.mult,
                    ["xh", "consts", "rstd", ("hT", kc)], [("hT", kc)])
            for blk in range(5):
                cp("dve", onesv[:, blk, :], kval[:, blk:blk + 1].to_broadcast([128, 64]), ["kval"], ["onesv"])
            hS = lambda kc: hT[:, kc, HALO:TX]
            rot = [2, 3, 4, 5, 6]
            rr = [0]

            def nxt():
                rr[0] = (rr[0] + 1) % len(rot)
                return rot[rr[0]]

            S.add("pool", lambda e: e.memset(small[:, 6:10], 0.0), writes=["small6", "small7", "small8", "small9"])
            for h in range(8):
                pi = nxt()
                for kc in range(KC):
                    mm(ps[pi][0:64, :], wst[:, kc, h * 64:(h + 1) * 64], hS(kc), kc == 0, kc == KC - 1,
                       ["R3_wst", ("hT", kc)], [("ps", pi)])
                act(qa[:, h, :], ps[pi][0:64, :], AF.Identity, [("ps", pi)], [("qa", h)], scale=SC_A)
                act(sq[h % 3][0:64, 0:TS], qa[:, h, :], AF.Square, [("qa", h)], [("sq", h % 3)])
                mm(ps[1][:, :], ones[0:64, :], sq[h % 3][0:64, 0:TS], True, True, [("sq", h % 3), "ones"], [("ps", 1)])
                rmax(small[:, 5:6], ps[1][:, :], [("ps", 1)], ["small5"])
                tt("dve", small[:, 6:7], small[:, 6:7], small[:, 5:6], ALU.max, ["small6", "small5"], ["small6"])
            for g in range(2):
                pi = nxt()
                pj = nxt()
                for kc in range(KC):
                    mm(ps[pi][0:64, :], wst[:, kc, 512 + g * 64:512 + (g + 1) * 64], hT[:, kc, 0:TS], kc == 0, kc == KC - 1,
                       ["R3_wst", ("hT", kc)], [("ps", pi)])
                for kc in range(KC):
                    mm(ps[pj][0:64, 0:HALO], wst[:, kc, 512 + g * 64:512 + (g + 1) * 64], hT[:, kc, TS:TX], kc == 0, kc == KC - 1,
                       ["R3_wst", ("hT", kc)], [("ps", pj)])
                act(ka[:, g, 0:TS], ps[pi][0:64, :], AF.Identity, [("ps", pi)], ["ka"])
                act(ka[:, g, TS:TX], ps[pj][0:64, 0:HALO], AF.Identity, [("ps", pj)], ["ka"])
                act(sq[g][0:64, :], ka[:, g, :], AF.Square, ["ka"], [("sq", g)])
                mm(ps[0][:, :], ones[0:64, :], sq[g][0:64, 0:TS], True, True, [("sq", g), "ones"], [("ps", 0)])
                mm(ps[1][:, 0:HALO], ones[0:64, :], sq[g][0:64, TS:TX], True, True, [("sq", g), "ones"], [("ps", 1)])
                rmax(small[:, 5:6], ps[0][:, :], [("ps", 0)], ["small5"])
                tt("dve", small[:, 7:8], small[:, 7:8], small[:, 5:6], ALU.max, ["small7", "small5"], ["small7"])
                rmax(small[:, 5:6], ps[1][:, 0:HALO], [("ps", 1)], ["small5"])
                tt("dve", small[:, 7:8], small[:, 7:8], small[:, 5:6], ALU.max, ["small7", "small5"], ["small7"])
            for blk in range(5):
                pi = nxt()
                for kc in range(KC):
                    mm(ps[pi][:, 0:128], hT[:, kc, blk * 128:(blk + 1) * 128], wst[:, kc, 640:768], kc == 0, kc == KC - 1,
                       ["R3_wst", ("hT", kc)], [("ps", pi)])
                cp("dve", va[:, blk, :], ps[pi][:, 0:128], [("ps", pi)], ["va"])
            pq = [nxt(), nxt()]
            for c2 in range(2):
                for kc in range(KC):
                    mm(ps[pq[c2]][:, :], wst[:, kc, 768 + c2 * 128:768 + (c2 + 1) * 128], hS(kc), kc == 0, kc == KC - 1,
                       ["R3_wst", ("hT", kc)], [("ps", pq[c2])])
                act(sq[c2][:, 0:TS], ps[pq[c2]][:, :], AF.Square, [("ps", pq[c2])], [("sq", c2)])
                mm(ps[0][:, :], ones[:, :], sq[c2][:, 0:TS], c2 == 0, c2 == 1, [("sq", c2), "ones"], [("ps", 0)])
            rsqrt_from_ps(t32[0][:, :], ps[0][:, :], TS, 1.0 / 256, ("ps", 0), ("t32", 0), t32[1][:, :], ("t32", 1))
            for c2 in range(2):
                stt("dve", cqn[:, c2, :], ps[pq[c2]][:, :], C(24 + c2), t32[0][:, :], ALU.mult, ALU.mult,
                    [("ps", pq[c2]), "consts", ("t32", 0)], ["cqn"])
            rope_tables(bass.AP(pos_str, j * TS, [[0, 128], [1, TS]]), TS, cosq[:, :], sinq[:, :], "cosq", "sinq",
                        t32[2][:, :], ("t32", 2), t32[3][:, :], ("t32", 3), "d_pos")
            S.add("pool", lambda e: e.memset(small[:, 10:14], 0.0), writes=[("smq", 0), ("smq", 1), ("smq", 2), ("smq", 3)])
            for h in range(4):
                pn, pp, pr = nxt(), nxt(), nxt()
                for (o, pi) in ((0, pn), (128, pp), (256, pr)):
                    for c2 in range(2):
                        mm(ps[pi][:, :], wq[:, c2, h * 384 + o:h * 384 + o + 128], cqn[:, c2, :], c2 == 0, c2 == 1,
                           ["wq", "cqn"], [("ps", pi)])
                cp("act", qn, ps[pn][:, :], [("ps", pn)], ["qn"])
                tt("dve", t32[0][:, :], ps[pp][:, :], cosq[:, :], ALU.mult, [("ps", pp), "cosq"], [("t32", 0)])
                tt("dve", t32[1][:, :], ps[pr][:, :], sinq[:, :], ALU.mult, [("ps", pr), "sinq"], [("t32", 1)])
                tt("dve", t32[0][:, :], t32[0][:, :], t32[1][:, :], ALU.add, [("t32", 0), ("t32", 1)], [("t32", 0)])
                act(qpe[:, h, :], t32[0][:, :], AF.Identity, [("t32", 0)], [("qpe", h)], scale=SC_B)
                pa = nxt()
                mm(ps[pa][:, :], wkT[:, 2 * h, :], qn, True, True, ["wkT", "qn"], [("ps", pa)])
                act(qlat[:, h, :], ps[pa][:, :], AF.Identity, [("ps", pa)], [("qlat", h)], scale=SC_B)
                act(sq[0][:, 0:TS], qlat[:, h, :], AF.Square, [("qlat", h)], [("sq", 0)])
                act(sq[1][0:64, 0:TS], qpe[0:64, h, :], AF.Square, [("qpe", h)], [("sq", 1)])
                mm(ps[0][:, :], ones[:, :], sq[0][:, 0:TS], True, False, [("sq", 0), "ones"], [("ps", 0)])
                mm(ps[0][:, :], ones[0:64, :], sq[1][0:64, 0:TS], False, True, [("sq", 1), "ones"], [("ps", 0)])
                rmax(small[:, 10 + h:11 + h], ps[0][:, :], [("ps", 0)], [("smq", h)])
            for h in range(4):
                ts("dve", small[:, 14 + h:15 + h], small[:, 10 + h:11 + h], small[:, 2:3], 1.05, ALU.mult, ALU.mult,
                   [("smq", h), "small2"], [("negm", h)])
                ts("dve", small[:, 14 + h:15 + h], small[:, 14 + h:15 + h], 1e-20, None, ALU.max, None, [("negm", h)], [("negm", h)])
                act(small[:, 14 + h:15 + h], small[:, 14 + h:15 + h], AF.Ln, [("negm", h)], [("negm", h)])
                act(small[:, 14 + h:15 + h], small[:, 14 + h:15 + h], AF.Exp, [("negm", h)], [("negm", h)], scale=0.5)
                ts("dve", small[:, 14 + h:15 + h], small[:, 14 + h:15 + h], -1.0, None, ALU.mult, None, [("negm", h)], [("negm", h)])
            tt("dve", small[:, 18:19], small[:, 6:7], small[:, 7:8], ALU.mult, ["small6", "small7"], ["small18"])
            ts("dve", small[:, 18:19], small[:, 18:19], 1.05, 1e-20, ALU.mult, ALU.max, ["small18"], ["small18"])
            act(small[:, 18:19], small[:, 18:19], AF.Ln, ["small18"], ["small18"])
            act(small[:, 18:19], small[:, 18:19], AF.Exp, ["small18"], ["small18"], scale=0.5)
            tt("dve", small[:, 18:19], small[:, 18:19], small[:, 3:4], ALU.add, ["small18", "small3"], ["small18"])
            tt("dve", small[:, 18:19], small[:, 18:19], small[:, 4:5], ALU.max, ["small18", "small4"], ["small18"])
            ts("dve", small[:, 18:19], small[:, 18:19], -1.0, None, ALU.mult, None, ["small18"], ["small18"])
            for hh in range(8):
                act(psink[0:1, hh * 128:(hh + 1) * 128], small[0:1, 20 + hh:21 + hh].to_broadcast([1, 128]), AF.Exp,
                    ["sinks8", "small18", "psink"], ["psink"], bias=small[0:1, 18:19])

            SSA, SSB = 5, 6
            for qb in range(4):
                for g in range(2):
                    for w_, kblk in ((0, qb), (1, qb + 1)):
                        pS = w_
                        mm(ps[pS][:, :].rearrange("p (a b) -> p a b", a=4), ka[:, g, kblk * 128:(kblk + 1) * 128],
                           qa[:, 4 * g:4 * g + 4, qb * 128:(qb + 1) * 128], True, True, ["ka", "qa_all"], [("ps", pS)])
                        tt("dve", t32[w_][:, :].rearrange("p (a b) -> p a b", a=4), ps[pS][:, :].rearrange("p (a b) -> p a b", a=4),
                           biasT[:, w_, 4 * g:4 * g + 4, :], ALU.add, [("ps", pS), "biasT"], [("t32", w_)])
                        act(pT[w_], t32[w_][:, :], AF.Exp, [("t32", w_), "small18"], [("pT", w_)], bias=small[:, 18:19])
                    for w_, kblk in ((0, qb), (1, qb + 1)):
                        mm(ps[2][0:64, :], va[:, kblk, g * 64:(g + 1) * 64], pT[w_], w_ == 0, w_ == 1, ["va", ("pT", w_)], [("ps", 2)])
                    for w_, kblk in ((0, qb), (1, qb + 1)):
                        mm(ps[3][0:64, :], onesv[:, kblk, :], pT[w_], w_ == 0, False, ["onesv", ("pT", w_)], [("ps", 3)])
                    mm(ps[3][0:64, :], ones_row[0:1, :], psink[0:1, g * 512:(g + 1) * 512], False, True, ["ones_row", "psink"], [("ps", 3)])
                    act(t32[2][0:64, :], ps[3][0:64, :], AF.Ln, [("ps", 3)], [("t32", 2)])
                    act(t32[2][0:64, :], t32[2][0:64, :], AF.Exp, [("t32", 2)], [("t32", 2)], scale=-1.0)
                    tt("dve", t32[3][0:64, :], ps[2][0:64, :], t32[2][0:64, :], ALU.mult, [("ps", 2), ("t32", 2)], [("t32", 3)])
                    act(sq[2][0:64, 0:TS], t32[3][0:64, :], AF.Square, [("t32", 3)], [("sq", 2)])
                    for hh in range(4):
                        first = (g == 0 and hh == 0)
                        last = (g == 1 and hh == 3)
                        mm(ps[SSA][:, qb * 128:(qb + 1) * 128], ones[0:64, :], sq[2][0:64, hh * 128:(hh + 1) * 128], first, last,
                           [("sq", 2), "ones"], [("ps", SSA)])
                    for hh in range(4):
                        ts("dve", out_a[:, 4 * g + hh, qb * 128:(qb + 1) * 128], t32[3][0:64, hh * 128:(hh + 1) * 128],
                           C(31 + 4 * g + hh, 1, 64), None, ALU.mult, None, [("t32", 3), "consts", "R3_oa"], ["R3_oa"])

            rsqrt_from_ps(t32[0][:, :], ps[SSA][:, :], TS, 1.0 / 512, ("ps", SSA), ("t32", 0), lnt[:, 0:TS], "lnt")
            dma("pool", woa, woa_v, [], ["woa"], "d_woa")
            dma("pool", wob, wob_v, [], ["wob"], "d_wob")
            SBK = [0, 1, 4]
            ABK = [2, 5]
            Anb = [(An, "An"), (qn, "qn")]
            pending = [None, None, None]
            for h in range(4):
                pA = ABK[h % 2]

                def qk(kb, h=h):
                    pS = SBK[kb % 3]
                    half = (kb % 2) * 64
                    col = (kb // 2) * 128
                    mm(ps[pS][:, :], KT[:, kb * 128:(kb + 1) * 128], qlat[:, h, :], True, False, [("KT", kb // 4), ("qlat", h)], [("ps", pS)])
                    mm(ps[pS][:, :], KPE[half:half + 64, col:col + 128], qpe[half:half + 64, h, :], False, True,
                       [("KPEa", kb // 4), ("KPEb", kb // 4), ("qpe", h)], [("ps", pS)])
                qk(0)
                if nk > 1:
                    qk(1)
                for kb in range(nk):
                    pS = SBK[kb % 3]
                    bi = kb % 4
                    if kb + 2 < nk:
                        qk(kb + 2)
                    act(pT[bi], ps[pS][:, :], AF.Exp, [("ps", pS), ("negm", h)], [("pT", bi)], bias=small[:, 14 + h:15 + h])
                    if kb >= nfull:
                        stt("dve", pT[bi], tq_b[:, :], ukeys[:, kb:kb + 1], pT[bi], ALU.is_ge, ALU.mult,
                            ["tq_b", "ukeys", ("pT", bi)], [("pT", bi)])
                    mm(ps[pA][:, :], V[:, kb, :], pT[bi], kb == 0, kb == nk - 1, [("V", kb // 4), ("pT", bi)], [("ps", pA)])
                    mm(ps[3][:, :], ones[:, :], pT[bi], kb == 0, kb == nk - 1, [("pT", bi), "ones"], [("ps", 3)])
                    for (kx, idx) in ((1, 0), (4, 1), (7, 2)):
                        if kb == min(kx, nk - 1) and pending[idx] is not None:
                            pending[idx]()
                            pending[idx] = None
                ts("dve", t32[2][:, :], ps[3][:, :], 1e-30, None, ALU.max, None, [("ps", 3)], [("t32", 2)])
                An_ap, An_k = Anb[h % 2]

                def part0(h=h, pA=pA, An_ap=An_ap, An_k=An_k):
                    act(t32[2][:, :], t32[2][:, :], AF.Ln, [("t32", 2)], [("t32", 2)])
                    act(t32[2][:, :], t32[2][:, :], AF.Exp, [("t32", 2)], [("t32", 2)], scale=-1.0)
                    tt("dve", An_ap, ps[pA][:, :], t32[2][:, :], ALU.mult, [("ps", pA), ("t32", 2)], [An_k])

                def part1(h=h, pA=pA, An_ap=An_ap, An_k=An_k):
                    mm(ps[pA][:, :], wv[:, h * 256 + 128:h * 256 + 256], An_ap, True, True, ["wv", An_k], [("ps", pA)])
                    cp("act", t32[3][:, :], ps[pA][:, :], [("ps", pA)], [("t32", 3)])
                    act(sq[0][:, 0:TS], t32[3][:, :], AF.Square, [("t32", 3)], [("sq", 0)])
                    ts("dve", o_b[:, h, :], t32[3][:, :], C(27 + h), None, ALU.mult, None, [("t32", 3), "consts"], ["R3_ob"])

                def part2(h=h):
                    mm(ps[SSB][:, :], ones[:, :], sq[0][:, 0:TS], h == 0, h == 3, [("sq", 0), "ones"], [("ps", SSB)])
                if h < 3:
                    pending[0], pending[1], pending[2] = part0, part1, part2
                else:
                    part0(); part1(); part2()
            if dbg and j == 0:
                for hh in range(8):
                    cp("dve", t32[0][0:64, :], out_a[:, hh, :], ["R3_oa"], [("t32", 0)])
                    dma("sp", dbg_d["oa"].ap()[:, hh * TS:(hh + 1) * TS], t32[0][0:64, :], [("t32", 0)], [], "dbg3")
                for hh in range(4):
                    cp("dve", t32[0][:, :], o_b[:, hh, :], ["R3_ob"], [("t32", 0)])
                    dma("sp", dbg_d["ob"].ap()[:, hh * TS:(hh + 1) * TS], t32[0][:, :], [("t32", 0)], [], "dbg4")

            rsqrt_from_ps(t32[1][:, :], ps[SSB][:, :], TS, 1.0 / 512, ("ps", SSB), ("t32", 1), lnt[:, 0:TS], "lnt")
            for m in range(KC):
                pa_, pb_ = (0, 1) if m % 2 == 0 else (2, 3)
                for hh in range(8):
                    mm(ps[pa_][:, :], woa[:, hh, m * 128:(m + 1) * 128], out_a[:, hh, :], hh == 0, hh == 7, ["woa", "R3_oa"], [("ps", pa_)])
                for hh in range(4):
                    mm(ps[pb_][:, :], wob[:, hh, m * 128:(m + 1) * 128], o_b[:, hh, :], hh == 0, hh == 3, ["wob", "R3_ob"], [("ps", pb_)])
                tt("dve", t32[2][:, :], ps[pa_][:, :], t32[0][:, :], ALU.mult, [("ps", pa_), ("t32", 0)], [("t32", 2)])
                tt("dve", t32[3][:, :], ps[pb_][:, :], t32[1][:, :], ALU.mult, [("ps", pb_), ("t32", 1)], [("t32", 3)])
                tt("pool", t32[2][:, :], t32[2][:, :], t32[3][:, :], ALU.add, [("t32", 2), ("t32", 3)], [("t32", 2)])
                tt("pool", xs[:, m, :], xs[:, m, :], t32[2][:, :], ALU.add, ["xs", ("t32", 2)], ["xs"])
            if dbg and j == 0:
                dma("sp", dbg_d["x1"].ap().rearrange("(a p) t -> p a t", p=128), xs[:, :, :], ["xs"], [], "dbg5")

            for kc in range(KC):
                s_ = sq[kc % 3]
                act(s_[:, 0:TS], xs[:, kc, :], AF.Square, ["xs"], [("sq", kc % 3)])
                mm(ps[4][:, :], ones[:, :], s_[:, 0:TS], kc == 0, kc == KC - 1, [("sq", kc % 3), "ones"], [("ps", 4)])
            rsqrt_from_ps(rstd[:, 0:TS], ps[4][:, :], TS, 1.0 / D, ("ps", 4), "rstd", lnt[:, 0:TS], "lnt")
            for kc in range(KC):
                eng = "dve"
                stt(eng, h1[:, kc, :], xs[:, kc, :], C(8 + kc), rstd[:, 0:TS], ALU.mult, ALU.mult,
                    ["xs", "consts", "rstd"], ["h1"])
            NV = TS - 2
            groups = [(0, 8), (8, 16), (16, NFC)]
            wd_loads = [(pss, g0, g1) for pss in range(2) for (g0, g1) in groups]

            def load_wd(n):
                pss, g0, g1 = wd_loads[n]
                src = wdn_s.ap()[pss, g0:g1].rearrange("i p c -> p i c")
                dma("sp", wdb[n % 2][:, 0:g1 - g0, :], src, [("wdn_s", pss, i) for i in range(g0, g1)], [("wdb", n % 2)], "d_wdb%d" % (n % 2))
            load_wd(0)
            load_wd(1)
            for i in range(NFC):
                wb = i % 4
                par = i % 2
                dma("sp", wu[wb][:, 0:KC, :].rearrange("p a c -> p (a c)"), wup_s.ap()[i], [("wup_s", i)], [("wug", wb)], "d_wug%d" % wb)
                dma("sp", wu[wb][:, KC:2 * KC, :].rearrange("p a c -> p (a c)"), wup_s.ap()[NFC + i], [("wup_s", NFC + i)], [("wuv", wb)], "d_wuv%d" % wb)
                pg, pv = [(0, 1), (2, 3), (4, 5)][i % 3]
                for kc in range(KC):
                    mm(ps[pg][:, :], wu[wb][:, kc, :], h1[:, kc, :], kc == 0, kc == KC - 1, [("wug", wb), "h1"], [("ps", pg)])
                for kc in range(KC):
                    mm(ps[pv][:, :], wu[wb][:, KC + kc, :], h1[:, kc, :], kc == 0, kc == KC - 1, [("wuv", wb), "h1"], [("ps", pv)])
                for (pi, ch, ta) in ((pg, i, 2 * par), (pv, NFC + i, 2 * par + 1)):
                    cw = lambda jj: C(40 + jj * 44 + ch)
                    act(t32[ta][:, 0:NV], ps[pi][:, 0:NV], AF.Identity, [("ps", pi), "consts"], [("t32", ta)], bias=C(172 + ch), scale=cw(0))
                    stt("dve", t32[ta][:, 0:NV], ps[pi][:, 1:NV + 1], cw(1), t32[ta][:, 0:NV], ALU.mult, ALU.add,
                        [("ps", pi), "consts", ("t32", ta)], [("t32", ta)])
                    stt("dve", t32[ta][:, 0:NV], ps[pi][:, 2:NV + 2], cw(2), t32[ta][:, 0:NV], ALU.mult, ALU.add,
                        [("ps", pi), "consts", ("t32", ta)], [("t32", ta)])
                act(sgb[par][:, 0:NV], t32[2 * par][:, 0:NV], AF.Silu, [("t32", 2 * par)], [("sgb", par)])
                tt("pool", aT[:, i, 0:NV], sgb[par][:, 0:NV], t32[2 * par + 1][:, 0:NV], ALU.mult, [("sgb", par), ("t32", 2 * par + 1)], [("aT", i)])
            for n, (pss, g0, g1) in enumerate(wd_loads):
                for i in range(g0, g1):
                    for m in range(4):
                        pi = 4 + m if m < 3 else 0
                        mm(ps[pi][:, 0:NV], wdb[n % 2][:, i - g0, m * 128:(m + 1) * 128], aT[:, i, 0:NV], i == 0, i == NFC - 1,
                           [("wdb", n % 2), ("aT", i)], [("ps", pi)])
                if n + 2 < len(wd_loads):
                    load_wd(n + 2)
                if g1 == NFC:
                    for m in range(4):
                        pi = 4 + m if m < 3 else 0
                        kc = pss * 4 + m
                        tt("dve", xs[:, kc, 2:TS], xs[:, kc, 2:TS], ps[pi][:, 0:NV], ALU.add, ["xs", ("ps", pi)], ["xs"])
            for kc in range(KC):
                s_ = sq[kc % 3]
                act(s_[:, 0:NV], xs[:, kc, 2:TS], AF.Square, ["xs"], [("sq", kc % 3)])
                mm(ps[1][:, 0:NV], ones[:, :], s_[:, 0:NV], kc == 0, kc == KC - 1, [("sq", kc % 3), "ones"], [("ps", 1)])
            rsqrt_from_ps(rstd[:, 0:NV], ps[1][:, 0:NV], NV, 1.0 / D, ("ps", 1), "rstd", lnt[:, 0:NV], "lnt")
            for kc in range(KC):
                eng = "dve"
                stt(eng, xs[:, kc, 2:TS], xs[:, kc, 2:TS], C(16 + kc), rstd[:, 0:NV], ALU.mult, ALU.mult,
                    ["xs", "consts", "rstd"], ["xs"])
            outs.append(dma("sp", out_v[j], xs[:, :, :], ["xs"], [], "d_out"))

        final = [i for i in S.order if i.dma is not None and i.dma.startswith("dbg")] + outs
        S.finalize(final)
        sems = {}
        for en in ("pe", "act", "dve", "pool"):
            sems[("eng", en)] = es.enter_context(nc.semaphore("s_" + en))
        for k in S.dma_keys():
            sems[("dma", k)] = es.enter_context(nc.semaphore("sd_" + k))
        block = es.enter_context(nc.Block())

        @block.sync
        def _(e):
            S.emit_stream("sp", e, sems, extra_waits=final)

        @block.tensor
        def _(e):
            S.emit_stream("pe", e, sems)

        @block.scalar
        def _(e):
            S.emit_stream("act", e, sems)

        @block.vector
        def _(e):
            S.emit_stream("dve", e, sems)

        @block.gpsimd
        def _(e):
            S.emit_stream("pool", e, sems)
    return nc


def _t5_bucket_static():
    d = np.arange(256)
    n = np.maximum(d, 0)
    large = 16 + (np.log(np.maximum(n, 1).astype(np.float32) / 16) / math.log(128 / 16) * 16).astype(np.int32)
    large = np.minimum(large, 31)
    return np.where(n < 16, n, large)


def make_in_maps(inp, nslot=NSLOT):
    x = np.asarray(inp["x"], np.float32)
    pos = np.asarray(inp["positions"], np.int32)
    w_in = np.asarray(inp["w_in"], np.float32)[0]
    w_qb = np.asarray(inp["w_q_b"], np.float32)[0]
    w_kv = np.asarray(inp["w_kv_b"], np.float32)[0]
    w_out = np.asarray(inp["w_out"], np.float32)[0]
    w_up = np.asarray(inp["w_up"], np.float32)[0]
    w_dn = np.asarray(inp["w_down"], np.float32)[0]
    relb = np.asarray(inp["rel_bias_table"], np.float32)
    sinks = np.asarray(inp["sinks"], np.float32)[0]
    g_attn = np.asarray(inp["attn_norm_g"], np.float32)[0]
    g_ffn = np.asarray(inp["ffn_norm_g"], np.float32)[0]
    g_fin = np.asarray(inp["final_norm_g"], np.float32)
    g_q = np.asarray(inp["q_norm_g"], np.float32)[0]
    g_kv = np.asarray(inp["kv_norm_g"], np.float32)[0]
    g_a = np.asarray(inp["a_out_norm_g"], np.float32)[0]
    g_b = np.asarray(inp["b_out_norm_g"], np.float32)[0]
    conv_w = np.asarray(inp["conv_w"], np.float32)[0]
    conv_b = np.asarray(inp["conv_b"], np.float32)[0]

    consts = np.zeros((128, NC_CONST), np.float32)
    consts[:, 0:8] = g_attn.reshape(8, 128).T
    consts[:, 8:16] = g_ffn.reshape(8, 128).T
    consts[:, 16:24] = g_fin.reshape(8, 128).T
    consts[:, 24:26] = g_q.reshape(2, 128).T
    consts[:, 26] = g_kv
    consts[:, 27:31] = g_b.reshape(4, 128).T
    consts[0:64, 31:39] = g_a.reshape(8, 64).T
    invf = (10000.0 ** (-np.arange(0, 64, 2, dtype=np.float32) / 64)).astype(np.float32)
    consts[:, 39] = np.tile(invf, 4)
    for jj in range(3):
        consts[:, 40 + jj * 44:40 + (jj + 1) * 44] = conv_w[jj].reshape(44, 128).T
    consts[:, 172:216] = conv_b.reshape(44, 128).T

    kp1, kp2 = w_in[:, 1152:1184], w_in[:, 1184:1216]
    w_in_ext = np.concatenate([w_in[:, 0:1152], kp1, kp2, kp1, kp2, kp2, kp1, kp2, kp1], axis=1)
    cols = []
    for h in range(4):
        b0 = h * 192
        nope = w_qb[:, b0:b0 + 128]
        p1, p2 = w_qb[:, b0 + 128:b0 + 160], w_qb[:, b0 + 160:b0 + 192]
        cols += [nope, p1, p2, p1, p2, p2, p1, p2, p1]
    w_qb_ext = np.concatenate(cols, axis=1)
    w_kvT = np.ascontiguousarray(w_kv.T)

    bucket = _t5_bucket_static()
    u = np.arange(128)[:, None]
    t = np.arange(128)[None, :]
    biasT = np.full((128, 2, 8, 128), NEGB, np.float32)
    d_prev = t - u + 128
    d_cur = t - u
    for h in range(8):
        vp = relb[bucket[np.clip(d_prev, 0, 255)], h]
        biasT[:, 0, h, :] = np.where(d_prev < 128, vp, NEGB)
        vc = relb[bucket[np.clip(d_cur, 0, 255)], h]
        biasT[:, 1, h, :] = np.where(d_cur >= 0, vc, NEGB)

    common = dict(consts=consts, w_in_ext=np.ascontiguousarray(w_in_ext), w_qb_ext=np.ascontiguousarray(w_qb_ext),
                  w_kvT=w_kvT, w_kv=np.ascontiguousarray(w_kv), w_out=np.ascontiguousarray(w_out),
                  w_up=np.ascontiguousarray(w_up), w_down=np.ascontiguousarray(w_dn),
                  biasT=np.ascontiguousarray(biasT.reshape(128, -1)),
                  sinks=sinks[None, :].copy(), relb=relb.reshape(1, 256).copy(), gkv_row=g_kv[None, :].copy(),
                  ident=np.eye(128, dtype=np.float32))
    maps = []
    for core in range(8):
        b, c = core // 4, core % 4
        xT = np.ascontiguousarray(x[b].T)
        xT_str = np.zeros((NSLOT, D, TX), np.float32)
        pos_str = np.zeros((1, NSLOT * TS), np.int32)
        tq = np.full((1, NSLOT * TS), -1e9, np.float32)
        kval = np.zeros((NSLOT, 128, 5), np.float32)
        for j in range(NSLOT):
            k = stripe_of(c, j)
            if k is None:
                continue
            q0 = OWN * k - 2
            toks = np.arange(q0 - HALO, q0 + TS)
            ok = (toks >= 0) & (toks < SEQ)
            xT_str[j][:, ok] = xT[:, toks[ok]]
            tk = toks[HALO:]
            pos_str[0, j * TS:(j + 1) * TS] = pos[b, np.clip(tk, 0, SEQ - 1)]
            tq[0, j * TS:(j + 1) * TS] = tk.astype(np.float32)
            kval[j] = ok.astype(np.float32).reshape(5, 128).T
        m = dict(common)
        m.update(xT_seq=xT, xT_str=xT_str, pos_seq=pos[b][None, :].copy(), pos_str=pos_str, tq_str=tq, kval_str=kval)
        maps.append(m)
    return maps


def assemble(results):
    out = np.zeros((NB, SEQ, D), np.float32)
    for core in range(8):
        b, c = core // 4, core % 4
        o = results[core]["out"]
        for j in range(NSLOT):
            k = stripe_of(c, j)
            if k is None:
                continue
            t0 = OWN * k
            n = min(OWN, SEQ - t0)
            if n <= 0:
                continue
            out[b, t0:t0 + n, :] = o[j][:, 2:2 + n].T
    return out


def kernel(**inputs):
    nc = build_nc()
    maps = make_in_maps(inputs)
    res = run_bass_kernel_spmd(nc, maps, core_ids=list(range(8)))
    return assemble(res.results)
```

```python
import math
from contextlib import ExitStack

import numpy as np
import concourse.bass as bass
import concourse.mybir as mybir
from concourse.bass_utils import run_bass_kernel_spmd

F32 = mybir.dt.float32
BF16 = mybir.dt.bfloat16
I32 = mybir.dt.int32
ALU = mybir.AluOpType
AF = mybir.ActivationFunctionType
AX = mybir.AxisListType

D = 1024
SEQ = 16384
NB = 2
KC = 8
TS = 512
OWN = 510
HALO = 128
TX = TS + HALO
NSLOT = 9
NKT = SEQ // TS
NKB = SEQ // 128
DFF = 2816
NFC = DFF // 128
EPS = 1e-6
SC_A = 64 ** -0.5
SC_B = 192 ** -0.5
NEGB = -30000.0
NC_CONST = 216
DEBUG = False


def stripe_of(c, j):
    if j == 0:
        return 0 if c == 0 else None
    return 4 * j - 3 + c


def slot_extent(j):
    if j == 0:
        return 4, 0
    kmax = 4 * j
    kmin = 4 * j - 3
    last_q = OWN * kmax - 2 + TS - 1
    nk = min(NKB, last_q // 128 + 1)
    q0min = OWN * kmin - 2
    nfull = max(0, (q0min + 1) // 128)
    return nk, min(nfull, nk)


class _Ins:
    __slots__ = ("eng", "fn", "deps", "sig", "dma", "need")

    def __init__(self, eng, fn, dma):
        self.eng = eng
        self.fn = fn
        self.dma = dma
        self.deps = []
        self.sig = None
        self.need = False


class _Set:
    __slots__ = ("eng", "dma")

    def __init__(self):
        self.eng = {}
        self.dma = []

    def add(self, ins):
        if ins.dma is None:
            self.eng[ins.eng] = ins
        else:
            self.dma.append(ins)

    def all(self):
        return list(self.eng.values()) + self.dma


PARENTS = {"hT": "R1_hT", "hk": "hk_all", "qa": "qa_all", "qpe": "qpe_all", "qlat": "qlat_all", "aT": "R1_aT"}


class Sched:
    ENGS = ("pe", "act", "dve", "pool", "sp")

    def __init__(self):
        self.streams = {e: [] for e in self.ENGS}
        self.order = []
        self.lw = {}
        self.rd = {}
        self.rc = {}
        self.pc = {}
        self.alias = {}
        self.batch = set()

    def add_alias(self, keys):
        for k in keys:
            self.alias.setdefault(k, set()).update(x for x in keys if x != k)

    @staticmethod
    def parent(k):
        if isinstance(k, tuple) and k[0] in PARENTS:
            return PARENTS[k[0]]
        return None

    def _g(self, d, k):
        v = d.get(k)
        if v is None:
            v = d[k] = _Set()
        return v

    def add(self, eng, fn, reads=(), writes=(), dma=None, batch=False):
        ins = _Ins(eng, fn, dma)
        if batch:
            self.batch.add(dma)
        deps = {}

        def dep(i):
            if i is not None:
                deps[id(i)] = i

        def depall(st):
            if st is not None:
                for i in st.all():
                    deps[id(i)] = i

        for k in reads:
            p = self.parent(k)
            dep(self.lw.get(k))
            depall(self.pc.get(k))
            if p is not None:
                dep(self.lw.get(p))
        for k in writes:
            p = self.parent(k)
            dep(self.lw.get(k)); depall(self.rd.get(k)); depall(self.rc.get(k)); depall(self.pc.get(k))
            if p is not None:
                dep(self.lw.get(p)); depall(self.rd.get(p))
            for a in self.alias.get(p if p is not None else k, ()):
                dep(self.lw.get(a)); depall(self.rd.get(a)); depall(self.rc.get(a))
        for d in deps.values():
            if d.eng == eng and d.dma is None and dma is None and eng == "pe":
                continue
            d.need = True
            ins.deps.append(d)
        for k in reads:
            p = self.parent(k)
            self._g(self.rd, k).add(ins)
            if p is not None:
                self._g(self.rc, p).add(ins)
        for k in writes:
            p = self.parent(k)
            self.lw[k] = ins
            self.rd[k] = _Set(); self.rc[k] = _Set(); self.pc[k] = _Set()
            if p is not None:
                self._g(self.pc, p).add(ins)
            for a in self.alias.get(p if p is not None else k, ()):
                self._g(self.pc, a).add(ins)
        self.streams[eng].append(ins)
        self.order.append(ins)
        return ins

    def dma_keys(self):
        ks = []
        seen = set()
        for ins in self.order:
            if ins.dma is not None and ins.dma not in seen:
                seen.add(ins.dma)
                ks.append(ins.dma)
        return ks

    def finalize(self, final_wait):
        for ins in final_wait:
            ins.need = True
        cnt = {}
        for ins in self.order:
            if ins.dma in self.batch:
                ins.need = True
            if not ins.need:
                continue
            key = ("dma", ins.dma) if ins.dma is not None else ("eng", ins.eng)
            cnt[key] = cnt.get(key, 0) + (16 if ins.dma is not None else 1)
            ins.sig = (key, cnt[key])
        for ins in self.order:
            if ins.dma in self.batch:
                ins.sig = (("dma", ins.dma), cnt[("dma", ins.dma)])

    def emit_stream(self, eng_name, eng, sems, extra_waits=()):
        waited = {}
        for ins in self.streams[eng_name]:
            for d in ins.deps:
                key, val = d.sig
                if waited.get(key, 0) < val:
                    eng.wait_ge(sems[key], val)
                    waited[key] = val
            bi = ins.fn(eng)
            if ins.sig is not None:
                key, val = ins.sig
                bi.then_inc(sems[key], 16 if key[0] == "dma" else 1)
        for d in extra_waits:
            key, val = d.sig
            if waited.get(key, 0) < val:
                eng.wait_ge(sems[key], val)
                waited[key] = val


def build_nc(nslot=NSLOT, nkt=NKT, dbg=False):
    nc = bass.Bass("TRN2", target_bir_lowering=False)
    S = Sched()

    def din(name, shape, dt=F32):
        return nc.dram_tensor(name, list(shape), dt, kind="ExternalInput")

    xT_seq = din("xT_seq", [D, SEQ])
    xT_str = din("xT_str", [NSLOT, D, TX])
    pos_seq = din("pos_seq", [1, SEQ], I32)
    pos_str = din("pos_str", [1, NSLOT * TS], I32)
    tq_str = din("tq_str", [1, NSLOT * TS])
    kval_str = din("kval_str", [NSLOT, 128, 5])
    consts_d = din("consts", [128, NC_CONST])
    w_in_d = din("w_in_ext", [D, 1408])
    w_qb_d = din("w_qb_ext", [256, 1536])
    w_kvT_d = din("w_kvT", [1024, 128])
    w_kv_d = din("w_kv", [128, 1024])
    w_out_d = din("w_out", [1024, D])
    w_up_d = din("w_up", [D, 2 * DFF])
    w_dn_d = din("w_down", [DFF, D])
    biasT_d = din("biasT", [128, 2 * 8 * 128])
    sinks_d = din("sinks", [1, 8])
    relb_d = din("relb", [1, 256])
    gkv_row_d = din("gkv_row", [1, 128])
    ident_d = din("ident", [128, 128])
    out_d = nc.dram_tensor("out", [NSLOT, D, TS], F32, kind="ExternalOutput")
    wup_s = nc.dram_tensor("wup_s", [2 * NFC, 128, KC * 128], BF16)
    wdn_s = nc.dram_tensor("wdn_s", [2, NFC, 128, 512], BF16)
    dbg_d = {}
    if dbg:
        dbg_d["kt"] = nc.dram_tensor("dbg_kt", [128, 512], F32, kind="ExternalOutput")
        dbg_d["kpe"] = nc.dram_tensor("dbg_kpe", [128, 256], F32, kind="ExternalOutput")
        dbg_d["v"] = nc.dram_tensor("dbg_v", [128, 512], F32, kind="ExternalOutput")
        dbg_d["x1"] = nc.dram_tensor("dbg_x1", [D, TS], F32, kind="ExternalOutput")
        dbg_d["oa"] = nc.dram_tensor("dbg_oa", [64, 8 * TS], F32, kind="ExternalOutput")
        dbg_d["ob"] = nc.dram_tensor("dbg_ob", [128, 4 * TS], F32, kind="ExternalOutput")

    es = ExitStack()
    with es:
        def sb(name, shape, dt):
            return es.enter_context(nc.sbuf_tensor(name, list(shape), dt))

        KT = sb("KT", [128, SEQ], BF16)
        KPE = sb("KPE", [128, SEQ // 2], BF16)
        V = sb("V", [128, NKB, 128], BF16)
        consts = sb("consts_s", [128, NC_CONST], F32)
        wq = sb("wq", [128, 2, 1536], BF16)
        wkT = sb("wkT", [128, 8, 128], BF16)
        wv = sb("wv", [128, 1024], BF16)
        ident = sb("ident_s", [128, 128], BF16)
        ones = sb("ones", [128, 128], BF16)
        ukeys = sb("ukeys", [128, NKB], F32)
        biasT = sb("biasT_s", [128, 2, 8, 128], F32)
        psink = sb("psink", [1, 1024], BF16)
        ones_row = sb("ones_row", [1, 64], BF16)
        small = sb("small", [128, 32], F32)
        xs = sb("xs", [128, KC, TS], F32)
        rstd = sb("rstd", [128, TX], F32)
        lnt = sb("lnt", [128, TX], F32)
        R1 = sb("R1", [128, 24 * 1024 // 2], BF16)
        hT = R1[:, 0:KC * TX].rearrange("p (a b) -> p a b", a=KC)
        wob = R1[:, 0:4 * 1024].rearrange("p (a b) -> p a b", a=4)
        woa = R1[0:64, 4 * 1024:12 * 1024].rearrange("p (a b) -> p a b", a=8)
        aT = R1[:, 0:NFC * TS].rearrange("p (a b) -> p a b", a=NFC)
        R3 = sb("R3", [128, 8 * 1024], BF16)
        wst = R3[:, :].rearrange("p (a b) -> p a b", a=KC)
        out_a = R3[0:64, 0:8 * TS].rearrange("p (a b) -> p a b", a=8)
        o_b = R3[:, 4096:4096 + 4 * TS].rearrange("p (a b) -> p a b", a=4)
        pT = [R3[:, 6144 + i * TS:6144 + (i + 1) * TS] for i in range(4)]
        R2 = sb("R2", [128, 12480], BF16)
        qlat = R2[:, 0:2048].rearrange("p (a b) -> p a b", a=4)
        qpe = R2[:, 2048:4096].rearrange("p (a b) -> p a b", a=4)
        qa = R2[0:64, 4096:8192].rearrange("p (a b) -> p a b", a=8)
        ka = R2[0:64, 8192:8192 + 2 * TX].rearrange("p (a b) -> p a b", a=2)
        va = R2[:, 9472:9472 + 5 * 128].rearrange("p (a b) -> p a b", a=5)
        onesv = R2[:, 10112:10112 + 5 * 64].rearrange("p (a b) -> p a b", a=5)
        cqn = R2[:, 10432:10432 + 2 * TS].rearrange("p (a b) -> p a b", a=2)
        qn = R2[:, 11456:11456 + TS]
        An = R2[:, 11968:11968 + TS]
        h1 = R2[:, 0:KC * TS].rearrange("p (a b) -> p a b", a=KC)
        wu = [R2[:, 4096 + i * 2048:4096 + (i + 1) * 2048].rearrange("p (a b) -> p a b", a=2 * KC) for i in range(4)]
        wdb = [R3[:, i * 4096:(i + 1) * 4096].rearrange("p (a b) -> p a b", a=8) for i in range(2)]
        sgb = [R1[:, 11264 + i * 512:11264 + (i + 1) * 512] for i in range(2)]
        mk = [R2[:, 4096 + i * 512:4096 + (i + 1) * 512] for i in range(3)]
        tq_b = sb("tq_b", [128, TS], F32)
        cosq = R1[:, 8192:9216].bitcast(F32)
        sinq = R1[:, 9216:10240].bitcast(F32)
        t32 = [sb("t32_%d" % i, [128, TS], F32) for i in range(4)]
        pos_i = lnt[:, 0:TS].bitcast(I32)
        sq = [sb("sq%d" % i, [128, TX], BF16) for i in range(3)]
        kval = sb("kval", [128, 5], F32)
        ps = [es.enter_context(nc.psum_tensor("ps%d" % i, [128, TS], F32)) for i in range(7)]
        pst = es.enter_context(nc.psum_tensor("pst", [128, 4, 128], BF16))
        xk = [xs[:, :, :], R1[:, 0:8192].bitcast(F32).rearrange("p (a b) -> p a b", a=KC)]
        xh = R1[:, 6144:8192].bitcast(F32).rearrange("p (a b) -> p a b", a=KC)
        hk = R2[:, 0:KC * TS].rearrange("p (a b) -> p a b", a=KC)
        wk = R3[:, 0:KC * 384].rearrange("p (a b) -> p a b", a=KC)

        S.add_alias(["R1_hT", "R1_aT"])
        for wkey in ("woa", "wob"):
            for X in ("R1_hT", "R1_aT", ("xk", 1)):
                S.add_alias([wkey, X])
        for X in ("xh", ("sgb", 0), ("sgb", 1)):
            S.add_alias(["woa", X])
        S.add_alias([("sgb", 0), "R1_hT"]); S.add_alias([("sgb", 1), "R1_hT"])
        S.add_alias([("xk", 1), "R1_hT"]); S.add_alias([("xk", 1), "R1_aT"])
        S.add_alias(["xh", "R1_aT"]); S.add_alias(["xh", ("xk", 1)])
        S.add_alias(["R3_wst", "R3_oa"]); S.add_alias(["R3_wst", "R3_ob"])
        for i in range(4):
            S.add_alias(["R3_wst", ("pT", i)])
            S.add_alias([("wdb", 1), ("pT", i)])
        S.add_alias(["R3_wst", "R3_wk"]); S.add_alias(["R3_wk", "R3_oa"])
        S.add_alias([("wdb", 0), "R3_wst"]); S.add_alias([("wdb", 0), "R3_oa"]); S.add_alias([("wdb", 0), "R3_wk"])
        S.add_alias([("wdb", 1), "R3_wst"]); S.add_alias([("wdb", 1), "R3_ob"])
        att_keys = ["qlat_all", "qpe_all", "qa_all", "ka", "va", "onesv", "cqn", "qn", "An", "hk_all",
                    ("mk", 0), ("mk", 1), ("mk", 2)]
        ffn_keys = ["h1"] + [("wug", i) for i in range(4)] + [("wuv", i) for i in range(4)]
        for a_ in att_keys:
            for f_ in ffn_keys:
                S.add_alias([a_, f_])
        for i in range(3):
            S.add_alias([("mk", i), "qa_all"])
        S.add_alias(["hk_all", "qlat_all"]); S.add_alias(["hk_all", "qpe_all"])
        S.add_alias([("xk", 0), "xs"])
        S.add_alias(["lnt", "pos_i"])
        for X_ in ("woa", "R1_aT", ("sgb", 0), ("sgb", 1)):
            S.add_alias(["cosq", X_]); S.add_alias(["sinq", X_])

        def C(col, n=1, rows=128):
            return consts[0:rows, col:col + n]

        def mm(out, lhsT, rhs, start, stop, r, w):
            return S.add("pe", lambda e: e.matmul(out, lhsT=lhsT, rhs=rhs, start=start, stop=stop), reads=r, writes=w)

        def act(out, in_, func, r, w, bias=None, scale=None):
            kw = {}
            if bias is not None:
                kw["bias"] = bias
            if scale is not None:
                kw["scale"] = scale
            return S.add("act", lambda e: e.activation(out=out, in_=in_, func=func, **kw), reads=r, writes=w)

        def tt(eng, out, in0, in1, op, r, w):
            return S.add(eng, lambda e: e.tensor_tensor(out=out, in0=in0, in1=in1, op=op), reads=r, writes=w)

        def ts(eng, out, in0, s1, s2, op0, op1, r, w):
            if op1 is None:
                return S.add(eng, lambda e: e.tensor_scalar(out=out, in0=in0, scalar1=s1, scalar2=None, op0=op0), reads=r, writes=w)
            return S.add(eng, lambda e: e.tensor_scalar(out=out, in0=in0, scalar1=s1, scalar2=s2, op0=op0, op1=op1), reads=r, writes=w)

        def stt(eng, out, in0, scalar, in1, op0, op1, r, w):
            return S.add(eng, lambda e: e.scalar_tensor_tensor(out=out, in0=in0, scalar=scalar, in1=in1, op0=op0, op1=op1), reads=r, writes=w)

        def cp(eng, out, in_, r, w):
            if eng == "act":
                return S.add(eng, lambda e: e.activation(out=out, in_=in_, func=AF.Identity), reads=r, writes=w)
            return S.add(eng, lambda e: e.tensor_copy(out=out, in_=in_), reads=r, writes=w)

        def rmax(out, in_, r, w):
            return S.add("dve", lambda e: e.reduce_max(out=out, in_=in_, axis=AX.X), reads=r, writes=w)

        def dma(eng, out, in_, r, w, key, batch=False):
            return S.add(eng, lambda e: e.dma_start(out=out, in_=in_), reads=r, writes=w, dma=key, batch=batch)

        def rsqrt_from_ps(dst, ps_ap, n, inv_n, r_key, w_key, tmp, tmp_key):
            act(tmp, ps_ap, AF.Ln, [r_key], [tmp_key], bias=EPS, scale=inv_n)
            act(dst, tmp, AF.Exp, [tmp_key], [w_key], scale=-0.5)

        def rope_tables(pos_ap_dram, n, cos_t, sin_t, ck, sk, tmpa, tmpa_k, tmpb, tmpb_k, dkey, pbuf=None, pkey=None):
            MAGIC = 12582912.0
            C1 = 6.28125
            C2 = 2 * math.pi - 6.28125
            if pbuf is None:
                pbuf, pkey = pos_i[:, 0:n], "pos_i"
                dma("sp", pbuf, pos_ap_dram, [], [pkey], dkey)
            cp("dve", tmpa, pbuf, [pkey], [tmpa_k])
            ts("dve", tmpa, tmpa, C(39), None, ALU.mult, None, [tmpa_k, "consts"], [tmpa_k])
            ts("dve", tmpb, tmpa, 1.0 / (2 * math.pi), MAGIC, ALU.mult, ALU.add, [tmpa_k], [tmpb_k])
            ts("dve", tmpb, tmpb, MAGIC, None, ALU.subtract, None, [tmpb_k], [tmpb_k])
            stt("dve", tmpa, tmpb, -C1, tmpa, ALU.mult, ALU.add, [tmpa_k, tmpb_k], [tmpa_k])
            stt("dve", tmpa, tmpb, -C2, tmpa, ALU.mult, ALU.add, [tmpa_k, tmpb_k], [tmpa_k])
            ts("dve", tmpa, tmpa, -math.pi, math.pi, ALU.max, ALU.min, [tmpa_k], [tmpa_k])
            stt("dve", tmpb, tmpa, -1.0, tmpa, ALU.mult, ALU.max, [tmpa_k], [tmpb_k])
            act(sin_t, tmpa, AF.Sin, [tmpa_k], [sk])
            act(cos_t, tmpb, AF.Sin, [tmpb_k], [ck], bias=math.pi / 2, scale=-1.0)

        dma("sp", consts[:, :], consts_d.ap(), [], ["consts"], "c_consts")
        dma("sp", biasT[:, :, :, :].rearrange("p a b c -> p (a b c)"), biasT_d.ap(), [], ["biasT"], "c_bias")
        dma("pool", ident[:, :], ident_d.ap(), [], ["ident"], "c_ident")
        dma("pool", wq[:, :, :], w_qb_d.ap().rearrange("(a p) n -> p a n", p=128), [], ["wq"], "c_wq")
        dma("pool", wkT[:, :, :], w_kvT_d.ap().rearrange("(a p) n -> p a n", p=128), [], ["wkT"], "c_wkT")
        dma("pool", wv[:, :], w_kv_d.ap(), [], ["wv"], "c_wv")
        dma("pool", wk, w_in_d.ap()[:, 1024:1408].rearrange("(a p) n -> p a n", p=128), [], ["R3_wk"], "c_wk")
        S.add("pool", lambda e: e.memset(ones[:, :], 1.0), writes=["ones"])
        S.add("pool", lambda e: e.memset(ones_row[:, :], 1.0), writes=["ones_row"])
        S.add("pool", lambda e: e.memset(small[:, :], 0.0), writes=["small"])
        dma("sp", small[0:1, 20:28], sinks_d.ap(), ["small"], ["sinks8"], "c_sink")
        S.add("pool", lambda e: e.iota(ukeys[:, :], pattern=[[128, NKB]], base=0, channel_multiplier=1,
                                        allow_small_or_imprecise_dtypes=True), writes=["ukeys"])
        for o in (256, 320):
            ts("dve", wk[:, :, o:o + 32], wk[:, :, o:o + 32], -1.0, None, ALU.mult, None, ["R3_wk"], ["R3_wk"])
        wq4 = wq[:, :, :].rearrange("p a (h c) -> p a h c", h=4)
        for o in (256, 320):
            for a in range(2):
                ts("dve", wq4[:, a, :, o:o + 32], wq4[:, a, :, o:o + 32], -1.0, None, ALU.mult, None, ["wq"], ["wq"])
        dma("sp", t32[0][:, 0:128], bass.AP(gkv_row_d, 0, [[0, 128], [1, 128]]), [], [("t32", 0)], "c_t0")
        tt("dve", t32[0][:, 0:128], t32[0][:, 0:128], t32[0][:, 0:128], ALU.mult, [("t32", 0)], [("t32", 0)])
        rmax(small[:, 1:2], t32[0][:, 0:128], [("t32", 0)], ["small1"])
        dma("sp", t32[1][:, 0:256], bass.AP(relb_d, 0, [[0, 128], [1, 256]]), [], [("t32", 1)], "c_t1")
        rmax(small[:, 3:4], t32[1][:, 0:256], [("t32", 1)], ["small3"])
        dma("sp", t32[2][:, 0:8], bass.AP(sinks_d, 0, [[0, 128], [1, 8]]), [], [("t32", 2)], "c_t2")
        rmax(small[:, 4:5], t32[2][:, 0:8], [("t32", 2)], ["small4"])
        wup_v = w_up_d.ap().rearrange("(a p) n -> p a n", p=128)
        prep = []
        for i in range(2 * NFC):
            prep.append((wup_s.ap()[i].rearrange("p (a c) -> p a c", a=KC), wup_v[:, :, i * 128:(i + 1) * 128],
                         [("wup_s", i)], "c_wups"))
        for pss in range(2):
            for i in range(NFC):
                prep.append((wdn_s.ap()[pss, i], w_dn_d.ap()[i * 128:(i + 1) * 128, pss * 512:(pss + 1) * 512],
                             [("wdn_s", pss, i)], "c_wdns"))

        def emit_prep(n):
            for _ in range(n):
                if prep:
                    d_, s__, w_, k_ = prep.pop(0)
                    dma("pool", d_, s__, [], w_, k_, batch=True)

        xT_v = xT_seq.ap().rearrange("(a p) t -> p a t", p=128)
        hk2 = [R2[:, b_ * 4096:(b_ + 1) * 4096].rearrange("p (a b) -> p a b", a=KC) for b_ in range(2)]
        rstdK = [(rstd[:, 0:TS], "rstd"), (tq_b[:, :], "tq_b")]
        csK = R1[:, 8192:12288].bitcast(F32)
        cosK = [csK[:, b_ * 1024:b_ * 1024 + 512] for b_ in range(2)]
        sinK = [csK[:, b_ * 1024 + 512:(b_ + 1) * 1024] for b_ in range(2)]
        tAB = R2[:, 8192:12288].bitcast(F32)
        tA, tB = tAB[:, 0:512], tAB[:, 512:1024]
        sqK = [R3[:, 3072 + i * 512:3072 + (i + 1) * 512] for i in range(8)]
        for b_ in range(2):
            for X_ in ("woa", "R1_aT", ("sgb", 0), ("sgb", 1)):
                S.add_alias([("cosK", b_), X_]); S.add_alias([("sinK", b_), X_])
        for X_ in att_keys + ffn_keys:
            S.add_alias(["tA", X_]); S.add_alias(["tB", X_])
        for i in range(8):
            for X_ in ["R3_wst", "R3_oa", "R3_ob", ("wdb", 0), ("wdb", 1)] + [("pT", q_) for q_ in range(4)]:
                S.add_alias([("sqK", i), X_])
        S.add_alias(["hk_all", "qa_all"])
        for i in range(3):
            S.add_alias(["hk_all", ("mk", i)])
        for kc in range(KC):
            ts("dve", wk[:, kc, :], wk[:, kc, :], C(kc), None, ALU.mult, None, ["R3_wk", "consts"], ["R3_wk"])

        xb = [R2[:, b_ * 4096:(b_ + 1) * 4096].rearrange("p (a b) -> p a b", a=KC) for b_ in range(2)]
        S.add_alias(["xb0", "hk_all"]); S.add_alias(["xb1", "hk_all"])
        for X_ in att_keys + ffn_keys:
            S.add_alias(["xb0", X_]); S.add_alias(["xb1", X_])

        posK = [xs[:, b_, :].bitcast(I32) for b_ in range(2)]
        S.add_alias([("posK", 0), "xs"]); S.add_alias([("posK", 1), "xs"])

        def load_pos(it):
            if it < nkt:
                dma("sp", posK[it % 2], bass.AP(pos_seq, it * TS, [[0, 128], [1, TS]]), [], [("posK", it % 2)], "d_posK%d" % (it % 2))
        load_pos(0)

        def kfront(it):
            b = it % 2
            xkk = "xb%d" % b
            X = xb[b]
            t0 = it * TS
            rs_ap, rs_k = rstdK[b]
            dma("pool", X, xT_v[:, :, t0:t0 + TS], [], [xkk], "d_xb%d" % b)
            emit_prep(3)
            load_pos(it + 1)
            for kc in range(KC):
                act(sqK[kc], X[:, kc, :], AF.Square, [xkk], [("sqK", kc)])
                mm(ps[0][:, :], ones[:, :], sqK[kc], kc == 0, kc == KC - 1, [("sqK", kc), "ones"], [("ps", 0)])
            rope_tables(None, TS, cosK[b], sinK[b], ("cosK", b), ("sinK", b),
                        t32[2][:, :], ("t32", 2), t32[3][:, :], ("t32", 3), "d_pos", pbuf=posK[b], pkey=("posK", b))
            rsqrt_from_ps(rs_ap, ps[0][:, :], TS, 1.0 / D, ("ps", 0), rs_k, lnt[:, 0:TS], "lnt")
            tt("dve", cosK[b], cosK[b], rs_ap, ALU.mult, [("cosK", b), rs_k], [("cosK", b)])
            tt("dve", sinK[b], sinK[b], rs_ap, ALU.mult, [("sinK", b), rs_k], [("sinK", b)])

        def kback(it):
            b = it % 2
            xkk = "xb%d" % b
            t0 = it * TS
            rs_ap, rs_k = rstdK[b]
            for (o, pi) in ((0, 1), (128, 2), (256, 3)):
                for kc in range(KC):
                    mm(ps[pi][:, :], wk[:, kc, o:o + 128], xb[b][:, kc, :], kc == 0, kc == KC - 1,
                       ["R3_wk", xkk], [("ps", pi)])
            act(sq[0][:, 0:TS], ps[1][:, :], AF.Square, [("ps", 1)], [("sq", 0)])
            mm(ps[4][:, :], ones[:, :], sq[0][:, 0:TS], True, True, [("sq", 0), "ones"], [("ps", 4)])
            tt("dve", t32[0][:, :], ps[4][:, :], rs_ap, ALU.mult, [("ps", 4), rs_k], [("t32", 0)])
            tt("dve", t32[0][:, :], t32[0][:, :], rs_ap, ALU.mult, [("t32", 0), rs_k], [("t32", 0)])
            act(t32[1][:, :], t32[0][:, :], AF.Ln, [("t32", 0)], [("t32", 1)], bias=EPS, scale=1.0 / 128)
            act(t32[0][:, :], t32[1][:, :], AF.Exp, [("t32", 1)], [("t32", 0)], scale=-0.5)
            tt("dve", t32[0][:, :], t32[0][:, :], rs_ap, ALU.mult, [("t32", 0), rs_k], [("t32", 0)])
            stt("dve", KT[:, t0:t0 + TS], ps[1][:, :], C(26), t32[0][:, :], ALU.mult, ALU.mult,
                [("ps", 1), "consts", ("t32", 0)], [("KT", it)])
            act(sq[1][0:64, 0:TS], ps[2][0:64, :], AF.Square, [("ps", 2)], [("sq", 1)])
            mm(ps[5][:, :], ones[0:64, :], sq[1][0:64, 0:TS], True, True, [("sq", 1), "ones"], [("ps", 5)])
            tt("dve", t32[1][:, :], ps[5][:, :], rs_ap, ALU.mult, [("ps", 5), rs_k], [("t32", 1)])
            tt("dve", t32[1][:, :], t32[1][:, :], rs_ap, ALU.mult, [("t32", 1), rs_k], [("t32", 1)])
            rmax(small[:, 5:6], t32[1][:, :], [("t32", 1)], ["small5"])
            tt("dve", small[:, 0:1], small[:, 0:1], small[:, 5:6], ALU.max, ["small", "small5"], ["small0"])
            tt("dve", tA, ps[2][:, :], cosK[b], ALU.mult, [("ps", 2), ("cosK", b)], ["tA"])
            tt("dve", tB, ps[3][:, :], sinK[b], ALU.mult, [("ps", 3), ("sinK", b)], ["tB"])
            kdst = KPE[:, 2 * it * 128:(2 * it + 2) * 128].rearrange("p (a b) -> p a b", a=2)
            a3 = tA.rearrange("p (a b) -> p a b", a=4)
            b3 = tB.rearrange("p (a b) -> p a b", a=4)
            tt("dve", kdst[0:64, :, :], a3[0:64, 0:4:2, :], b3[0:64, 0:4:2, :], ALU.add, ["tA", "tB"], [("KPEa", it)])
            tt("dve", kdst[64:128, :, :], a3[64:128, 1:4:2, :], b3[64:128, 1:4:2, :], ALU.add, ["tA", "tB"], [("KPEb", it)])
            for q in range(4):
                S.add("pe", (lambda o_=pst[:, q, :], i_=KT[:, t0 + q * 128:t0 + (q + 1) * 128]: (lambda e: e.transpose(o_, i_, ident[:, :])))(),
                      reads=[("KT", it), "ident"], writes=["pst"])
            cp("act", V[:, 4 * it:4 * it + 4, :], pst[:, :, :], ["pst"], [("V", it)])

        kfront(0)
        for it in range(nkt):
            if it + 1 < nkt:
                kfront(it + 1)
            kback(it)
        emit_prep(len(prep))
        ts("dve", small[:, 2:3], small[:, 1:2], 128.0, None, ALU.mult, None, ["small1"], ["small2"])
        tt("dve", small[:, 2:3], small[:, 2:3], small[:, 0:1], ALU.add, ["small2", "small0"], ["small2"])
        ts("dve", small[:, 2:3], small[:, 2:3], 1.05, None, ALU.mult, None, ["small2"], ["small2"])
        if dbg:
            cp("dve", t32[0][:, :], KT[:, 0:TS], [("KT", 0)], [("t32", 0)])
            dma("sp", dbg_d["kt"].ap(), t32[0][:, :], [("t32", 0)], [], "dbg0")
            cp("dve", t32[1][:, 0:256], KPE[:, 0:256], [("KPEa", 0), ("KPEb", 0)], [("t32", 1)])
            dma("sp", dbg_d["kpe"].ap(), t32[1][:, 0:256], [("t32", 1)], [], "dbg1")
            cp("dve", t32[2][:, :], V[:, 0:4, :].rearrange("p a b -> p (a b)"), [("V", 0)], [("t32", 2)])
            dma("sp", dbg_d["v"].ap(), t32[2][:, :], [("t32", 2)], [], "dbg2")

        outs = []
        xstr_v = xT_str.ap().rearrange("j (a p) t -> j p a t", p=128)
        win_v = w_in_d.ap()[:, 0:1024].rearrange("(a p) n -> p a n", p=128)
        woa_v = w_out_d.ap()[0:512, :].rearrange("(h d) n -> d h n", d=64)
        wob_v = w_out_d.ap()[512:1024, :].rearrange("(h p) n -> p h n", p=128)
        out_v = out_d.ap().rearrange("j (a p) t -> j p a t", p=128)
        for j in range(nslot):
            nk, nfull = slot_extent(j)
            dma("sp", xs[:, :, :], xstr_v[j][:, :, HALO:TX], [], ["xs"], "d_xs")
            dma("sp", xh, xstr_v[j][:, :, 0:HALO], [], ["xh"], "d_xh")
            dma("pool", wst, win_v, [], ["R3_wst"], "d_wst")
            dma("sp", tq_b[:, :], bass.AP(tq_str, j * TS, [[0, 128], [1, TS]]), [], ["tq_b"], "d_tq")
            dma("sp", kval[:, :], kval_str.ap()[j], [], ["kval"], "d_kval")
            rope_tables(bass.AP(pos_str, j * TS, [[0, 128], [1, TS]]), TS, cosq[:, :], sinq[:, :], "cosq", "sinq",
                        t32[2][:, :], ("t32", 2), t32[3][:, :], ("t32", 3), "d_pos")
            for kc in range(KC):
                s_ = sq[kc % 3]
                act(s_[:, HALO:TX], xs[:, kc, :], AF.Square, ["xs"], [("sq", kc % 3)])
                act(s_[:, 0:HALO], xh[:, kc, :], AF.Square, ["xh", ("sq", kc % 3)], [("sq", kc % 3)])
                mm(ps[0][:, :], ones[:, :], s_[:, 0:TS], kc == 0, kc == KC - 1, [("sq", kc % 3), "ones"], [("ps", 0)])
                mm(ps[1][:, 0:HALO], ones[:, :], s_[:, TS:TX], kc == 0, kc == KC - 1, [("sq", kc % 3), "ones"], [("ps", 1)])
            rsqrt_from_ps(rstd[:, 0:TS], ps[0][:, :], TS, 1.0 / D, ("ps", 0), "rstd", lnt[:, 0:TS], "lnt")
            rsqrt_from_ps(rstd[:, TS:TX], ps[1][:, 0:HALO], HALO, 1.0 / D, ("ps", 1), "rstd", lnt[:, TS:TX], "lnt")
            for kc in range(KC):
                eng = "dve"
                stt(eng, hT[:, kc, HALO:TX], xs[:, kc, :], C(kc), rstd[:, HALO:TX], ALU.mult, ALU.mult,
                    ["xs", "consts", "rstd"], [("hT", kc)])
                stt(eng, hT[:, kc, 0:HALO], xh[:, kc, :], C(kc), rstd[:, 0:HALO], ALU.mult, ALU.mult,
                    ["xh", "consts", "rstd", ("hT", kc)], [("hT", kc)])
            for blk in range(5):
                cp("dve", onesv[:, blk, :], kval[:, blk:blk + 1].to_broadcast([128, 64]), ["kval"], ["onesv"])
            hS = lambda kc: hT[:, kc, HALO:TX]
            rot = [2, 3, 4, 5, 6]
            rr = [0]

            def nxt():
                rr[0] = (rr[0] + 1) % len(rot)
                return rot[rr[0]]

            S.add("pool", lambda e: e.memset(small[:, 6:10], 0.0), writes=["small6", "small7", "small8", "small9"])
            for h in range(8):
                pi = nxt()
                for kc in range(KC):
                    mm(ps[pi][0:64, :], wst[:, kc, h * 64:(h + 1) * 64], hS(kc), kc == 0, kc == KC - 1,
                       ["R3_wst", ("hT", kc)], [("ps", pi)])
                act(qa[:, h, :], ps[pi][0:64, :], AF.Identity, [("ps", pi)], [("qa", h)], scale=SC_A)
                act(sq[h % 3][0:64, 0:TS], qa[:, h, :], AF.Square, [("qa", h)], [("sq", h % 3)])
                mm(ps[1][:, :], ones[0:64, :], sq[h % 3][0:64, 0:TS], True, True, [("sq", h % 3), "ones"], [("ps", 1)])
                rmax(small[:, 5:6], ps[1][:, :], [("ps", 1)], ["small5"])
                tt("dve", small[:, 6:7], small[:, 6:7], small[:, 5:6], ALU.max, ["small6", "small5"], ["small6"])
            for g in range(2):
                pi = nxt()
                pj = nxt()
                for kc in range(KC):
                    mm(ps[pi][0:64, :], wst[:, kc, 512 + g * 64:512 + (g + 1) * 64], hT[:, kc, 0:TS], kc == 0, kc == KC - 1,
                       ["R3_wst", ("hT", kc)], [("ps", pi)])
                for kc in range(KC):
                    mm(ps[pj][0:64, 0:HALO], wst[:, kc, 512 + g * 64:512 + (g + 1) * 64], hT[:, kc, TS:TX], kc == 0, kc == KC - 1,
                       ["R3_wst", ("hT", kc)], [("ps", pj)])
                act(ka[:, g, 0:TS], ps[pi][0:64, :], AF.Identity, [("ps", pi)], ["ka"])
                act(ka[:, g, TS:TX], ps[pj][0:64, 0:HALO], AF.Identity, [("ps", pj)], ["ka"])
                act(sq[g][0:64, :], ka[:, g, :], AF.Square, ["ka"], [("sq", g)])
                mm(ps[0][:, :], ones[0:64, :], sq[g][0:64, 0:TS], True, True, [("sq", g), "ones"], [("ps", 0)])
                mm(ps[1][:, 0:HALO], ones[0:64, :], sq[g][0:64, TS:TX], True, True, [("sq", g), "ones"], [("ps", 1)])
                rmax(small[:, 5:6], ps[0][:, :], [("ps", 0)], ["small5"])
                tt("dve", small[:, 7:8], small[:, 7:8], small[:, 5:6], ALU.max, ["small7", "small5"], ["small7"])
                rmax(small[:, 5:6], ps[1][:, 0:HALO], [("ps", 1)], ["small5"])
                tt("dve", small[:, 7:8], small[:, 7:8], small[:, 5:6], ALU.max, ["small7", "small5"], ["small7"])
            for blk in range(5):
                pi = nxt()
                for kc in range(KC):
                    mm(ps[pi][:, 0:128], hT[:, kc, blk * 128:(blk + 1) * 128], wst[:, kc, 640:768], kc == 0, kc == KC - 1,
                       ["R3_wst", ("hT", kc)], [("ps", pi)])
                cp("dve", va[:, blk, :], ps[pi][:, 0:128], [("ps", pi)], ["va"])
            pq = [nxt(), nxt()]
            for c2 in range(2):
                for kc in range(KC):
                    mm(ps[pq[c2]][:, :], wst[:, kc, 768 + c2 * 128:768 + (c2 + 1) * 128], hS(kc), kc == 0, kc == KC - 1,
                       ["R3_wst", ("hT", kc)], [("ps", pq[c2])])
                act(sq[c2][:, 0:TS], ps[pq[c2]][:, :], AF.Square, [("ps", pq[c2])], [("sq", c2)])
                mm(ps[0][:, :], ones[:, :], sq[c2][:, 0:TS], c2 == 0, c2 == 1, [("sq", c2), "ones"], [("ps", 0)])
            rsqrt_from_ps(t32[0][:, :], ps[0][:, :], TS, 1.0 / 256, ("ps", 0), ("t32", 0), t32[1][:, :], ("t32", 1))
            for c2 in range(2):
                stt("dve", cqn[:, c2, :], ps[pq[c2]][:, :], C(24 + c2), t32[0][:, :], ALU.mult, ALU.mult,
                    [("ps", pq[c2]), "consts", ("t32", 0)], ["cqn"])
            S.add("pool", lambda e: e.memset(small[:, 10:14], 0.0), writes=[("smq", 0), ("smq", 1), ("smq", 2), ("smq", 3)])
            for h in range(4):
                pn, pp, pr = nxt(), nxt(), nxt()
                for (o, pi) in ((0, pn), (128, pp), (256, pr)):
                    for c2 in range(2):
                        mm(ps[pi][:, :], wq[:, c2, h * 384 + o:h * 384 + o + 128], cqn[:, c2, :], c2 == 0, c2 == 1,
                           ["wq", "cqn"], [("ps", pi)])
                cp("act", qn, ps[pn][:, :], [("ps", pn)], ["qn"])
                tt("dve", t32[0][:, :], ps[pp][:, :], cosq[:, :], ALU.mult, [("ps", pp), "cosq"], [("t32", 0)])
                tt("dve", t32[1][:, :], ps[pr][:, :], sinq[:, :], ALU.mult, [("ps", pr), "sinq"], [("t32", 1)])
                tt("dve", t32[0][:, :], t32[0][:, :], t32[1][:, :], ALU.add, [("t32", 0), ("t32", 1)], [("t32", 0)])
                act(qpe[:, h, :], t32[0][:, :], AF.Identity, [("t32", 0)], [("qpe", h)], scale=SC_B)
                pa = nxt()
                mm(ps[pa][:, :], wkT[:, 2 * h, :], qn, True, True, ["wkT", "qn"], [("ps", pa)])
                act(qlat[:, h, :], ps[pa][:, :], AF.Identity, [("ps", pa)], [("qlat", h)], scale=SC_B)
                act(sq[0][:, 0:TS], qlat[:, h, :], AF.Square, [("qlat", h)], [("sq", 0)])
                act(sq[1][0:64, 0:TS], qpe[0:64, h, :], AF.Square, [("qpe", h)], [("sq", 1)])
                mm(ps[0][:, :], ones[:, :], sq[0][:, 0:TS], True, False, [("sq", 0), "ones"], [("ps", 0)])
                mm(ps[0][:, :], ones[0:64, :], sq[1][0:64, 0:TS], False, True, [("sq", 1), "ones"], [("ps", 0)])
                rmax(small[:, 10 + h:11 + h], ps[0][:, :], [("ps", 0)], [("smq", h)])
            for h in range(4):
                ts("dve", small[:, 14 + h:15 + h], small[:, 10 + h:11 + h], small[:, 2:3], 1.05, ALU.mult, ALU.mult,
                   [("smq", h), "small2"], [("negm", h)])
                ts("dve", small[:, 14 + h:15 + h], small[:, 14 + h:15 + h], 1e-20, None, ALU.max, None, [("negm", h)], [("negm", h)])
                act(small[:, 14 + h:15 + h], small[:, 14 + h:15 + h], AF.Ln, [("negm", h)], [("negm", h)])
                act(small[:, 14 + h:15 + h], small[:, 14 + h:15 + h], AF.Exp, [("negm", h)], [("negm", h)], scale=0.5)
                ts("dve", small[:, 14 + h:15 + h], small[:, 14 + h:15 + h], -1.0, None, ALU.mult, None, [("negm", h)], [("negm", h)])
            tt("dve", small[:, 18:19], small[:, 6:7], small[:, 7:8], ALU.mult, ["small6", "small7"], ["small18"])
            ts("dve", small[:, 18:19], small[:, 18:19], 1.05, 1e-20, ALU.mult, ALU.max, ["small18"], ["small18"])
            act(small[:, 18:19], small[:, 18:19], AF.Ln, ["small18"], ["small18"])
            act(small[:, 18:19], small[:, 18:19], AF.Exp, ["small18"], ["small18"], scale=0.5)
            tt("dve", small[:, 18:19], small[:, 18:19], small[:, 3:4], ALU.add, ["small18", "small3"], ["small18"])
            tt("dve", small[:, 18:19], small[:, 18:19], small[:, 4:5], ALU.max, ["small18", "small4"], ["small18"])
            ts("dve", small[:, 18:19], small[:, 18:19], -1.0, None, ALU.mult, None, ["small18"], ["small18"])
            for hh in range(8):
                act(psink[0:1, hh * 128:(hh + 1) * 128], small[0:1, 20 + hh:21 + hh].to_broadcast([1, 128]), AF.Exp,
                    ["sinks8", "small18", "psink"], ["psink"], bias=small[0:1, 18:19])

            SSA, SSB = 5, 6
            for qb in range(4):
                for g in range(2):
                    for w_, kblk in ((0, qb), (1, qb + 1)):
                        pS = w_
                        mm(ps[pS][:, :].rearrange("p (a b) -> p a b", a=4), ka[:, g, kblk * 128:(kblk + 1) * 128],
                           qa[:, 4 * g:4 * g + 4, qb * 128:(qb + 1) * 128], True, True, ["ka", "qa_all"], [("ps", pS)])
                        tt("dve", t32[w_][:, :].rearrange("p (a b) -> p a b", a=4), ps[pS][:, :].rearrange("p (a b) -> p a b", a=4),
                           biasT[:, w_, 4 * g:4 * g + 4, :], ALU.add, [("ps", pS), "biasT"], [("t32", w_)])
                        act(pT[w_], t32[w_][:, :], AF.Exp, [("t32", w_), "small18"], [("pT", w_)], bias=small[:, 18:19])
                    for w_, kblk in ((0, qb), (1, qb + 1)):
                        mm(ps[2][0:64, :], va[:, kblk, g * 64:(g + 1) * 64], pT[w_], w_ == 0, w_ == 1, ["va", ("pT", w_)], [("ps", 2)])
                    for w_, kblk in ((0, qb), (1, qb + 1)):
                        mm(ps[3][0:64, :], onesv[:, kblk, :], pT[w_], w_ == 0, False, ["onesv", ("pT", w_)], [("ps", 3)])
                    mm(ps[3][0:64, :], ones_row[0:1, :], psink[0:1, g * 512:(g + 1) * 512], False, True, ["ones_row", "psink"], [("ps", 3)])
                    act(t32[2][0:64, :], ps[3][0:64, :], AF.Ln, [("ps", 3)], [("t32", 2)])
                    act(t32[2][0:64, :], t32[2][0:64, :], AF.Exp, [("t32", 2)], [("t32", 2)], scale=-1.0)
                    tt("dve", t32[3][0:64, :], ps[2][0:64, :], t32[2][0:64, :], ALU.mult, [("ps", 2), ("t32", 2)], [("t32", 3)])
                    act(sq[2][0:64, 0:TS], t32[3][0:64, :], AF.Square, [("t32", 3)], [("sq", 2)])
                    for hh in range(4):
                        first = (g == 0 and hh == 0)
                        last = (g == 1 and hh == 3)
                        mm(ps[SSA][:, qb * 128:(qb + 1) * 128], ones[0:64, :], sq[2][0:64, hh * 128:(hh + 1) * 128], first, last,
                           [("sq", 2), "ones"], [("ps", SSA)])
                    for hh in range(4):
                        ts("dve", out_a[:, 4 * g + hh, qb * 128:(qb + 1) * 128], t32[3][0:64, hh * 128:(hh + 1) * 128],
                           C(31 + 4 * g + hh, 1, 64), None, ALU.mult, None, [("t32", 3), "consts", "R3_oa"], ["R3_oa"])

            rsqrt_from_ps(t32[0][:, :], ps[SSA][:, :], TS, 1.0 / 512, ("ps", SSA), ("t32", 0), lnt[:, 0:TS], "lnt")
            dma("pool", woa, woa_v, [], ["woa"], "d_woa")
            dma("pool", wob, wob_v, [], ["wob"], "d_wob")
            SBK = [0, 1, 4]
            ABK = [2, 5]
            Anb = [(An, "An"), (qn, "qn")]
            pending = [None, None, None]
            for h in range(4):
                pA = ABK[h % 2]

                def qk(kb, h=h):
                    pS = SBK[kb % 3]
                    half = (kb % 2) * 64
                    col = (kb // 2) * 128
                    mm(ps[pS][:, :], KT[:, kb * 128:(kb + 1) * 128], qlat[:, h, :], True, False, [("KT", kb // 4), ("qlat", h)], [("ps", pS)])
                    mm(ps[pS][:, :], KPE[half:half + 64, col:col + 128], qpe[half:half + 64, h, :], False, True,
                       [("KPEa", kb // 4), ("KPEb", kb // 4), ("qpe", h)], [("ps", pS)])
                qk(0)
                if nk > 1:
                    qk(1)
                for kb in range(nk):
                    pS = SBK[kb % 3]
                    bi = kb % 4
                    if kb + 2 < nk:
                        qk(kb + 2)
                    act(pT[bi], ps[pS][:, :], AF.Exp, [("ps", pS), ("negm", h)], [("pT", bi)], bias=small[:, 14 + h:15 + h])
                    if kb >= nfull:
                        stt("dve", pT[bi], tq_b[:, :], ukeys[:, kb:kb + 1], pT[bi], ALU.is_ge, ALU.mult,
                            ["tq_b", "ukeys", ("pT", bi)], [("pT", bi)])
                    mm(ps[pA][:, :], V[:, kb, :], pT[bi], kb == 0, kb == nk - 1, [("V", kb // 4), ("pT", bi)], [("ps", pA)])
                    mm(ps[3][:, :], ones[:, :], pT[bi], kb == 0, kb == nk - 1, [("pT", bi), "ones"], [("ps", 3)])
                    for (kx, idx) in ((1, 0), (4, 1), (7, 2)):
                        if kb == min(kx, nk - 1) and pending[idx] is not None:
                            pending[idx]()
                            pending[idx] = None
                ts("dve", t32[2][:, :], ps[3][:, :], 1e-30, None, ALU.max, None, [("ps", 3)], [("t32", 2)])
                An_ap, An_k = Anb[h % 2]

                def part0(h=h, pA=pA, An_ap=An_ap, An_k=An_k):
                    act(t32[2][:, :], t32[2][:, :], AF.Ln, [("t32", 2)], [("t32", 2)])
                    act(t32[2][:, :], t32[2][:, :], AF.Exp, [("t32", 2)], [("t32", 2)], scale=-1.0)
                    tt("dve", An_ap, ps[pA][:, :], t32[2][:, :], ALU.mult, [("ps", pA), ("t32", 2)], [An_k])

                def part1(h=h, pA=pA, An_ap=An_ap, An_k=An_k):
                    mm(ps[pA][:, :], wv[:, h * 256 + 128:h * 256 + 256], An_ap, True, True, ["wv", An_k], [("ps", pA)])
                    cp("act", t32[3][:, :], ps[pA][:, :], [("ps", pA)], [("t32", 3)])
                    act(sq[0][:, 0:TS], t32[3][:, :], AF.Square, [("t32", 3)], [("sq", 0)])
                    ts("dve", o_b[:, h, :], t32[3][:, :], C(27 + h), None, ALU.mult, None, [("t32", 3), "consts"], ["R3_ob"])

                def part2(h=h):
                    mm(ps[SSB][:, :], ones[:, :], sq[0][:, 0:TS], h == 0, h == 3, [("sq", 0), "ones"], [("ps", SSB)])
                if h < 3:
                    pending[0], pending[1], pending[2] = part0, part1, part2
                else:
                    part0(); part1(); part2()
            if dbg and j == 0:
                for hh in range(8):
                    cp("dve", t32[0][0:64, :], out_a[:, hh, :], ["R3_oa"], [("t32", 0)])
                    dma("sp", dbg_d["oa"].ap()[:, hh * TS:(hh + 1) * TS], t32[0][0:64, :], [("t32", 0)], [], "dbg3")
                for hh in range(4):
                    cp("dve", t32[0][:, :], o_b[:, hh, :], ["R3_ob"], [("t32", 0)])
                    dma("sp", dbg_d["ob"].ap()[:, hh * TS:(hh + 1) * TS], t32[0][:, :], [("t32", 0)], [], "dbg4")

            rsqrt_from_ps(t32[1][:, :], ps[SSB][:, :], TS, 1.0 / 512, ("ps", SSB), ("t32", 1), lnt[:, 0:TS], "lnt")
            for m in range(KC):
                pa_, pb_ = (0, 1) if m % 2 == 0 else (2, 3)
                for hh in range(8):
                    mm(ps[pa_][:, :], woa[:, hh, m * 128:(m + 1) * 128], out_a[:, hh, :], hh == 0, hh == 7, ["woa", "R3_oa"], [("ps", pa_)])
                for hh in range(4):
                    mm(ps[pb_][:, :], wob[:, hh, m * 128:(m + 1) * 128], o_b[:, hh, :], hh == 0, hh == 3, ["wob", "R3_ob"], [("ps", pb_)])
                tt("dve", t32[2][:, :], ps[pa_][:, :], t32[0][:, :], ALU.mult, [("ps", pa_), ("t32", 0)], [("t32", 2)])
                tt("dve", t32[3][:, :], ps[pb_][:, :], t32[1][:, :], ALU.mult, [("ps", pb_), ("t32", 1)], [("t32", 3)])
                tt("pool", t32[2][:, :], t32[2][:, :], t32[3][:, :], ALU.add, [("t32", 2), ("t32", 3)], [("t32", 2)])
                tt("pool", xs[:, m, :], xs[:, m, :], t32[2][:, :], ALU.add, ["xs", ("t32", 2)], ["xs"])
            if dbg and j == 0:
                dma("sp", dbg_d["x1"].ap().rearrange("(a p) t -> p a t", p=128), xs[:, :, :], ["xs"], [], "dbg5")

            for kc in range(KC):
                s_ = sq[kc % 3]
                act(s_[:, 0:TS], xs[:, kc, :], AF.Square, ["xs"], [("sq", kc % 3)])
                mm(ps[4][:, :], ones[:, :], s_[:, 0:TS], kc == 0, kc == KC - 1, [("sq", kc % 3), "ones"], [("ps", 4)])
            rsqrt_from_ps(rstd[:, 0:TS], ps[4][:, :], TS, 1.0 / D, ("ps", 4), "rstd", lnt[:, 0:TS], "lnt")
            for kc in range(KC):
                eng = "dve"
                stt(eng, h1[:, kc, :], xs[:, kc, :], C(8 + kc), rstd[:, 0:TS], ALU.mult, ALU.mult,
                    ["xs", "consts", "rstd"], ["h1"])
            NV = TS - 2
            groups = [(0, 8), (8, 16), (16, NFC)]
            wd_loads = [(pss, g0, g1) for pss in range(2) for (g0, g1) in groups]

            def load_wd(n):
                pss, g0, g1 = wd_loads[n]
                src = wdn_s.ap()[pss, g0:g1].rearrange("i p c -> p i c")
                dma("sp", wdb[n % 2][:, 0:g1 - g0, :], src, [("wdn_s", pss, i) for i in range(g0, g1)], [("wdb", n % 2)], "d_wdb%d" % (n % 2))
            load_wd(0)
            load_wd(1)
            for i in range(NFC):
                wb = i % 4
                par = i % 2
                dma("sp", wu[wb][:, 0:KC, :].rearrange("p a c -> p (a c)"), wup_s.ap()[i], [("wup_s", i)], [("wug", wb)], "d_wug%d" % wb)
                dma("sp", wu[wb][:, KC:2 * KC, :].rearrange("p a c -> p (a c)"), wup_s.ap()[NFC + i], [("wup_s", NFC + i)], [("wuv", wb)], "d_wuv%d" % wb)
                pg, pv = [(0, 1), (2, 3), (4, 5)][i % 3]
                for kc in range(KC):
                    mm(ps[pg][:, :], wu[wb][:, kc, :], h1[:, kc, :], kc == 0, kc == KC - 1, [("wug", wb), "h1"], [("ps", pg)])
                for kc in range(KC):
                    mm(ps[pv][:, :], wu[wb][:, KC + kc, :], h1[:, kc, :], kc == 0, kc == KC - 1, [("wuv", wb), "h1"], [("ps", pv)])
                for (pi, ch, ta) in ((pg, i, 2 * par), (pv, NFC + i, 2 * par + 1)):
                    cw = lambda jj: C(40 + jj * 44 + ch)
                    act(t32[ta][:, 0:NV], ps[pi][:, 0:NV], AF.Identity, [("ps", pi), "consts"], [("t32", ta)], bias=C(172 + ch), scale=cw(0))
                    stt("dve", t32[ta][:, 0:NV], ps[pi][:, 1:NV + 1], cw(1), t32[ta][:, 0:NV], ALU.mult, ALU.add,
                        [("ps", pi), "consts", ("t32", ta)], [("t32", ta)])
                    stt("dve", t32[ta][:, 0:NV], ps[pi][:, 2:NV + 2], cw(2), t32[ta][:, 0:NV], ALU.mult, ALU.add,
                        [("ps", pi), "consts", ("t32", ta)], [("t32", ta)])
                act(sgb[par][:, 0:NV], t32[2 * par][:, 0:NV], AF.Silu, [("t32", 2 * par)], [("sgb", par)])
                tt("pool", aT[:, i, 0:NV], sgb[par][:, 0:NV], t32[2 * par + 1][:, 0:NV], ALU.mult, [("sgb", par), ("t32", 2 * par + 1)], [("aT", i)])
            for n, (pss, g0, g1) in enumerate(wd_loads):
                for i in range(g0, g1):
                    for m in range(4):
                        pi = 4 + m if m < 3 else 0
                        mm(ps[pi][:, 0:NV], wdb[n % 2][:, i - g0, m * 128:(m + 1) * 128], aT[:, i, 0:NV], i == 0, i == NFC - 1,
                           [("wdb", n % 2), ("aT", i)], [("ps", pi)])
                if n + 2 < len(wd_loads):
                    load_wd(n + 2)
                if g1 == NFC:
                    for m in range(4):
                        pi = 4 + m if m < 3 else 0
                        kc = pss * 4 + m
                        tt("dve", xs[:, kc, 2:TS], xs[:, kc, 2:TS], ps[pi][:, 0:NV], ALU.add, ["xs", ("ps", pi)], ["xs"])
            for kc in range(KC):
                s_ = sq[kc % 3]
                act(s_[:, 0:NV], xs[:, kc, 2:TS], AF.Square, ["xs"], [("sq", kc % 3)])
                mm(ps[1][:, 0:NV], ones[:, :], s_[:, 0:NV], kc == 0, kc == KC - 1, [("sq", kc % 3), "ones"], [("ps", 1)])
            rsqrt_from_ps(rstd[:, 0:NV], ps[1][:, 0:NV], NV, 1.0 / D, ("ps", 1), "rstd", lnt[:, 0:NV], "lnt")
            for kc in range(KC):
                eng = "dve"
                stt(eng, xs[:, kc, 2:TS], xs[:, kc, 2:TS], C(16 + kc), rstd[:, 0:NV], ALU.mult, ALU.mult,
                    ["xs", "consts", "rstd"], ["xs"])
            outs.append(dma("sp", out_v[j], xs[:, :, :], ["xs"], [], "d_out"))

        final = [i for i in S.order if i.dma is not None and i.dma.startswith("dbg")] + outs
        S.finalize(final)
        sems = {}
        for en in ("pe", "act", "dve", "pool"):
            sems[("eng", en)] = es.enter_context(nc.semaphore("s_" + en))
        for k in S.dma_keys():
            sems[("dma", k)] = es.enter_context(nc.semaphore("sd_" + k))
        block = es.enter_context(nc.Block())

        @block.sync
        def _(e):
            S.emit_stream("sp", e, sems, extra_waits=final)

        @block.tensor
        def _(e):
            S.emit_stream("pe", e, sems)

        @block.scalar
        def _(e):
            S.emit_stream("act", e, sems)

        @block.vector
        def _(e):
            S.emit_stream("dve", e, sems)

        @block.gpsimd
        def _(e):
            S.emit_stream("pool", e, sems)
    return nc


def _t5_bucket_static():
    d = np.arange(256)
    n = np.maximum(d, 0)
    large = 16 + (np.log(np.maximum(n, 1).astype(np.float32) / 16) / math.log(128 / 16) * 16).astype(np.int32)
    large = np.minimum(large, 31)
    return np.where(n < 16, n, large)


def make_in_maps(inp, nslot=NSLOT):
    x = np.asarray(inp["x"], np.float32)
    pos = np.asarray(inp["positions"], np.int32)
    w_in = np.asarray(inp["w_in"], np.float32)[0]
    w_qb = np.asarray(inp["w_q_b"], np.float32)[0]
    w_kv = np.asarray(inp["w_kv_b"], np.float32)[0]
    w_out = np.asarray(inp["w_out"], np.float32)[0]
    w_up = np.asarray(inp["w_up"], np.float32)[0]
    w_dn = np.asarray(inp["w_down"], np.float32)[0]
    relb = np.asarray(inp["rel_bias_table"], np.float32)
    sinks = np.asarray(inp["sinks"], np.float32)[0]
    g_attn = np.asarray(inp["attn_norm_g"], np.float32)[0]
    g_ffn = np.asarray(inp["ffn_norm_g"], np.float32)[0]
    g_fin = np.asarray(inp["final_norm_g"], np.float32)
    g_q = np.asarray(inp["q_norm_g"], np.float32)[0]
    g_kv = np.asarray(inp["kv_norm_g"], np.float32)[0]
    g_a = np.asarray(inp["a_out_norm_g"], np.float32)[0]
    g_b = np.asarray(inp["b_out_norm_g"], np.float32)[0]
    conv_w = np.asarray(inp["conv_w"], np.float32)[0]
    conv_b = np.asarray(inp["conv_b"], np.float32)[0]

    consts = np.zeros((128, NC_CONST), np.float32)
    consts[:, 0:8] = g_attn.reshape(8, 128).T
    consts[:, 8:16] = g_ffn.reshape(8, 128).T
    consts[:, 16:24] = g_fin.reshape(8, 128).T
    consts[:, 24:26] = g_q.reshape(2, 128).T
    consts[:, 26] = g_kv
    consts[:, 27:31] = g_b.reshape(4, 128).T
    consts[0:64, 31:39] = g_a.reshape(8, 64).T
    invf = (10000.0 ** (-np.arange(0, 64, 2, dtype=np.float32) / 64)).astype(np.float32)
    consts[:, 39] = np.tile(invf, 4)
    for jj in range(3):
        consts[:, 40 + jj * 44:40 + (jj + 1) * 44] = conv_w[jj].reshape(44, 128).T
    consts[:, 172:216] = conv_b.reshape(44, 128).T

    kp1, kp2 = w_in[:, 1152:1184], w_in[:, 1184:1216]
    w_in_ext = np.concatenate([w_in[:, 0:1152], kp1, kp2, kp1, kp2, kp2, kp1, kp2, kp1], axis=1)
    cols = []
    for h in range(4):
        b0 = h * 192
        nope = w_qb[:, b0:b0 + 128]
        p1, p2 = w_qb[:, b0 + 128:b0 + 160], w_qb[:, b0 + 160:b0 + 192]
        cols += [nope, p1, p2, p1, p2, p2, p1, p2, p1]
    w_qb_ext = np.concatenate(cols, axis=1)
    w_kvT = np.ascontiguousarray(w_kv.T)

    bucket = _t5_bucket_static()
    u = np.arange(128)[:, None]
    t = np.arange(128)[None, :]
    biasT = np.full((128, 2, 8, 128), NEGB, np.float32)
    d_prev = t - u + 128
    d_cur = t - u
    for h in range(8):
        vp = relb[bucket[np.clip(d_prev, 0, 255)], h]
        biasT[:, 0, h, :] = np.where(d_prev < 128, vp, NEGB)
        vc = relb[bucket[np.clip(d_cur, 0, 255)], h]
        biasT[:, 1, h, :] = np.where(d_cur >= 0, vc, NEGB)

    common = dict(consts=consts, w_in_ext=np.ascontiguousarray(w_in_ext), w_qb_ext=np.ascontiguousarray(w_qb_ext),
                  w_kvT=w_kvT, w_kv=np.ascontiguousarray(w_kv), w_out=np.ascontiguousarray(w_out),
                  w_up=np.ascontiguousarray(w_up), w_down=np.ascontiguousarray(w_dn),
                  biasT=np.ascontiguousarray(biasT.reshape(128, -1)),
                  sinks=sinks[None, :].copy(), relb=relb.reshape(1, 256).copy(), gkv_row=g_kv[None, :].copy(),
                  ident=np.eye(128, dtype=np.float32))
    maps = []
    for core in range(8):
        b, c = core // 4, core % 4
        xT = np.ascontiguousarray(x[b].T)
        xT_str = np.zeros((NSLOT, D, TX), np.float32)
        pos_str = np.zeros((1, NSLOT * TS), np.int32)
        tq = np.full((1, NSLOT * TS), -1e9, np.float32)
        kval = np.zeros((NSLOT, 128, 5), np.float32)
        for j in range(NSLOT):
            k = stripe_of(c, j)
            if k is None:
                continue
            q0 = OWN * k - 2
            toks = np.arange(q0 - HALO, q0 + TS)
            ok = (toks >= 0) & (toks < SEQ)
            xT_str[j][:, ok] = xT[:, toks[ok]]
            tk = toks[HALO:]
            pos_str[0, j * TS:(j + 1) * TS] = pos[b, np.clip(tk, 0, SEQ - 1)]
            tq[0, j * TS:(j + 1) * TS] = tk.astype(np.float32)
            kval[j] = ok.astype(np.float32).reshape(5, 128).T
        m = dict(common)
        m.update(xT_seq=xT, xT_str=xT_str, pos_seq=pos[b][None, :].copy(), pos_str=pos_str, tq_str=tq, kval_str=kval)
        maps.append(m)
    return maps


def assemble(results):
    out = np.zeros((NB, SEQ, D), np.float32)
    for core in range(8):
        b, c = core // 4, core % 4
        o = results[core]["out"]
        for j in range(NSLOT):
            k = stripe_of(c, j)
            if k is None:
                continue
            t0 = OWN * k
            n = min(OWN, SEQ - t0)
            if n <= 0:
                continue
            out[b, t0:t0 + n, :] = o[j][:, 2:2 + n].T
    return out


def kernel(**inputs):
    nc = build_nc()
    maps = make_in_maps(inputs)
    res = run_bass_kernel_spmd(nc, maps, core_ids=list(range(8)))
    return assemble(res.results)
```

```python
import math
from contextlib import ExitStack

import numpy as np
import concourse.bass as bass
import concourse.mybir as mybir
from concourse.bass_utils import run_bass_kernel_spmd

F32 = mybir.dt.float32
BF16 = mybir.dt.bfloat16
I32 = mybir.dt.int32
ALU = mybir.AluOpType
AF = mybir.ActivationFunctionType
AX = mybir.AxisListType

D = 1024
SEQ = 16384
NB = 2
KC = 8
TS = 512
OWN = 510
HALO = 128
TX = TS + HALO
NSLOT = 9
NKT = SEQ // TS
NKB = SEQ // 128
DFF = 2816
NFC = DFF // 128
EPS = 1e-6
SC_A = 64 ** -0.5
SC_B = 192 ** -0.5
NEGB = -30000.0
NC_CONST = 216
DEBUG = False


def stripe_of(c, j):
    if j == 0:
        return 0 if c == 0 else None
    return 4 * j - 3 + c


def slot_extent(j):
    if j == 0:
        return 4, 0
    kmax = 4 * j
    kmin = 4 * j - 3
    last_q = OWN * kmax - 2 + TS - 1
    nk = min(NKB, last_q // 128 + 1)
    q0min = OWN * kmin - 2
    nfull = max(0, (q0min + 1) // 128)
    return nk, min(nfull, nk)


class _Ins:
    __slots__ = ("eng", "fn", "deps", "sig", "dma", "need")

    def __init__(self, eng, fn, dma):
        self.eng = eng
        self.fn = fn
        self.dma = dma
        self.deps = []
        self.sig = None
        self.need = False


class _Set:
    __slots__ = ("eng", "dma")

    def __init__(self):
        self.eng = {}
        self.dma = []

    def add(self, ins):
        if ins.dma is None:
            self.eng[ins.eng] = ins
        else:
            self.dma.append(ins)

    def all(self):
        return list(self.eng.values()) + self.dma


PARENTS = {"hT": "R1_hT", "hk": "hk_all", "qa": "qa_all", "qpe": "qpe_all", "qlat": "qlat_all", "aT": "R1_aT"}


class Sched:
    ENGS = ("pe", "act", "dve", "pool", "sp")

    def __init__(self):
        self.streams = {e: [] for e in self.ENGS}
        self.order = []
        self.lw = {}
        self.rd = {}
        self.rc = {}
        self.pc = {}
        self.alias = {}
        self.batch = set()

    def add_alias(self, keys):
        for k in keys:
            self.alias.setdefault(k, set()).update(x for x in keys if x != k)

    @staticmethod
    def parent(k):
        if isinstance(k, tuple) and k[0] in PARENTS:
            return PARENTS[k[0]]
        return None

    def _g(self, d, k):
        v = d.get(k)
        if v is None:
            v = d[k] = _Set()
        return v

    def add(self, eng, fn, reads=(), writes=(), dma=None, batch=False):
        ins = _Ins(eng, fn, dma)
        if batch:
            self.batch.add(dma)
        deps = {}

        def dep(i):
            if i is not None:
                deps[id(i)] = i

        def depall(st):
            if st is not None:
                for i in st.all():
                    deps[id(i)] = i

        for k in reads:
            p = self.parent(k)
            dep(self.lw.get(k))
            depall(self.pc.get(k))
            if p is not None:
                dep(self.lw.get(p))
        for k in writes:
            p = self.parent(k)
            dep(self.lw.get(k)); depall(self.rd.get(k)); depall(self.rc.get(k)); depall(self.pc.get(k))
            if p is not None:
                dep(self.lw.get(p)); depall(self.rd.get(p))
            for a in self.alias.get(p if p is not None else k, ()):
                dep(self.lw.get(a)); depall(self.rd.get(a)); depall(self.rc.get(a))
        for d in deps.values():
            if d.eng == eng and d.dma is None and dma is None and eng == "pe":
                continue
            d.need = True
            ins.deps.append(d)
        for k in reads:
            p = self.parent(k)
            self._g(self.rd, k).add(ins)
            if p is not None:
                self._g(self.rc, p).add(ins)
        for k in writes:
            p = self.parent(k)
            self.lw[k] = ins
            self.rd[k] = _Set(); self.rc[k] = _Set(); self.pc[k] = _Set()
            if p is not None:
                self._g(self.pc, p).add(ins)
            for a in self.alias.get(p if p is not None else k, ()):
                self._g(self.pc, a).add(ins)
        self.streams[eng].append(ins)
        self.order.append(ins)
        return ins

    def dma_keys(self):
        ks = []
        seen = set()
        for ins in self.order:
            if ins.dma is not None and ins.dma not in seen:
                seen.add(ins.dma)
                ks.append(ins.dma)
        return ks

    def finalize(self, final_wait):
        for ins in final_wait:
            ins.need = True
        cnt = {}
        for ins in self.order:
            if ins.dma in self.batch:
                ins.need = True
            if not ins.need:
                continue
            key = ("dma", ins.dma) if ins.dma is not None else ("eng", ins.eng)
            cnt[key] = cnt.get(key, 0) + (16 if ins.dma is not None else 1)
            ins.sig = (key, cnt[key])
        for ins in self.order:
            if ins.dma in self.batch:
                ins.sig = (("dma", ins.dma), cnt[("dma", ins.dma)])

    def emit_stream(self, eng_name, eng, sems, extra_waits=()):
        waited = {}
        for ins in self.streams[eng_name]:
            for d in ins.deps:
                key, val = d.sig
                if waited.get(key, 0) < val:
                    eng.wait_ge(sems[key], val)
                    waited[key] = val
            bi = ins.fn(eng)
            if ins.sig is not None:
                key, val = ins.sig
                bi.then_inc(sems[key], 16 if key[0] == "dma" else 1)
        for d in extra_waits:
            key, val = d.sig
            if waited.get(key, 0) < val:
                eng.wait_ge(sems[key], val)
                waited[key] = val


def build_nc(nslot=NSLOT, nkt=NKT, dbg=False):
    nc = bass.Bass("TRN2", target_bir_lowering=False)
    S = Sched()

    def din(name, shape, dt=F32):
        return nc.dram_tensor(name, list(shape), dt, kind="ExternalInput")

    xT_seq = din("xT_seq", [D, SEQ])
    xT_str = din("xT_str", [NSLOT, D, TX])
    pos_seq = din("pos_seq", [1, SEQ], I32)
    pos_str = din("pos_str", [1, NSLOT * TS], I32)
    tq_str = din("tq_str", [1, NSLOT * TS])
    kval_str = din("kval_str", [NSLOT, 128, 5])
    consts_d = din("consts", [128, NC_CONST])
    w_in_d = din("w_in_ext", [D, 1408])
    w_qb_d = din("w_qb_ext", [256, 1536])
    w_kvT_d = din("w_kvT", [1024, 128])
    w_kv_d = din("w_kv", [128, 1024])
    w_out_d = din("w_out", [1024, D])
    w_up_d = din("w_up", [D, 2 * DFF])
    w_dn_d = din("w_down", [DFF, D])
    biasT_d = din("biasT", [128, 2 * 8 * 128])
    sinks_d = din("sinks", [1, 8])
    relb_d = din("relb", [1, 256])
    gkv_row_d = din("gkv_row", [1, 128])
    ident_d = din("ident", [128, 128])
    out_d = nc.dram_tensor("out", [NSLOT, D, TS], F32, kind="ExternalOutput")
    wup_s = nc.dram_tensor("wup_s", [2 * NFC, 128, KC * 128], BF16)
    wdn_s = nc.dram_tensor("wdn_s", [2, NFC, 128, 512], BF16)
    dbg_d = {}
    if dbg:
        dbg_d["kt"] = nc.dram_tensor("dbg_kt", [128, 512], F32, kind="ExternalOutput")
        dbg_d["kpe"] = nc.dram_tensor("dbg_kpe", [128, 256], F32, kind="ExternalOutput")
        dbg_d["v"] = nc.dram_tensor("dbg_v", [128, 512], F32, kind="ExternalOutput")
        dbg_d["x1"] = nc.dram_tensor("dbg_x1", [D, TS], F32, kind="ExternalOutput")
        dbg_d["oa"] = nc.dram_tensor("dbg_oa", [64, 8 * TS], F32, kind="ExternalOutput")
        dbg_d["ob"] = nc.dram_tensor("dbg_ob", [128, 4 * TS], F32, kind="ExternalOutput")

    es = ExitStack()
    with es:
        def sb(name, shape, dt):
            return es.enter_context(nc.sbuf_tensor(name, list(shape), dt))

        KT = sb("KT", [128, SEQ], BF16)
        KPE = sb("KPE", [128, SEQ // 2], BF16)
        V = sb("V", [128, NKB, 128], BF16)
        consts = sb("consts_s", [128, NC_CONST], F32)
        wq = sb("wq", [128, 2, 1536], BF16)
        wkT = sb("wkT", [128, 8, 128], BF16)
        wv = sb("wv", [128, 1024], BF16)
        ident = sb("ident_s", [128, 128], BF16)
        ones = sb("ones", [128, 128], BF16)
        ukeys = sb("ukeys", [128, NKB], F32)
        biasT = sb("biasT_s", [128, 2, 8, 128], F32)
        psink = sb("psink", [1, 1024], BF16)
        ones_row = sb("ones_row", [1, 64], BF16)
        small = sb("small", [128, 32], F32)
        xs = sb("xs", [128, KC, TS], F32)
        rstd = sb("rstd", [128, TX], F32)
        lnt = sb("lnt", [128, TX], F32)
        R1 = sb("R1", [128, 24 * 1024 // 2], BF16)
        hT = R1[:, 0:KC * TX].rearrange("p (a b) -> p a b", a=KC)
        wob = R1[:, 0:4 * 1024].rearrange("p (a b) -> p a b", a=4)
        woa = R1[0:64, 4 * 1024:12 * 1024].rearrange("p (a b) -> p a b", a=8)
        aT = R1[:, 0:NFC * TS].rearrange("p (a b) -> p a b", a=NFC)
        R3 = sb("R3", [128, 8 * 1024], BF16)
        wst = R3[:, :].rearrange("p (a b) -> p a b", a=KC)
        out_a = R3[0:64, 0:8 * TS].rearrange("p (a b) -> p a b", a=8)
        o_b = R3[:, 4096:4096 + 4 * TS].rearrange("p (a b) -> p a b", a=4)
        pT = [R3[:, 6144 + i * TS:6144 + (i + 1) * TS] for i in range(4)]
        R2 = sb("R2", [128, 12480], BF16)
        qlat = R2[:, 0:2048].rearrange("p (a b) -> p a b", a=4)
        qpe = R2[:, 2048:4096].rearrange("p (a b) -> p a b", a=4)
        qa = R2[0:64, 4096:8192].rearrange("p (a b) -> p a b", a=8)
        ka = R2[0:64, 8192:8192 + 2 * TX].rearrange("p (a b) -> p a b", a=2)
        va = R2[:, 9472:9472 + 5 * 128].rearrange("p (a b) -> p a b", a=5)
        onesv = R2[:, 10112:10112 + 5 * 64].rearrange("p (a b) -> p a b", a=5)
        cqn = R2[:, 10432:10432 + 2 * TS].rearrange("p (a b) -> p a b", a=2)
        qn = R2[:, 11456:11456 + TS]
        An = R2[:, 11968:11968 + TS]
        h1 = R2[:, 0:KC * TS].rearrange("p (a b) -> p a b", a=KC)
        wu = [R2[:, 4096 + i * 2048:4096 + (i + 1) * 2048].rearrange("p (a b) -> p a b", a=2 * KC) for i in range(4)]
        wdb = [R3[:, i * 4096:(i + 1) * 4096].rearrange("p (a b) -> p a b", a=8) for i in range(2)]
        sgb = [R1[:, 11264 + i * 512:11264 + (i + 1) * 512] for i in range(2)]
        mk = [R2[:, 4096 + i * 512:4096 + (i + 1) * 512] for i in range(3)]
        tq_b = sb("tq_b", [128, TS], F32)
        cosq = R1[:, 8192:9216].bitcast(F32)
        sinq = R1[:, 9216:10240].bitcast(F32)
        t32 = [sb("t32_%d" % i, [128, TS], F32) for i in range(4)]
        pos_i = lnt[:, 0:TS].bitcast(I32)
        sq = [sb("sq%d" % i, [128, TX], BF16) for i in range(3)]
        kval = sb("kval", [128, 5], F32)
        ps = [es.enter_context(nc.psum_tensor("ps%d" % i, [128, TS], F32)) for i in range(7)]
        pst = es.enter_context(nc.psum_tensor("pst", [128, 4, 128], BF16))
        xk = [xs[:, :, :], R1[:, 0:8192].bitcast(F32).rearrange("p (a b) -> p a b", a=KC)]
        xh = R1[:, 6144:8192].bitcast(F32).rearrange("p (a b) -> p a b", a=KC)
        hk = R2[:, 0:KC * TS].rearrange("p (a b) -> p a b", a=KC)
        wk = R3[:, 0:KC * 384].rearrange("p (a b) -> p a b", a=KC)

        S.add_alias(["R1_hT", "R1_aT"])
        for wkey in ("woa", "wob"):
            for X in ("R1_hT", "R1_aT", ("xk", 1)):
                S.add_alias([wkey, X])
        for X in ("xh", ("sgb", 0), ("sgb", 1)):
            S.add_alias(["woa", X])
        S.add_alias([("sgb", 0), "R1_hT"]); S.add_alias([("sgb", 1), "R1_hT"])
        S.add_alias([("xk", 1), "R1_hT"]); S.add_alias([("xk", 1), "R1_aT"])
        S.add_alias(["xh", "R1_aT"]); S.add_alias(["xh", ("xk", 1)])
        S.add_alias(["R3_wst", "R3_oa"]); S.add_alias(["R3_wst", "R3_ob"])
        for i in range(4):
            S.add_alias(["R3_wst", ("pT", i)])
            S.add_alias([("wdb", 1), ("pT", i)])
        S.add_alias(["R3_wst", "R3_wk"]); S.add_alias(["R3_wk", "R3_oa"])
        S.add_alias([("wdb", 0), "R3_wst"]); S.add_alias([("wdb", 0), "R3_oa"]); S.add_alias([("wdb", 0), "R3_wk"])
        S.add_alias([("wdb", 1), "R3_wst"]); S.add_alias([("wdb", 1), "R3_ob"])
        att_keys = ["qlat_all", "qpe_all", "qa_all", "ka", "va", "onesv", "cqn", "qn", "An", "hk_all",
                    ("mk", 0), ("mk", 1), ("mk", 2)]
        ffn_keys = ["h1"] + [("wug", i) for i in range(4)] + [("wuv", i) for i in range(4)]
        for a_ in att_keys:
            for f_ in ffn_keys:
                S.add_alias([a_, f_])
        for i in range(3):
            S.add_alias([("mk", i), "qa_all"])
        S.add_alias(["hk_all", "qlat_all"]); S.add_alias(["hk_all", "qpe_all"])
        S.add_alias([("xk", 0), "xs"])
        S.add_alias(["lnt", "pos_i"])
        for X_ in ("woa", "R1_aT", ("sgb", 0), ("sgb", 1)):
            S.add_alias(["cosq", X_]); S.add_alias(["sinq", X_])

        def C(col, n=1, rows=128):
            return consts[0:rows, col:col + n]

        def mm(out, lhsT, rhs, start, stop, r, w):
            return S.add("pe", lambda e: e.matmul(out, lhsT=lhsT, rhs=rhs, start=start, stop=stop), reads=r, writes=w)

        def act(out, in_, func, r, w, bias=None, scale=None):
            kw = {}
            if bias is not None:
                kw["bias"] = bias
            if scale is not None:
                kw["scale"] = scale
            return S.add("act", lambda e: e.activation(out=out, in_=in_, func=func, **kw), reads=r, writes=w)

        def tt(eng, out, in0, in1, op, r, w):
            return S.add(eng, lambda e: e.tensor_tensor(out=out, in0=in0, in1=in1, op=op), reads=r, writes=w)

        def ts(eng, out, in0, s1, s2, op0, op1, r, w):
            if op1 is None:
                return S.add(eng, lambda e: e.tensor_scalar(out=out, in0=in0, scalar1=s1, scalar2=None, op0=op0), reads=r, writes=w)
            return S.add(eng, lambda e: e.tensor_scalar(out=out, in0=in0, scalar1=s1, scalar2=s2, op0=op0, op1=op1), reads=r, writes=w)

        def stt(eng, out, in0, scalar, in1, op0, op1, r, w):
            return S.add(eng, lambda e: e.scalar_tensor_tensor(out=out, in0=in0, scalar=scalar, in1=in1, op0=op0, op1=op1), reads=r, writes=w)

        def cp(eng, out, in_, r, w):
            if eng == "act":
                return S.add(eng, lambda e: e.activation(out=out, in_=in_, func=AF.Identity), reads=r, writes=w)
            return S.add(eng, lambda e: e.tensor_copy(out=out, in_=in_), reads=r, writes=w)

        def rmax(out, in_, r, w):
            return S.add("dve", lambda e: e.reduce_max(out=out, in_=in_, axis=AX.X), reads=r, writes=w)

        def dma(eng, out, in_, r, w, key, batch=False):
            return S.add(eng, lambda e: e.dma_start(out=out, in_=in_), reads=r, writes=w, dma=key, batch=batch)

        def rsqrt_from_ps(dst, ps_ap, n, inv_n, r_key, w_key, tmp, tmp_key):
            act(tmp, ps_ap, AF.Ln, [r_key], [tmp_key], bias=EPS, scale=inv_n)
            act(dst, tmp, AF.Exp, [tmp_key], [w_key], scale=-0.5)

        def rope_tables(pos_ap_dram, n, cos_t, sin_t, ck, sk, tmpa, tmpa_k, tmpb, tmpb_k, dkey, pbuf=None, pkey=None):
            MAGIC = 12582912.0
            C1 = 6.28125
            C2 = 2 * math.pi - 6.28125
            if pbuf is None:
                pbuf, pkey = pos_i[:, 0:n], "pos_i"
                dma("sp", pbuf, pos_ap_dram, [], [pkey], dkey)
            cp("dve", tmpa, pbuf, [pkey], [tmpa_k])
            ts("dve", tmpa, tmpa, C(39), None, ALU.mult, None, [tmpa_k, "consts"], [tmpa_k])
            ts("dve", tmpb, tmpa, 1.0 / (2 * math.pi), MAGIC, ALU.mult, ALU.add, [tmpa_k], [tmpb_k])
            ts("dve", tmpb, tmpb, MAGIC, None, ALU.subtract, None, [tmpb_k], [tmpb_k])
            stt("dve", tmpa, tmpb, -C1, tmpa, ALU.mult, ALU.add, [tmpa_k, tmpb_k], [tmpa_k])
            stt("dve", tmpa, tmpb, -C2, tmpa, ALU.mult, ALU.add, [tmpa_k, tmpb_k], [tmpa_k])
            ts("dve", tmpa, tmpa, -math.pi, math.pi, ALU.max, ALU.min, [tmpa_k], [tmpa_k])
            stt("dve", tmpb, tmpa, -1.0, tmpa, ALU.mult, ALU.max, [tmpa_k], [tmpb_k])
            act(sin_t, tmpa, AF.Sin, [tmpa_k], [sk])
            act(cos_t, tmpb, AF.Sin, [tmpb_k], [ck], bias=math.pi / 2, scale=-1.0)

        dma("sp", consts[:, :], consts_d.ap(), [], ["consts"], "c_consts")
        dma("sp", biasT[:, :, :, :].rearrange("p a b c -> p (a b c)"), biasT_d.ap(), [], ["biasT"], "c_bias")
        dma("pool", ident[:, :], ident_d.ap(), [], ["ident"], "c_ident")
        dma("pool", wq[:, :, :], w_qb_d.ap().rearrange("(a p) n -> p a n", p=128), [], ["wq"], "c_wq")
        dma("pool", wkT[:, :, :], w_kvT_d.ap().rearrange("(a p) n -> p a n", p=128), [], ["wkT"], "c_wkT")
        dma("pool", wv[:, :], w_kv_d.ap(), [], ["wv"], "c_wv")
        dma("pool", wk, w_in_d.ap()[:, 1024:1408].rearrange("(a p) n -> p a n", p=128), [], ["R3_wk"], "c_wk")
        S.add("pool", lambda e: e.memset(ones[:, :], 1.0), writes=["ones"])
        S.add("pool", lambda e: e.memset(ones_row[:, :], 1.0), writes=["ones_row"])
        S.add("pool", lambda e: e.memset(small[:, :], 0.0), writes=["small"])
        dma("sp", small[0:1, 20:28], sinks_d.ap(), ["small"], ["sinks8"], "c_sink")
        S.add("pool", lambda e: e.iota(ukeys[:, :], pattern=[[128, NKB]], base=0, channel_multiplier=1,
                                        allow_small_or_imprecise_dtypes=True), writes=["ukeys"])
        for o in (256, 320):
            ts("dve", wk[:, :, o:o + 32], wk[:, :, o:o + 32], -1.0, None, ALU.mult, None, ["R3_wk"], ["R3_wk"])
        wq4 = wq[:, :, :].rearrange("p a (h c) -> p a h c", h=4)
        for o in (256, 320):
            for a in range(2):
                ts("dve", wq4[:, a, :, o:o + 32], wq4[:, a, :, o:o + 32], -1.0, None, ALU.mult, None, ["wq"], ["wq"])
        dma("sp", t32[0][:, 0:128], bass.AP(gkv_row_d, 0, [[0, 128], [1, 128]]), [], [("t32", 0)], "c_t0")
        tt("dve", t32[0][:, 0:128], t32[0][:, 0:128], t32[0][:, 0:128], ALU.mult, [("t32", 0)], [("t32", 0)])
        rmax(small[:, 1:2], t32[0][:, 0:128], [("t32", 0)], ["small1"])
        dma("sp", t32[1][:, 0:256], bass.AP(relb_d, 0, [[0, 128], [1, 256]]), [], [("t32", 1)], "c_t1")
        rmax(small[:, 3:4], t32[1][:, 0:256], [("t32", 1)], ["small3"])
        dma("sp", t32[2][:, 0:8], bass.AP(sinks_d, 0, [[0, 128], [1, 8]]), [], [("t32", 2)], "c_t2")
        rmax(small[:, 4:5], t32[2][:, 0:8], [("t32", 2)], ["small4"])
        wup_v = w_up_d.ap().rearrange("(a p) n -> p a n", p=128)
        prep = []
        for i in range(2 * NFC):
            prep.append((wup_s.ap()[i].rearrange("p (a c) -> p a c", a=KC), wup_v[:, :, i * 128:(i + 1) * 128],
                         [("wup_s", i)], "c_wups"))
        for pss in range(2):
            for i in range(NFC):
                prep.append((wdn_s.ap()[pss, i], w_dn_d.ap()[i * 128:(i + 1) * 128, pss * 512:(pss + 1) * 512],
                             [("wdn_s", pss, i)], "c_wdns"))

        def emit_prep(n):
            for _ in range(n):
                if prep:
                    d_, s__, w_, k_ = prep.pop(0)
                    dma("pool", d_, s__, [], w_, k_, batch=True)

        xT_v = xT_seq.ap().rearrange("(a p) t -> p a t", p=128)
        hk2 = [R2[:, b_ * 4096:(b_ + 1) * 4096].rearrange("p (a b) -> p a b", a=KC) for b_ in range(2)]
        rstdK = [(rstd[:, 0:TS], "rstd"), (tq_b[:, :], "tq_b")]
        csK = R1[:, 8192:12288].bitcast(F32)
        cosK = [csK[:, b_ * 1024:b_ * 1024 + 512] for b_ in range(2)]
        sinK = [csK[:, b_ * 1024 + 512:(b_ + 1) * 1024] for b_ in range(2)]
        tA, tB = xs[:, 2, :], xs[:, 3, :]
        sqK = [R3[:, 3072 + i * 512:3072 + (i + 1) * 512] for i in range(8)]
        for b_ in range(2):
            for X_ in ("woa", "R1_aT", ("sgb", 0), ("sgb", 1)):
                S.add_alias([("cosK", b_), X_]); S.add_alias([("sinK", b_), X_])
        S.add_alias(["tA", "xs"]); S.add_alias(["tB", "xs"])
        for i in range(8):
            for X_ in ["R3_wst", "R3_oa", "R3_ob", ("wdb", 0), ("wdb", 1)] + [("pT", q_) for q_ in range(4)]:
                S.add_alias([("sqK", i), X_])
        S.add_alias(["hk_all", "qa_all"])
        for i in range(3):
            S.add_alias(["hk_all", ("mk", i)])
        for kc in range(KC):
            ts("dve", wk[:, kc, :], wk[:, kc, :], C(kc), None, ALU.mult, None, ["R3_wk", "consts"], ["R3_wk"])

        NXB = 3
        xb = [R2[:, b_ * 4096:(b_ + 1) * 4096].rearrange("p (a b) -> p a b", a=KC) for b_ in range(NXB)]
        for b_ in range(NXB):
            S.add_alias(["xb%d" % b_, "hk_all"])
            for X_ in att_keys + ffn_keys:
                S.add_alias(["xb%d" % b_, X_])

        def load_x(it):
            if it < nkt:
                bx = it % NXB
                dma("pool", xb[bx], xT_v[:, :, it * TS:(it + 1) * TS], [], ["xb%d" % bx], "d_xb%d" % bx)
                emit_prep(3)
        load_x(0)
        load_x(1)
        load_x(2)

        posK = [xs[:, b_, :].bitcast(I32) for b_ in range(2)]
        S.add_alias([("posK", 0), "xs"]); S.add_alias([("posK", 1), "xs"])

        def load_pos(it):
            if it < nkt:
                dma("sp", posK[it % 2], bass.AP(pos_seq, it * TS, [[0, 128], [1, TS]]), [], [("posK", it % 2)], "d_posK%d" % (it % 2))
        load_pos(0)

        def kfront(it):
            b = it % 2
            xkk = "xb%d" % (it % NXB)
            X = xb[it % NXB]
            t0 = it * TS
            rs_ap, rs_k = rstdK[b]
            load_pos(it + 1)
            for kc in range(KC):
                act(sqK[kc], X[:, kc, :], AF.Square, [xkk], [("sqK", kc)])
                mm(ps[0][:, :], ones[:, :], sqK[kc], kc == 0, kc == KC - 1, [("sqK", kc), "ones"], [("ps", 0)])
            rope_tables(None, TS, cosK[b], sinK[b], ("cosK", b), ("sinK", b),
                        t32[2][:, :], ("t32", 2), t32[3][:, :], ("t32", 3), "d_pos", pbuf=posK[b], pkey=("posK", b))
            rsqrt_from_ps(rs_ap, ps[0][:, :], TS, 1.0 / D, ("ps", 0), rs_k, lnt[:, 0:TS], "lnt")
            tt("dve", cosK[b], cosK[b], rs_ap, ALU.mult, [("cosK", b), rs_k], [("cosK", b)])
            tt("dve", sinK[b], sinK[b], rs_ap, ALU.mult, [("sinK", b), rs_k], [("sinK", b)])

        def kback(it):
            b = it % 2
            xkk = "xb%d" % (it % NXB)
            t0 = it * TS
            rs_ap, rs_k = rstdK[b]
            for (o, pi) in ((0, 1), (128, 2), (256, 3)):
                for kc in range(KC):
                    mm(ps[pi][:, :], wk[:, kc, o:o + 128], xb[it % NXB][:, kc, :], kc == 0, kc == KC - 1,
                       ["R3_wk", xkk], [("ps", pi)])
            act(sq[0][:, 0:TS], ps[1][:, :], AF.Square, [("ps", 1)], [("sq", 0)])
            mm(ps[4][:, :], ones[:, :], sq[0][:, 0:TS], True, True, [("sq", 0), "ones"], [("ps", 4)])
            tt("dve", t32[0][:, :], ps[4][:, :], rs_ap, ALU.mult, [("ps", 4), rs_k], [("t32", 0)])
            tt("dve", t32[0][:, :], t32[0][:, :], rs_ap, ALU.mult, [("t32", 0), rs_k], [("t32", 0)])
            act(t32[1][:, :], t32[0][:, :], AF.Ln, [("t32", 0)], [("t32", 1)], bias=EPS, scale=1.0 / 128)
            act(t32[0][:, :], t32[1][:, :], AF.Exp, [("t32", 1)], [("t32", 0)], scale=-0.5)
            tt("dve", t32[0][:, :], t32[0][:, :], rs_ap, ALU.mult, [("t32", 0), rs_k], [("t32", 0)])
            stt("dve", KT[:, t0:t0 + TS], ps[1][:, :], C(26), t32[0][:, :], ALU.mult, ALU.mult,
                [("ps", 1), "consts", ("t32", 0)], [("KT", it)])
            act(sq[1][0:64, 0:TS], ps[2][0:64, :], AF.Square, [("ps", 2)], [("sq", 1)])
            mm(ps[5][:, :], ones[0:64, :], sq[1][0:64, 0:TS], True, True, [("sq", 1), "ones"], [("ps", 5)])
            tt("dve", t32[1][:, :], ps[5][:, :], rs_ap, ALU.mult, [("ps", 5), rs_k], [("t32", 1)])
            tt("dve", t32[1][:, :], t32[1][:, :], rs_ap, ALU.mult, [("t32", 1), rs_k], [("t32", 1)])
            rmax(small[:, 5:6], t32[1][:, :], [("t32", 1)], ["small5"])
            tt("dve", small[:, 0:1], small[:, 0:1], small[:, 5:6], ALU.max, ["small", "small5"], ["small0"])
            tt("dve", tA, ps[2][:, :], cosK[b], ALU.mult, [("ps", 2), ("cosK", b)], ["tA"])
            tt("dve", tB, ps[3][:, :], sinK[b], ALU.mult, [("ps", 3), ("sinK", b)], ["tB"])
            kdst = KPE[:, 2 * it * 128:(2 * it + 2) * 128].rearrange("p (a b) -> p a b", a=2)
            a3 = tA.rearrange("p (a b) -> p a b", a=4)
            b3 = tB.rearrange("p (a b) -> p a b", a=4)
            tt("dve", kdst[0:64, :, :], a3[0:64, 0:4:2, :], b3[0:64, 0:4:2, :], ALU.add, ["tA", "tB"], [("KPEa", it)])
            tt("dve", kdst[64:128, :, :], a3[64:128, 1:4:2, :], b3[64:128, 1:4:2, :], ALU.add, ["tA", "tB"], [("KPEb", it)])
            for q in range(4):
                S.add("pe", (lambda o_=pst[:, q, :], i_=KT[:, t0 + q * 128:t0 + (q + 1) * 128]: (lambda e: e.transpose(o_, i_, ident[:, :])))(),
                      reads=[("KT", it), "ident"], writes=["pst"])
            cp("act", V[:, 4 * it:4 * it + 4, :], pst[:, :, :], ["pst"], [("V", it)])

        kfront(0)
        for it in range(nkt):
            if it + 1 < nkt:
                kfront(it + 1)
            kback(it)
            load_x(it + 3)
        emit_prep(len(prep))
        ts("dve", small[:, 2:3], small[:, 1:2], 128.0, None, ALU.mult, None, ["small1"], ["small2"])
        tt("dve", small[:, 2:3], small[:, 2:3], small[:, 0:1], ALU.add, ["small2", "small0"], ["small2"])
        ts("dve", small[:, 2:3], small[:, 2:3], 1.05, None, ALU.mult, None, ["small2"], ["small2"])
        if dbg:
            cp("dve", t32[0][:, :], KT[:, 0:TS], [("KT", 0)], [("t32", 0)])
            dma("sp", dbg_d["kt"].ap(), t32[0][:, :], [("t32", 0)], [], "dbg0")
            cp("dve", t32[1][:, 0:256], KPE[:, 0:256], [("KPEa", 0), ("KPEb", 0)], [("t32", 1)])
            dma("sp", dbg_d["kpe"].ap(), t32[1][:, 0:256], [("t32", 1)], [], "dbg1")
            cp("dve", t32[2][:, :], V[:, 0:4, :].rearrange("p a b -> p (a b)"), [("V", 0)], [("t32", 2)])
            dma("sp", dbg_d["v"].ap(), t32[2][:, :], [("t32", 2)], [], "dbg2")

        outs = []
        xstr_v = xT_str.ap().rearrange("j (a p) t -> j p a t", p=128)
        win_v = w_in_d.ap()[:, 0:1024].rearrange("(a p) n -> p a n", p=128)
        woa_v = w_out_d.ap()[0:512, :].rearrange("(h d) n -> d h n", d=64)
        wob_v = w_out_d.ap()[512:1024, :].rearrange("(h p) n -> p h n", p=128)
        out_v = out_d.ap().rearrange("j (a p) t -> j p a t", p=128)
        for j in range(nslot):
            nk, nfull = slot_extent(j)
            dma("sp", xs[:, :, :], xstr_v[j][:, :, HALO:TX], [], ["xs"], "d_xs")
            dma("sp", xh, xstr_v[j][:, :, 0:HALO], [], ["xh"], "d_xh")
            dma("pool", wst, win_v, [], ["R3_wst"], "d_wst")
            dma("sp", tq_b[:, :], bass.AP(tq_str, j * TS, [[0, 128], [1, TS]]), [], ["tq_b"], "d_tq")
            dma("sp", kval[:, :], kval_str.ap()[j], [], ["kval"], "d_kval")
            rope_tables(bass.AP(pos_str, j * TS, [[0, 128], [1, TS]]), TS, cosq[:, :], sinq[:, :], "cosq", "sinq",
                        t32[2][:, :], ("t32", 2), t32[3][:, :], ("t32", 3), "d_pos")
            for kc in range(KC):
                s_ = sq[kc % 3]
                act(s_[:, HALO:TX], xs[:, kc, :], AF.Square, ["xs"], [("sq", kc % 3)])
                act(s_[:, 0:HALO], xh[:, kc, :], AF.Square, ["xh", ("sq", kc % 3)], [("sq", kc % 3)])
                mm(ps[0][:, :], ones[:, :], s_[:, 0:TS], kc == 0, kc == KC - 1, [("sq", kc % 3), "ones"], [("ps", 0)])
                mm(ps[1][:, 0:HALO], ones[:, :], s_[:, TS:TX], kc == 0, kc == KC - 1, [("sq", kc % 3), "ones"], [("ps", 1)])
            rsqrt_from_ps(rstd[:, 0:TS], ps[0][:, :], TS, 1.0 / D, ("ps", 0), "rstd", lnt[:, 0:TS], "lnt")
            rsqrt_from_ps(rstd[:, TS:TX], ps[1][:, 0:HALO], HALO, 1.0 / D, ("ps", 1), "rstd", lnt[:, TS:TX], "lnt")
            for kc in range(KC):
                eng = "dve"
                stt(eng, hT[:, kc, HALO:TX], xs[:, kc, :], C(kc), rstd[:, HALO:TX], ALU.mult, ALU.mult,
                    ["xs", "consts", "rstd"], [("hT", kc)])
                stt(eng, hT[:, kc, 0:HALO], xh[:, kc, :], C(kc), rstd[:, 0:HALO], ALU.mult, ALU.mult,
                    ["xh", "consts", "rstd", ("hT", kc)], [("hT", kc)])
            for blk in range(5):
                cp("dve", onesv[:, blk, :], kval[:, blk:blk + 1].to_broadcast([128, 64]), ["kval"], ["onesv"])
            hS = lambda kc: hT[:, kc, HALO:TX]
            rot = [2, 3, 4, 5, 6]
            rr = [0]

            def nxt():
                rr[0] = (rr[0] + 1) % len(rot)
                return rot[rr[0]]

            S.add("pool", lambda e: e.memset(small[:, 6:10], 0.0), writes=["small6", "small7", "small8", "small9"])
            for h in range(8):
                pi = nxt()
                for kc in range(KC):
                    mm(ps[pi][0:64, :], wst[:, kc, h * 64:(h + 1) * 64], hS(kc), kc == 0, kc == KC - 1,
                       ["R3_wst", ("hT", kc)], [("ps", pi)])
                act(qa[:, h, :], ps[pi][0:64, :], AF.Identity, [("ps", pi)], [("qa", h)], scale=SC_A)
                act(sq[h % 3][0:64, 0:TS], qa[:, h, :], AF.Square, [("qa", h)], [("sq", h % 3)])
                mm(ps[1][:, :], ones[0:64, :], sq[h % 3][0:64, 0:TS], True, True, [("sq", h % 3), "ones"], [("ps", 1)])
                rmax(small[:, 5:6], ps[1][:, :], [("ps", 1)], ["small5"])
                tt("dve", small[:, 6:7], small[:, 6:7], small[:, 5:6], ALU.max, ["small6", "small5"], ["small6"])
            for g in range(2):
                pi = nxt()
                pj = nxt()
                for kc in range(KC):
                    mm(ps[pi][0:64, :], wst[:, kc, 512 + g * 64:512 + (g + 1) * 64], hT[:, kc, 0:TS], kc == 0, kc == KC - 1,
                       ["R3_wst", ("hT", kc)], [("ps", pi)])
                for kc in range(KC):
                    mm(ps[pj][0:64, 0:HALO], wst[:, kc, 512 + g * 64:512 + (g + 1) * 64], hT[:, kc, TS:TX], kc == 0, kc == KC - 1,
                       ["R3_wst", ("hT", kc)], [("ps", pj)])
                act(ka[:, g, 0:TS], ps[pi][0:64, :], AF.Identity, [("ps", pi)], ["ka"])
                act(ka[:, g, TS:TX], ps[pj][0:64, 0:HALO], AF.Identity, [("ps", pj)], ["ka"])
                act(sq[g][0:64, :], ka[:, g, :], AF.Square, ["ka"], [("sq", g)])
                mm(ps[0][:, :], ones[0:64, :], sq[g][0:64, 0:TS], True, True, [("sq", g), "ones"], [("ps", 0)])
                mm(ps[1][:, 0:HALO], ones[0:64, :], sq[g][0:64, TS:TX], True, True, [("sq", g), "ones"], [("ps", 1)])
                rmax(small[:, 5:6], ps[0][:, :], [("ps", 0)], ["small5"])
                tt("dve", small[:, 7:8], small[:, 7:8], small[:, 5:6], ALU.max, ["small7", "small5"], ["small7"])
                rmax(small[:, 5:6], ps[1][:, 0:HALO], [("ps", 1)], ["small5"])
                tt("dve", small[:, 7:8], small[:, 7:8], small[:, 5:6], ALU.max, ["small7", "small5"], ["small7"])
            for blk in range(5):
                pi = nxt()
                for kc in range(KC):
                    mm(ps[pi][:, 0:128], hT[:, kc, blk * 128:(blk + 1) * 128], wst[:, kc, 640:768], kc == 0, kc == KC - 1,
                       ["R3_wst", ("hT", kc)], [("ps", pi)])
                cp("dve", va[:, blk, :], ps[pi][:, 0:128], [("ps", pi)], ["va"])
            pq = [nxt(), nxt()]
            for c2 in range(2):
                for kc in range(KC):
                    mm(ps[pq[c2]][:, :], wst[:, kc, 768 + c2 * 128:768 + (c2 + 1) * 128], hS(kc), kc == 0, kc == KC - 1,
                       ["R3_wst", ("hT", kc)], [("ps", pq[c2])])
                act(sq[c2][:, 0:TS], ps[pq[c2]][:, :], AF.Square, [("ps", pq[c2])], [("sq", c2)])
                mm(ps[0][:, :], ones[:, :], sq[c2][:, 0:TS], c2 == 0, c2 == 1, [("sq", c2), "ones"], [("ps", 0)])
            rsqrt_from_ps(t32[0][:, :], ps[0][:, :], TS, 1.0 / 256, ("ps", 0), ("t32", 0), t32[1][:, :], ("t32", 1))
            for c2 in range(2):
                stt("dve", cqn[:, c2, :], ps[pq[c2]][:, :], C(24 + c2), t32[0][:, :], ALU.mult, ALU.mult,
                    [("ps", pq[c2]), "consts", ("t32", 0)], ["cqn"])
            S.add("pool", lambda e: e.memset(small[:, 10:14], 0.0), writes=[("smq", 0), ("smq", 1), ("smq", 2), ("smq", 3)])
            for h in range(4):
                pn, pp, pr = nxt(), nxt(), nxt()
                for (o, pi) in ((0, pn), (128, pp), (256, pr)):
                    for c2 in range(2):
                        mm(ps[pi][:, :], wq[:, c2, h * 384 + o:h * 384 + o + 128], cqn[:, c2, :], c2 == 0, c2 == 1,
                           ["wq", "cqn"], [("ps", pi)])
                cp("act", qn, ps[pn][:, :], [("ps", pn)], ["qn"])
                tt("dve", t32[0][:, :], ps[pp][:, :], cosq[:, :], ALU.mult, [("ps", pp), "cosq"], [("t32", 0)])
                tt("dve", t32[1][:, :], ps[pr][:, :], sinq[:, :], ALU.mult, [("ps", pr), "sinq"], [("t32", 1)])
                tt("dve", t32[0][:, :], t32[0][:, :], t32[1][:, :], ALU.add, [("t32", 0), ("t32", 1)], [("t32", 0)])
                act(qpe[:, h, :], t32[0][:, :], AF.Identity, [("t32", 0)], [("qpe", h)], scale=SC_B)
                pa = nxt()
                mm(ps[pa][:, :], wkT[:, 2 * h, :], qn, True, True, ["wkT", "qn"], [("ps", pa)])
                act(qlat[:, h, :], ps[pa][:, :], AF.Identity, [("ps", pa)], [("qlat", h)], scale=SC_B)
                act(sq[0][:, 0:TS], qlat[:, h, :], AF.Square, [("qlat", h)], [("sq", 0)])
                act(sq[1][0:64, 0:TS], qpe[0:64, h, :], AF.Square, [("qpe", h)], [("sq", 1)])
                mm(ps[0][:, :], ones[:, :], sq[0][:, 0:TS], True, False, [("sq", 0), "ones"], [("ps", 0)])
                mm(ps[0][:, :], ones[0:64, :], sq[1][0:64, 0:TS], False, True, [("sq", 1), "ones"], [("ps", 0)])
                rmax(small[:, 10 + h:11 + h], ps[0][:, :], [("ps", 0)], [("smq", h)])
            for h in range(4):
                ts("dve", small[:, 14 + h:15 + h], small[:, 10 + h:11 + h], small[:, 2:3], 1.05, ALU.mult, ALU.mult,
                   [("smq", h), "small2"], [("negm", h)])
                ts("dve", small[:, 14 + h:15 + h], small[:, 14 + h:15 + h], 1e-20, None, ALU.max, None, [("negm", h)], [("negm", h)])
                act(small[:, 14 + h:15 + h], small[:, 14 + h:15 + h], AF.Ln, [("negm", h)], [("negm", h)])
                act(small[:, 14 + h:15 + h], small[:, 14 + h:15 + h], AF.Exp, [("negm", h)], [("negm", h)], scale=0.5)
                ts("dve", small[:, 14 + h:15 + h], small[:, 14 + h:15 + h], -1.0, None, ALU.mult, None, [("negm", h)], [("negm", h)])
            tt("dve", small[:, 18:19], small[:, 6:7], small[:, 7:8], ALU.mult, ["small6", "small7"], ["small18"])
            ts("dve", small[:, 18:19], small[:, 18:19], 1.05, 1e-20, ALU.mult, ALU.max, ["small18"], ["small18"])
            act(small[:, 18:19], small[:, 18:19], AF.Ln, ["small18"], ["small18"])
            act(small[:, 18:19], small[:, 18:19], AF.Exp, ["small18"], ["small18"], scale=0.5)
            tt("dve", small[:, 18:19], small[:, 18:19], small[:, 3:4], ALU.add, ["small18", "small3"], ["small18"])
            tt("dve", small[:, 18:19], small[:, 18:19], small[:, 4:5], ALU.max, ["small18", "small4"], ["small18"])
            ts("dve", small[:, 18:19], small[:, 18:19], -1.0, None, ALU.mult, None, ["small18"], ["small18"])
            for hh in range(8):
                act(psink[0:1, hh * 128:(hh + 1) * 128], small[0:1, 20 + hh:21 + hh].to_broadcast([1, 128]), AF.Exp,
                    ["sinks8", "small18", "psink"], ["psink"], bias=small[0:1, 18:19])

            SSA, SSB = 5, 6
            for qb in range(4):
                for g in range(2):
                    for w_, kblk in ((0, qb), (1, qb + 1)):
                        pS = w_
                        mm(ps[pS][:, :].rearrange("p (a b) -> p a b", a=4), ka[:, g, kblk * 128:(kblk + 1) * 128],
                           qa[:, 4 * g:4 * g + 4, qb * 128:(qb + 1) * 128], True, True, ["ka", "qa_all"], [("ps", pS)])
                        tt("dve", t32[w_][:, :].rearrange("p (a b) -> p a b", a=4), ps[pS][:, :].rearrange("p (a b) -> p a b", a=4),
                           biasT[:, w_, 4 * g:4 * g + 4, :], ALU.add, [("ps", pS), "biasT"], [("t32", w_)])
                        act(pT[w_], t32[w_][:, :], AF.Exp, [("t32", w_), "small18"], [("pT", w_)], bias=small[:, 18:19])
                    for w_, kblk in ((0, qb), (1, qb + 1)):
                        mm(ps[2][0:64, :], va[:, kblk, g * 64:(g + 1) * 64], pT[w_], w_ == 0, w_ == 1, ["va", ("pT", w_)], [("ps", 2)])
                    for w_, kblk in ((0, qb), (1, qb + 1)):
                        mm(ps[3][0:64, :], onesv[:, kblk, :], pT[w_], w_ == 0, False, ["onesv", ("pT", w_)], [("ps", 3)])
                    mm(ps[3][0:64, :], ones_row[0:1, :], psink[0:1, g * 512:(g + 1) * 512], False, True, ["ones_row", "psink"], [("ps", 3)])
                    act(t32[2][0:64, :], ps[3][0:64, :], AF.Ln, [("ps", 3)], [("t32", 2)])
                    act(t32[2][0:64, :], t32[2][0:64, :], AF.Exp, [("t32", 2)], [("t32", 2)], scale=-1.0)
                    tt("dve", t32[3][0:64, :], ps[2][0:64, :], t32[2][0:64, :], ALU.mult, [("ps", 2), ("t32", 2)], [("t32", 3)])
                    act(sq[2][0:64, 0:TS], t32[3][0:64, :], AF.Square, [("t32", 3)], [("sq", 2)])
                    for hh in range(4):
                        first = (g == 0 and hh == 0)
                        last = (g == 1 and hh == 3)
                        mm(ps[SSA][:, qb * 128:(qb + 1) * 128], ones[0:64, :], sq[2][0:64, hh * 128:(hh + 1) * 128], first, last,
                           [("sq", 2), "ones"], [("ps", SSA)])
                    for hh in range(4):
                        ts("dve", out_a[:, 4 * g + hh, qb * 128:(qb + 1) * 128], t32[3][0:64, hh * 128:(hh + 1) * 128],
                           C(31 + 4 * g + hh, 1, 64), None, ALU.mult, None, [("t32", 3), "consts", "R3_oa"], ["R3_oa"])

            rsqrt_from_ps(t32[0][:, :], ps[SSA][:, :], TS, 1.0 / 512, ("ps", SSA), ("t32", 0), lnt[:, 0:TS], "lnt")
            dma("pool", woa, woa_v, [], ["woa"], "d_woa")
            dma("pool", wob, wob_v, [], ["wob"], "d_wob")
            SBK = [0, 1, 4]
            ABK = [2, 5]
            Anb = [(An, "An"), (qn, "qn")]
            pending = [None, None, None]
            for h in range(4):
                pA = ABK[h % 2]

                def qk(kb, h=h):
                    pS = SBK[kb % 3]
                    half = (kb % 2) * 64
                    col = (kb // 2) * 128
                    mm(ps[pS][:, :], KT[:, kb * 128:(kb + 1) * 128], qlat[:, h, :], True, False, [("KT", kb // 4), ("qlat", h)], [("ps", pS)])
                    mm(ps[pS][:, :], KPE[half:half + 64, col:col + 128], qpe[half:half + 64, h, :], False, True,
                       [("KPEa", kb // 4), ("KPEb", kb // 4), ("qpe", h)], [("ps", pS)])
                qk(0)
                if nk > 1:
                    qk(1)
                for kb in range(nk):
                    pS = SBK[kb % 3]
                    bi = kb % 4
                    if kb + 2 < nk:
                        qk(kb + 2)
                    act(pT[bi], ps[pS][:, :], AF.Exp, [("ps", pS), ("negm", h)], [("pT", bi)], bias=small[:, 14 + h:15 + h])
                    if kb >= nfull:
                        stt("dve", pT[bi], tq_b[:, :], ukeys[:, kb:kb + 1], pT[bi], ALU.is_ge, ALU.mult,
                            ["tq_b", "ukeys", ("pT", bi)], [("pT", bi)])
                    mm(ps[pA][:, :], V[:, kb, :], pT[bi], kb == 0, kb == nk - 1, [("V", kb // 4), ("pT", bi)], [("ps", pA)])
                    mm(ps[3][:, :], ones[:, :], pT[bi], kb == 0, kb == nk - 1, [("pT", bi), "ones"], [("ps", 3)])
                    for (kx, idx) in ((1, 0), (4, 1), (7, 2)):
                        if kb == min(kx, nk - 1) and pending[idx] is not None:
                            pending[idx]()
                            pending[idx] = None
                ts("dve", t32[2][:, :], ps[3][:, :], 1e-30, None, ALU.max, None, [("ps", 3)], [("t32", 2)])
                An_ap, An_k = Anb[h % 2]

                def part0(h=h, pA=pA, An_ap=An_ap, An_k=An_k):
                    act(t32[2][:, :], t32[2][:, :], AF.Ln, [("t32", 2)], [("t32", 2)])
                    act(t32[2][:, :], t32[2][:, :], AF.Exp, [("t32", 2)], [("t32", 2)], scale=-1.0)
                    tt("dve", An_ap, ps[pA][:, :], t32[2][:, :], ALU.mult, [("ps", pA), ("t32", 2)], [An_k])

                def part1(h=h, pA=pA, An_ap=An_ap, An_k=An_k):
                    mm(ps[pA][:, :], wv[:, h * 256 + 128:h * 256 + 256], An_ap, True, True, ["wv", An_k], [("ps", pA)])
                    cp("act", t32[3][:, :], ps[pA][:, :], [("ps", pA)], [("t32", 3)])
                    act(sq[0][:, 0:TS], t32[3][:, :], AF.Square, [("t32", 3)], [("sq", 0)])
                    ts("dve", o_b[:, h, :], t32[3][:, :], C(27 + h), None, ALU.mult, None, [("t32", 3), "consts"], ["R3_ob"])

                def part2(h=h):
                    mm(ps[SSB][:, :], ones[:, :], sq[0][:, 0:TS], h == 0, h == 3, [("sq", 0), "ones"], [("ps", SSB)])
                if h < 3:
                    pending[0], pending[1], pending[2] = part0, part1, part2
                else:
                    part0(); part1(); part2()
            if dbg and j == 0:
                for hh in range(8):
                    cp("dve", t32[0][0:64, :], out_a[:, hh, :], ["R3_oa"], [("t32", 0)])
                    dma("sp", dbg_d["oa"].ap()[:, hh * TS:(hh + 1) * TS], t32[0][0:64, :], [("t32", 0)], [], "dbg3")
                for hh in range(4):
                    cp("dve", t32[0][:, :], o_b[:, hh, :], ["R3_ob"], [("t32", 0)])
                    dma("sp", dbg_d["ob"].ap()[:, hh * TS:(hh + 1) * TS], t32[0][:, :], [("t32", 0)], [], "dbg4")

            rsqrt_from_ps(t32[1][:, :], ps[SSB][:, :], TS, 1.0 / 512, ("ps", SSB), ("t32", 1), lnt[:, 0:TS], "lnt")
            for m in range(KC):
                pa_, pb_ = (0, 1) if m % 2 == 0 else (2, 3)
                for hh in range(8):
                    mm(ps[pa_][:, :], woa[:, hh, m * 128:(m + 1) * 128], out_a[:, hh, :], hh == 0, hh == 7, ["woa", "R3_oa"], [("ps", pa_)])
                for hh in range(4):
                    mm(ps[pb_][:, :], wob[:, hh, m * 128:(m + 1) * 128], o_b[:, hh, :], hh == 0, hh == 3, ["wob", "R3_ob"], [("ps", pb_)])
                tt("dve", t32[2][:, :], ps[pa_][:, :], t32[0][:, :], ALU.mult, [("ps", pa_), ("t32", 0)], [("t32", 2)])
                tt("dve", t32[3][:, :], ps[pb_][:, :], t32[1][:, :], ALU.mult, [("ps", pb_), ("t32", 1)], [("t32", 3)])
                tt("pool", t32[2][:, :], t32[2][:, :], t32[3][:, :], ALU.add, [("t32", 2), ("t32", 3)], [("t32", 2)])
                tt("pool", xs[:, m, :], xs[:, m, :], t32[2][:, :], ALU.add, ["xs", ("t32", 2)], ["xs"])
            if dbg and j == 0:
                dma("sp", dbg_d["x1"].ap().rearrange("(a p) t -> p a t", p=128), xs[:, :, :], ["xs"], [], "dbg5")

            for kc in range(KC):
                s_ = sq[kc % 3]
                act(s_[:, 0:TS], xs[:, kc, :], AF.Square, ["xs"], [("sq", kc % 3)])
                mm(ps[4][:, :], ones[:, :], s_[:, 0:TS], kc == 0, kc == KC - 1, [("sq", kc % 3), "ones"], [("ps", 4)])
            rsqrt_from_ps(rstd[:, 0:TS], ps[4][:, :], TS, 1.0 / D, ("ps", 4), "rstd", lnt[:, 0:TS], "lnt")
            for kc in range(KC):
                eng = "dve"
                stt(eng, h1[:, kc, :], xs[:, kc, :], C(8 + kc), rstd[:, 0:TS], ALU.mult, ALU.mult,
                    ["xs", "consts", "rstd"], ["h1"])
            NV = TS - 2
            groups = [(0, 8), (8, 16), (16, NFC)]
            wd_loads = [(pss, g0, g1) for pss in range(2) for (g0, g1) in groups]

            def load_wd(n):
                pss, g0, g1 = wd_loads[n]
                src = wdn_s.ap()[pss, g0:g1].rearrange("i p c -> p i c")
                dma("sp", wdb[n % 2][:, 0:g1 - g0, :], src, [("wdn_s", pss, i) for i in range(g0, g1)], [("wdb", n % 2)], "d_wdb%d" % (n % 2))
            load_wd(0)
            load_wd(1)
            for i in range(NFC):
                wb = i % 4
                par = i % 2
                dma("sp", wu[wb][:, 0:KC, :].rearrange("p a c -> p (a c)"), wup_s.ap()[i], [("wup_s", i)], [("wug", wb)], "d_wug%d" % wb)
                dma("sp", wu[wb][:, KC:2 * KC, :].rearrange("p a c -> p (a c)"), wup_s.ap()[NFC + i], [("wup_s", NFC + i)], [("wuv", wb)], "d_wuv%d" % wb)
                pg, pv = [(0, 1), (2, 3), (4, 5)][i % 3]
                for kc in range(KC):
                    mm(ps[pg][:, :], wu[wb][:, kc, :], h1[:, kc, :], kc == 0, kc == KC - 1, [("wug", wb), "h1"], [("ps", pg)])
                for kc in range(KC):
                    mm(ps[pv][:, :], wu[wb][:, KC + kc, :], h1[:, kc, :], kc == 0, kc == KC - 1, [("wuv", wb), "h1"], [("ps", pv)])
                for (pi, ch, ta) in ((pg, i, 2 * par), (pv, NFC + i, 2 * par + 1)):
                    cw = lambda jj: C(40 + jj * 44 + ch)
                    act(t32[ta][:, 0:NV], ps[pi][:, 0:NV], AF.Identity, [("ps", pi), "consts"], [("t32", ta)], bias=C(172 + ch), scale=cw(0))
                    stt("dve", t32[ta][:, 0:NV], ps[pi][:, 1:NV + 1], cw(1), t32[ta][:, 0:NV], ALU.mult, ALU.add,
                        [("ps", pi), "consts", ("t32", ta)], [("t32", ta)])
                    stt("dve", t32[ta][:, 0:NV], ps[pi][:, 2:NV + 2], cw(2), t32[ta][:, 0:NV], ALU.mult, ALU.add,
                        [("ps", pi), "consts", ("t32", ta)], [("t32", ta)])
                act(sgb[par][:, 0:NV], t32[2 * par][:, 0:NV], AF.Silu, [("t32", 2 * par)], [("sgb", par)])
                tt("pool", aT[:, i, 0:NV], sgb[par][:, 0:NV], t32[2 * par + 1][:, 0:NV], ALU.mult, [("sgb", par), ("t32", 2 * par + 1)], [("aT", i)])
            for n, (pss, g0, g1) in enumerate(wd_loads):
                for i in range(g0, g1):
                    for m in range(4):
                        pi = 4 + m if m < 3 else 0
                        mm(ps[pi][:, 0:NV], wdb[n % 2][:, i - g0, m * 128:(m + 1) * 128], aT[:, i, 0:NV], i == 0, i == NFC - 1,
                           [("wdb", n % 2), ("aT", i)], [("ps", pi)])
                if n + 2 < len(wd_loads):
                    load_wd(n + 2)
                if g1 == NFC:
                    for m in range(4):
                        pi = 4 + m if m < 3 else 0
                        kc = pss * 4 + m
                        tt("dve", xs[:, kc, 2:TS], xs[:, kc, 2:TS], ps[pi][:, 0:NV], ALU.add, ["xs", ("ps", pi)], ["xs"])
            for kc in range(KC):
                s_ = sq[kc % 3]
                act(s_[:, 0:NV], xs[:, kc, 2:TS], AF.Square, ["xs"], [("sq", kc % 3)])
                mm(ps[1][:, 0:NV], ones[:, :], s_[:, 0:NV], kc == 0, kc == KC - 1, [("sq", kc % 3), "ones"], [("ps", 1)])
            rsqrt_from_ps(rstd[:, 0:NV], ps[1][:, 0:NV], NV, 1.0 / D, ("ps", 1), "rstd", lnt[:, 0:NV], "lnt")
            for kc in range(KC):
                eng = "dve"
                stt(eng, xs[:, kc, 2:TS], xs[:, kc, 2:TS], C(16 + kc), rstd[:, 0:NV], ALU.mult, ALU.mult,
                    ["xs", "consts", "rstd"], ["xs"])
            outs.append(dma("sp", out_v[j], xs[:, :, :], ["xs"], [], "d_out"))

        final = [i for i in S.order if i.dma is not None and i.dma.startswith("dbg")] + outs
        S.finalize(final)
        sems = {}
        for en in ("pe", "act", "dve", "pool"):
            sems[("eng", en)] = es.enter_context(nc.semaphore("s_" + en))
        for k in S.dma_keys():
            sems[("dma", k)] = es.enter_context(nc.semaphore("sd_" + k))
        block = es.enter_context(nc.Block())

        @block.sync
        def _(e):
            S.emit_stream("sp", e, sems, extra_waits=final)

        @block.tensor
        def _(e):
            S.emit_stream("pe", e, sems)

        @block.scalar
        def _(e):
            S.emit_stream("act", e, sems)

        @block.vector
        def _(e):
            S.emit_stream("dve", e, sems)

        @block.gpsimd
        def _(e):
            S.emit_stream("pool", e, sems)
    return nc


def _t5_bucket_static():
    d = np.arange(256)
    n = np.maximum(d, 0)
    large = 16 + (np.log(np.maximum(n, 1).astype(np.float32) / 16) / math.log(128 / 16) * 16).astype(np.int32)
    large = np.minimum(large, 31)
    return np.where(n < 16, n, large)


def make_in_maps(inp, nslot=NSLOT):
    x = np.asarray(inp["x"], np.float32)
    pos = np.asarray(inp["positions"], np.int32)
    w_in = np.asarray(inp["w_in"], np.float32)[0]
    w_qb = np.asarray(inp["w_q_b"], np.float32)[0]
    w_kv = np.asarray(inp["w_kv_b"], np.float32)[0]
    w_out = np.asarray(inp["w_out"], np.float32)[0]
    w_up = np.asarray(inp["w_up"], np.float32)[0]
    w_dn = np.asarray(inp["w_down"], np.float32)[0]
    relb = np.asarray(inp["rel_bias_table"], np.float32)
    sinks = np.asarray(inp["sinks"], np.float32)[0]
    g_attn = np.asarray(inp["attn_norm_g"], np.float32)[0]
    g_ffn = np.asarray(inp["ffn_norm_g"], np.float32)[0]
    g_fin = np.asarray(inp["final_norm_g"], np.float32)
    g_q = np.asarray(inp["q_norm_g"], np.float32)[0]
    g_kv = np.asarray(inp["kv_norm_g"], np.float32)[0]
    g_a = np.asarray(inp["a_out_norm_g"], np.float32)[0]
    g_b = np.asarray(inp["b_out_norm_g"], np.float32)[0]
    conv_w = np.asarray(inp["conv_w"], np.float32)[0]
    conv_b = np.asarray(inp["conv_b"], np.float32)[0]

    consts = np.zeros((128, NC_CONST), np.float32)
    consts[:, 0:8] = g_attn.reshape(8, 128).T
    consts[:, 8:16] = g_ffn.reshape(8, 128).T
    consts[:, 16:24] = g_fin.reshape(8, 128).T
    consts[:, 24:26] = g_q.reshape(2, 128).T
    consts[:, 26] = g_kv
    consts[:, 27:31] = g_b.reshape(4, 128).T
    consts[0:64, 31:39] = g_a.reshape(8, 64).T
    invf = (10000.0 ** (-np.arange(0, 64, 2, dtype=np.float32) / 64)).astype(np.float32)
    consts[:, 39] = np.tile(invf, 4)
    for jj in range(3):
        consts[:, 40 + jj * 44:40 + (jj + 1) * 44] = conv_w[jj].reshape(44, 128).T
    consts[:, 172:216] = conv_b.reshape(44, 128).T

    kp1, kp2 = w_in[:, 1152:1184], w_in[:, 1184:1216]
    w_in_ext = np.concatenate([w_in[:, 0:1152], kp1, kp2, kp1, kp2, kp2, kp1, kp2, kp1], axis=1)
    cols = []
    for h in range(4):
        b0 = h * 192
        nope = w_qb[:, b0:b0 + 128]
        p1, p2 = w_qb[:, b0 + 128:b0 + 160], w_qb[:, b0 + 160:b0 + 192]
        cols += [nope, p1, p2, p1, p2, p2, p1, p2, p1]
    w_qb_ext = np.concatenate(cols, axis=1)
    w_kvT = np.ascontiguousarray(w_kv.T)

    bucket = _t5_bucket_static()
    u = np.arange(128)[:, None]
    t = np.arange(128)[None, :]
    biasT = np.full((128, 2, 8, 128), NEGB, np.float32)
    d_prev = t - u + 128
    d_cur = t - u
    for h in range(8):
        vp = relb[bucket[np.clip(d_prev, 0, 255)], h]
        biasT[:, 0, h, :] = np.where(d_prev < 128, vp, NEGB)
        vc = relb[bucket[np.clip(d_cur, 0, 255)], h]
        biasT[:, 1, h, :] = np.where(d_cur >= 0, vc, NEGB)

    common = dict(consts=consts, w_in_ext=np.ascontiguousarray(w_in_ext), w_qb_ext=np.ascontiguousarray(w_qb_ext),
                  w_kvT=w_kvT, w_kv=np.ascontiguousarray(w_kv), w_out=np.ascontiguousarray(w_out),
                  w_up=np.ascontiguousarray(w_up), w_down=np.ascontiguousarray(w_dn),
                  biasT=np.ascontiguousarray(biasT.reshape(128, -1)),
                  sinks=sinks[None, :].copy(), relb=relb.reshape(1, 256).copy(), gkv_row=g_kv[None, :].copy(),
                  ident=np.eye(128, dtype=np.float32))
    maps = []
    for core in range(8):
        b, c = core // 4, core % 4
        xT = np.ascontiguousarray(x[b].T)
        xT_str = np.zeros((NSLOT, D, TX), np.float32)
        pos_str = np.zeros((1, NSLOT * TS), np.int32)
        tq = np.full((1, NSLOT * TS), -1e9, np.float32)
        kval = np.zeros((NSLOT, 128, 5), np.float32)
        for j in range(NSLOT):
            k = stripe_of(c, j)
            if k is None:
                continue
            q0 = OWN * k - 2
            toks = np.arange(q0 - HALO, q0 + TS)
            ok = (toks >= 0) & (toks < SEQ)
            xT_str[j][:, ok] = xT[:, toks[ok]]
            tk = toks[HALO:]
            pos_str[0, j * TS:(j + 1) * TS] = pos[b, np.clip(tk, 0, SEQ - 1)]
            tq[0, j * TS:(j + 1) * TS] = tk.astype(np.float32)
            kval[j] = ok.astype(np.float32).reshape(5, 128).T
        m = dict(common)
        m.update(xT_seq=xT, xT_str=xT_str, pos_seq=pos[b][None, :].copy(), pos_str=pos_str, tq_str=tq, kval_str=kval)
        maps.append(m)
    return maps


def assemble(results):
    out = np.zeros((NB, SEQ, D), np.float32)
    for core in range(8):
        b, c = core // 4, core % 4
        o = results[core]["out"]
        for j in range(NSLOT):
            k = stripe_of(c, j)
            if k is None:
                continue
            t0 = OWN * k
            n = min(OWN, SEQ - t0)
            if n <= 0:
                continue
            out[b, t0:t0 + n, :] = o[j][:, 2:2 + n].T
    return out


def kernel(**inputs):
    nc = build_nc()
    maps = make_in_maps(inputs)
    res = run_bass_kernel_spmd(nc, maps, core_ids=list(range(8)))
    return assemble(res.results)
```
